# Optimizing a Trainium2 kernel written in Bass

```python
import jax
import jax.numpy as jnp
from jax import lax
import numpy as np

D_MODEL = 1024
BATCH = 2
SEQ = 16384
DEPTH = 4

CTX_LEN = 256
GRID_W = 64
HEAD_DIM = 64
GROUP_HEADS = 4
GROUP_W = GROUP_HEADS * HEAD_DIM
N_MIXERS = 4
MIX_W = N_MIXERS * GROUP_W
MLA_Q_RANK = 256
MLA_KV_RANK = 128
MLA_NOPE = 64
MLA_ROPE = 32
MLA_V = 64
MLA_BLOCK = 128
RET_CHUNK = 128
NA_KR = 8
NA_KC = 16
SWA_KV_HEADS = 2
SWA_WINDOW = 128
SWA_BLOCK = 128
FFN_HIDDEN = ((8 * D_MODEL + 3 * 256 - 1) // (3 * 256)) * 256
ROPE_THETA = 10000.0
EPS = 1e-6
NEG_INF = -1e30

IN_SIZES = (MLA_Q_RANK, MLA_KV_RANK, MLA_ROPE,
            GROUP_W, GROUP_W, GROUP_W, GROUP_W, GROUP_W,
            GROUP_W, GROUP_W, GROUP_W,
            GROUP_W, SWA_KV_HEADS * HEAD_DIM, SWA_KV_HEADS * HEAD_DIM)
IN_W = sum(IN_SIZES)
IN_SPLIT = tuple(sum(IN_SIZES[:i + 1]) for i in range(len(IN_SIZES) - 1))

kernel_name = 'hybrid_parallel_head_flow_block'


def rms_norm(x, g):
    xf = x.astype(jnp.float32)
    y = xf * lax.rsqrt(jnp.mean(xf * xf, axis=-1, keepdims=True) + EPS)
    return (y * g.astype(jnp.float32)).astype(x.dtype)


def head_rms(t):
    return t * lax.rsqrt(jnp.mean(t * t, axis=-1, keepdims=True) + EPS)


def heads(t, h):
    b, n, _ = t.shape
    return t.reshape(b, n, h, -1).transpose(0, 2, 1, 3)


def merge_heads(t):
    b, h, n, d = t.shape
    return t.transpose(0, 2, 1, 3).reshape(b, n, h * d)


def rope_1d(x, pos):
    d = x.shape[-1]
    inv = ROPE_THETA ** (-jnp.arange(0, d, 2, dtype=jnp.float32) / d)
    ang = pos.astype(jnp.float32)[:, None] * inv[None, :]
    cos, sin = jnp.cos(ang), jnp.sin(ang)
    xf = x.astype(jnp.float32)
    x1, x2 = xf[..., : d // 2], xf[..., d // 2:]
    return jnp.concatenate([x1 * cos - x2 * sin, x2 * cos + x1 * sin], axis=-1).astype(x.dtype)


def rope_2d(x, row, col):
    h = x.shape[-1] // 2
    return jnp.concatenate([rope_1d(x[..., :h], row), rope_1d(x[..., h:], col)], axis=-1)


def softmax_attend(q, k, v, scale, sink=None):
    s = jnp.einsum('bhqd,bhkd->bhqk', q, k, preferred_element_type=jnp.float32) * scale
    if sink is not None:
        s = jnp.concatenate([s, jnp.broadcast_to(sink.astype(jnp.float32)[None, :, None, None], s.shape[:-1] + (1,))], axis=-1)
    p = jax.nn.softmax(s, axis=-1)
    if sink is not None:
        p = p[..., :-1]
    return jnp.einsum('bhqk,bhkd->bhqd', p.astype(v.dtype), v)


def mla_mixer(xcq, xckv, xkr, ycq, yckv, ykr, row, col, q_norm, w_uq, kv_norm, w_ukv):
    H = GROUP_HEADS
    scale = (MLA_NOPE + MLA_ROPE) ** -0.5

    def qkv(cq, ckv, kr, rotate):
        q = heads(rms_norm(cq, q_norm) @ w_uq, H)
        kv = heads(rms_norm(ckv, kv_norm) @ w_ukv, H)
        q_nope, q_rope = q[..., :MLA_NOPE], q[..., MLA_NOPE:]
        k_nope, v = kv[..., :MLA_NOPE], kv[..., MLA_NOPE:]
        k_rope = kr[:, None]
        if rotate:
            q_rope = rope_2d(q_rope, row, col)
            k_rope = rope_2d(k_rope, row, col)
        k_rope = jnp.broadcast_to(k_rope, k_nope.shape[:-1] + (MLA_ROPE,))
        return (jnp.concatenate([q_nope, q_rope], axis=-1),
                jnp.concatenate([k_nope, k_rope], axis=-1), v)

    qx, kx, vx = qkv(xcq, xckv, xkr, True)
    qy, ky, vy = qkv(ycq, yckv, ykr, False)
    k_all = jnp.concatenate([kx, ky], axis=2)
    v_all = jnp.concatenate([vx, vy], axis=2)
    B, _, S, dq = qx.shape
    nb = S // MLA_BLOCK
    qb = jnp.moveaxis(qx.reshape(B, H, nb, MLA_BLOCK, dq), 2, 0)
    ob = lax.map(lambda qblk: softmax_attend(qblk, k_all, v_all, scale), qb)
    ox = jnp.moveaxis(ob, 0, 2).reshape(B, H, S, MLA_V)
    oy = softmax_attend(qy, ky, vy, scale)
    return merge_heads(ox), merge_heads(oy)


def retention_scan(q, k, v, log_g, state0):
    B, H, T, _ = q.shape
    C = RET_CHUNK
    n = T // C
    i = jnp.arange(C, dtype=jnp.float32)
    diff = i[:, None] - i[None, :]
    lg = log_g[:, None, None]
    inner_decay = jnp.where(diff >= 0, jnp.exp(lg * jnp.maximum(diff, 0.0)), 0.0)
    q_decay = jnp.exp(lg * (i + 1.0)[:, None])
    k_decay = jnp.exp(lg * (C - 1.0 - i)[:, None])
    chunk_decay = jnp.exp(lg * C)

    def chunks(t):
        return jnp.moveaxis(t.reshape(B, H, n, C, t.shape[-1]), 2, 0)

    def step(state, inp):
        qc, kc, vc = inp
        att = jnp.einsum('bhid,bhjd->bhij', qc, kc) * inner_decay
        out = (jnp.einsum('bhij,bhjd->bhid', att, vc)
               + jnp.einsum('bhid,bhde->bhie', qc, state) * q_decay)
        state = state * chunk_decay + jnp.einsum('bhjd,bhje->bhde', kc * k_decay, vc)
        return state, out

    state, out = lax.scan(step, state0, (chunks(q), chunks(k), chunks(v)))
    return jnp.moveaxis(out, 0, 2).reshape(B, H, T, v.shape[-1]), state


def retention_mixer(xq, xk, xv, xgf, xgb, yq, yk, yv, ygf, ygb, decay, row, col):
    H = GROUP_HEADS
    f32 = jnp.float32
    kscale = HEAD_DIM ** -0.5
    q = rope_2d(heads(xq, H), row, col).astype(f32)
    k = rope_2d(heads(xk, H), row, col).astype(f32) * kscale
    v = heads(xv, H).astype(f32)
    qy = heads(yq, H).astype(f32)
    ky = heads(yk, H).astype(f32) * kscale
    vy = heads(yv, H).astype(f32)
    log_g = jax.nn.log_sigmoid(decay.astype(f32))
    zero = jnp.zeros((q.shape[0], H, HEAD_DIM, HEAD_DIM), f32)
    flip = lambda t: jnp.flip(t, axis=2)
    yf, sf = retention_scan(qy, ky, vy, log_g[0], zero)
    of, _ = retention_scan(q, k, v, log_g[0], sf)
    yb, sb = retention_scan(flip(qy), flip(ky), flip(vy), log_g[1], zero)
    ob, _ = retention_scan(flip(q), flip(k), flip(v), log_g[1], sb)

    def gate_merge(o_f, o_b, gf, gb):
        out = (head_rms(o_f) * jax.nn.silu(heads(gf, H).astype(f32))
               + head_rms(flip(o_b)) * jax.nn.silu(heads(gb, H).astype(f32)))
        return merge_heads(out).astype(gf.dtype)

    return gate_merge(of, ob, xgf, xgb), gate_merge(yf, yb, ygf, ygb)


def na_mixer(xq, xk, xv, yq, yk, yv, rpb):
    B, S, _ = xq.shape
    H, d = GROUP_HEADS, HEAD_DIM
    rows = S // GRID_W
    kr = min(NA_KR, rows)
    scale = d ** -0.5
    q = xq.reshape(B, rows, GRID_W, H, d)
    k = xk.reshape(B, rows, GRID_W, H, d)
    v = xv.reshape(B, rows, GRID_W, H, d)
    r = jnp.arange(rows)
    r0 = jnp.clip(r - kr // 2, 0, rows - kr)
    row_idx = r0[:, None] + jnp.arange(kr)[None, :]
    kg = k[:, row_idx]
    vg = v[:, row_idx]
    c = jnp.arange(GRID_W)
    c0 = jnp.clip(c - NA_KC // 2, 0, GRID_W - NA_KC)
    col_in = (c[None, :] >= c0[:, None]) & (c[None, :] < c0[:, None] + NA_KC)
    dr = row_idx - r[:, None] + NA_KR - 1
    dc = jnp.clip(c[None, :] - c[:, None], -(NA_KC - 1), NA_KC - 1) + NA_KC - 1
    bias = rpb[:, dr[:, None, :, None], dc[None, :, None, :]].astype(jnp.float32)
    s_loc = jnp.einsum('brchd,brkwhd->bhrckw', q, kg, preferred_element_type=jnp.float32) * scale + bias
    s_loc = jnp.where(col_in[:, None, :], s_loc, NEG_INF)
    kyh = yk.reshape(B, -1, H, d)
    vyh = yv.reshape(B, -1, H, d)
    s_ctx = jnp.einsum('brchd,blhd->bhrcl', q, kyh, preferred_element_type=jnp.float32) * scale
    n_loc = kr * GRID_W
    p = jax.nn.softmax(jnp.concatenate([s_loc.reshape(B, H, rows, GRID_W, n_loc), s_ctx], axis=-1), axis=-1)
    p_loc = p[..., :n_loc].reshape(B, H, rows, GRID_W, kr, GRID_W).astype(xv.dtype)
    p_ctx = p[..., n_loc:].astype(xv.dtype)
    ox = (jnp.einsum('bhrckw,brkwhd->brchd', p_loc, vg)
          + jnp.einsum('bhrcl,blhd->brchd', p_ctx, vyh)).reshape(B, S, H * d)
    oy = softmax_attend(heads(yq, H), heads(yk, H), heads(yv, H), scale)
    return ox, merge_heads(oy)


def swa_mixer(xq, xk, xv, yq, yk, yv, sink, row, col):
    B, S, _ = xq.shape
    Hq, Hk = GROUP_HEADS, SWA_KV_HEADS
    G = Hq // Hk
    d = HEAD_DIM
    Bl = SWA_BLOCK
    nb = S // Bl
    scale = d ** -0.5
    q = rope_2d(heads(xq, Hq), row, col)
    k = rope_2d(heads(xk, Hk), row, col)
    v = heads(xv, Hk)
    qb = q.reshape(B, Hk, G, nb, Bl, d)

    def window(t):
        tp = jnp.pad(t, ((0, 0), (0, 0), (Bl, Bl), (0, 0))).reshape(B, Hk, nb + 2, Bl, d)
        return jnp.concatenate([tp[:, :, :-2], tp[:, :, 1:-1], tp[:, :, 2:]], axis=3)

    kw, vw = window(k), window(v)
    qi = jnp.arange(Bl)
    j = jnp.arange(3 * Bl)
    blk = jnp.arange(nb)
    key_pos = (blk[:, None] - 1) * Bl + j[None, :]
    delta = j[None, :] - Bl - qi[:, None]
    valid = (jnp.abs(delta) <= SWA_WINDOW)[None] & ((key_pos >= 0) & (key_pos < S))[:, None, :]
    s_win = jnp.einsum('bkgnqd,bknjd->bkgnqj', qb, kw, preferred_element_type=jnp.float32) * scale
    s_win = jnp.where(valid, s_win, NEG_INF)
    ky = heads(yk, Hk)
    vy = heads(yv, Hk)
    L = ky.shape[2]
    s_ctx = jnp.einsum('bkgnqd,bkld->bkgnql', qb, ky, preferred_element_type=jnp.float32) * scale
    s_sink = jnp.broadcast_to(sink.astype(jnp.float32).reshape(Hk, G, 1, 1, 1), (B, Hk, G, nb, Bl, 1))
    p = jax.nn.softmax(jnp.concatenate([s_win, s_ctx, s_sink], axis=-1), axis=-1)
    ox = (jnp.einsum('bkgnqj,bknjd->bkgnqd', p[..., :3 * Bl].astype(xv.dtype), vw)
          + jnp.einsum('bkgnql,bkld->bkgnqd', p[..., 3 * Bl:3 * Bl + L].astype(xv.dtype), vy))
    ox = ox.reshape(B, Hq, S, d)
    oy = softmax_attend(heads(yq, Hq), jnp.repeat(ky, G, axis=1), jnp.repeat(vy, G, axis=1), scale, sink)
    return merge_heads(ox), merge_heads(oy)


def token_mixers(px, py, row, col, mla_q_norm, mla_w_uq, mla_kv_norm, mla_w_ukv, ret_decay, na_rpb, swa_sink):
    xs = jnp.split(px, IN_SPLIT, axis=-1)
    ys = jnp.split(py, IN_SPLIT, axis=-1)
    mla_x, mla_y = mla_mixer(xs[0], xs[1], xs[2], ys[0], ys[1], ys[2], row, col,
                             mla_q_norm, mla_w_uq, mla_kv_norm, mla_w_ukv)
    ret_x, ret_y = retention_mixer(xs[3], xs[4], xs[5], xs[6], xs[7],
                                   ys[3], ys[4], ys[5], ys[6], ys[7], ret_decay, row, col)
    na_x, na_y = na_mixer(xs[8], xs[9], xs[10], ys[8], ys[9], ys[10], na_rpb)
    swa_x, swa_y = swa_mixer(xs[11], xs[12], xs[13], ys[11], ys[12], ys[13], swa_sink, row, col)
    return (jnp.concatenate([mla_x, ret_x, na_x, swa_x], axis=-1),
            jnp.concatenate([mla_y, ret_y, na_y, swa_y], axis=-1))


def swiglu(h, w1, w3, w2):
    return (jax.nn.silu(h @ w1) * (h @ w3)) @ w2


def setup_inputs(seed: int = 0) -> dict:
    key = jax.random.key(seed)
    ks = jax.random.split(key, 24)
    f32 = jnp.float32
    nrm = lambda k, shape, s: jax.random.normal(k, shape, f32) * s
    H = GROUP_HEADS
    L = DEPTH
    base_decay = jnp.log(2.0 ** (5.0 + jnp.arange(H, dtype=f32)) - 1.0)
    return {
        'x': nrm(ks[0], (BATCH, SEQ, D_MODEL), 1.0),
        'c': nrm(ks[1], (BATCH, D_MODEL), 1.0),
        'ctx': nrm(ks[2], (BATCH, CTX_LEN, D_MODEL), 1.0),
        'c_ctx': nrm(ks[3], (D_MODEL,), 1.0),
        'ada_w': nrm(ks[4], (L, D_MODEL, 6 * D_MODEL), 0.5 * D_MODEL ** -0.5),
        'ada_b': nrm(ks[5], (L, 6 * D_MODEL), 0.02),
        'norm1_g': 1.0 + nrm(ks[6], (L, D_MODEL), 0.02),
        'w_in': nrm(ks[7], (L, D_MODEL, IN_W), D_MODEL ** -0.5),
        'mla_q_norm': 1.0 + nrm(ks[8], (L, MLA_Q_RANK), 0.02),
        'mla_w_uq': nrm(ks[9], (L, MLA_Q_RANK, H * (MLA_NOPE + MLA_ROPE)), MLA_Q_RANK ** -0.5),
        'mla_kv_norm': 1.0 + nrm(ks[10], (L, MLA_KV_RANK), 0.02),
        'mla_w_ukv': nrm(ks[11], (L, MLA_KV_RANK, H * (MLA_NOPE + MLA_V)), MLA_KV_RANK ** -0.5),
        'ret_decay': base_decay[None, None, :] + nrm(ks[12], (L, 2, H), 0.1),
        'na_rpb': nrm(ks[13], (L, H, 2 * NA_KR - 1, 2 * NA_KC - 1), 0.1),
        'swa_sink': nrm(ks[14], (L, H), 0.5),
        'w_out': nrm(ks[15], (L, MIX_W, D_MODEL), MIX_W ** -0.5),
        'norm2_g': 1.0 + nrm(ks[16], (L, D_MODEL), 0.02),
        'ffn_w1': nrm(ks[17], (L, D_MODEL, FFN_HIDDEN), D_MODEL ** -0.5),
        'ffn_w3': nrm(ks[18], (L, D_MODEL, FFN_HIDDEN), D_MODEL ** -0.5),
        'ffn_w2': nrm(ks[19], (L, FFN_HIDDEN, D_MODEL), FFN_HIDDEN ** -0.5),
        'final_norm_g': 1.0 + nrm(ks[20], (D_MODEL,), 0.02),
    }


def reference(x, c, ctx, c_ctx, ada_w, ada_b, norm1_g, w_in, mla_q_norm, mla_w_uq, mla_kv_norm,
              mla_w_ukv, ret_decay, na_rpb, swa_sink, w_out, norm2_g, ffn_w1, ffn_w3, ffn_w2,
              final_norm_g):
    S = x.shape[1]
    t = jnp.arange(S)
    row = t // GRID_W
    col = t % GRID_W
    y = ctx
    sc = jax.nn.silu(c)
    scc = jax.nn.silu(c_ctx)
    for l in range(DEPTH):
        mod_x = (sc @ ada_w[l] + ada_b[l])[:, None, :]
        mod_y = scc @ ada_w[l] + ada_b[l]
        shx1, scx1, gx1, shx2, scx2, gx2 = jnp.split(mod_x, 6, axis=-1)
        shy1, scy1, gy1, shy2, scy2, gy2 = jnp.split(mod_y, 6, axis=-1)
        hx = rms_norm(x, norm1_g[l]) * (1.0 + scx1) + shx1
        hy = rms_norm(y, norm1_g[l]) * (1.0 + scy1) + shy1
        mx, my = token_mixers(hx @ w_in[l], hy @ w_in[l], row, col, mla_q_norm[l], mla_w_uq[l],
                              mla_kv_norm[l], mla_w_ukv[l], ret_decay[l], na_rpb[l], swa_sink[l])
        x = x + gx1 * (mx @ w_out[l])
        hx = rms_norm(x, norm2_g[l]) * (1.0 + scx2) + shx2
        x = x + gx2 * swiglu(hx, ffn_w1[l], ffn_w3[l], ffn_w2[l])
        if l < DEPTH - 1:
            y = y + gy1 * (my @ w_out[l])
            hy = rms_norm(y, norm2_g[l]) * (1.0 + scy2) + shy2
            y = y + gy2 * swiglu(hy, ffn_w1[l], ffn_w3[l], ffn_w2[l])
    return rms_norm(x, final_norm_g)
```

```python
import numpy as np
from contextlib import ExitStack
import ml_dtypes
import concourse.bass as bass
import concourse.mybir as mybir
from concourse.bass_utils import run_bass_kernel_spmd

F32 = mybir.dt.float32
BF16 = mybir.dt.bfloat16
AF = mybir.ActivationFunctionType
ALU = mybir.AluOpType

NC = 8
D = 1024
B = 2
SEQ = 16384
L = 4
TL = 4096
LC = 256
T = TL + LC
GROUPS = [(g * 512, 512) for g in range(8)] + [(TL, LC)]
FFN = 2816
EPS = 1e-6
THETA = 10000.0
NEG = -1e30
BIGE = 1e9

XO = dict(cq=0, ckv=256, kr=384, krp=416, rq=448, rqp=704, rk=960, rkp=1216, gf=1472, gb=1728,
          nq=1984, nk=2240, sq=2496, sqp=2752, sk=3008, skp=3136, rv=3264, nv=3520, sv=3776)
XW = 3904
EX = dict(mla=0, nkh=2560, nkt=2816, nvh=3072, nvt=3328, skh=3584, skt=3712, svh=3840, svt=3968)
EXR = 4096


class Buf:
    __slots__ = ("name", "w", "r")

    def __init__(self, name):
        self.name = name
        self.w = None
        self.r = []


class Tile:
    def __init__(self, t, b):
        self.t = t
        self.b = b


class Op:
    __slots__ = ("eng", "fn", "waits", "signal", "kind", "count", "sem", "val")

    def __init__(self, eng, fn, kind):
        self.eng = eng
        self.fn = fn
        self.kind = kind
        self.waits = []
        self.signal = False
        self.count = None
        self.sem = None
        self.val = None


class Sched:
    CE = ("pe", "act", "dve", "pool")

    def __init__(self, nc, es):
        self.nc = nc
        self.es = es
        self.ops = []
        self.pstack = []
        self.nbuf = 0
        self.strict_same = True

    def buf(self, name):
        return Buf(name)

    def sb(self, name, shape, dt):
        self.nbuf += 1
        t = self.pstack[-1].enter_context(self.nc.sbuf_tensor(f"{name}_{self.nbuf}", list(shape), dt))
        return Tile(t, Buf(name))

    def phase(self):
        sch = self

        class _P:
            def __enter__(s):
                sch.pstack.append(ExitStack())
                return s

            def __exit__(s, *a):
                sch.barrier()
                sch.pstack.pop().close()
                return False

        return _P()

    def _deps(self, op, R, W):
        for b in R:
            if b.w is not None:
                op.waits.append(b.w)
        for b in W:
            if b.w is not None:
                op.waits.append(b.w)
            op.waits.extend(b.r)
        for b in R:
            if op.kind == "c":
                b.r = [x for x in b.r if not (x.kind == "c" and x.eng == op.eng)]
            b.r.append(op)
        for b in W:
            b.w = op
            b.r = []

    @staticmethod
    def _eager(fn):
        calls = []

        class _Rec:
            def __getattr__(self, name):
                def f(*a, **k):
                    calls.append((name, a, k))
                    return self
                return f

        fn(_Rec())
        assert len(calls) == 1, calls
        name, a, k = calls[0]
        return lambda e: getattr(e, name)(*a, **k)

    def op(self, eng, fn, R=(), W=()):
        o = Op(eng, self._eager(fn), "c")
        self._deps(o, R, W)
        self.ops.append(o)
        return o

    def dma(self, q, out, in_, R=(), W=(), **kw):
        o = Op(q, (lambda e, out=out, in_=in_, kw=kw: e.dma_start(out=out, in_=in_, **kw)), "d")
        self._deps(o, R, W)
        self.ops.append(o)
        return o

    def cc(self, fn, R=(), W=()):
        o = Op("pool", self._eager(fn), "cc")
        self._deps(o, R, W)
        self.ops.append(o)
        return o

    def barrier(self):
        o = Op("sp", None, "bar")
        self.ops.append(o)

    def newsems(self):
        self.ops.append(Op("sp", None, "ns"))

    def emit(self):
        nc = self.nc
        es = self.es
        engs = {"pe": nc.tensor, "act": nc.scalar, "dve": nc.vector, "pool": nc.gpsimd, "sp": nc.sync}
        sem = {k: es.enter_context(nc.semaphore("s_" + k)) for k in self.CE}
        ccsem = es.enter_context(nc.semaphore("s_cc"))
        barsem = es.enter_context(nc.semaphore("s_bar"))
        K = 8
        dq = {q: [es.enter_context(nc.semaphore(f"d_{q}{i}")) for i in range(K)] for q in ("sp", "pool", "act")}
        dn = {q: 0 for q in dq}
        dlast = {q: [0] * K for q in dq}
        cnt = {k: 0 for k in self.CE}
        cccnt = 0
        barn = 0
        waited = {}
        for o in self.ops:
            for p in o.waits:
                p.signal = True

        def wait(e, s, v):
            key = (e, id(s))
            if waited.get(key, 0) >= v:
                return
            engs[e].wait_ge(s, v)
            waited[key] = v

        for o in self.ops:
            e = o.eng
            if o.kind == "ns":
                nsn = getattr(self, "_nsn", 0) + 1
                self._nsn = nsn
                sem = {k: es.enter_context(nc.semaphore(f"s{nsn}_" + k)) for k in self.CE}
                cnt = {k: 0 for k in self.CE}
                continue
            if o.kind == "bar":
                for k in self.CE:
                    if cnt[k] > 0:
                        wait("sp", sem[k], cnt[k])
                if cccnt:
                    wait("sp", ccsem, cccnt)
                for q in dq:
                    for i in range(K):
                        if dlast[q][i]:
                            wait("sp", dq[q][i], dlast[q][i])
                barn += 1
                nc.sync.sem_inc(barsem, 1)
                for k in ("pe", "act", "dve", "pool"):
                    wait(k, barsem, barn)
                continue
            for p in o.waits:
                if p.kind == "c" and p.eng == e and (e == "pe" or not self.strict_same):
                    continue
                if p.kind == "d" and p.eng == e and False:
                    continue
                wait(e, p.sem, p.val)
            if o.kind == "c":
                ins = o.fn(engs[e])
                if o.signal:
                    ins.then_inc(sem[e], 1)
                    cnt[e] += 1
                    o.sem, o.val = sem[e], cnt[e]
            elif o.kind == "cc":
                ins = o.fn(engs[e])
                ins.then_inc(ccsem)
                cccnt += 1
                o.sem, o.val = ccsem, cccnt
            else:
                n = dn[e]
                slot = n % K
                if n >= K:
                    wait(e, dq[e][slot], 16 * (n // K))
                ins = o.fn(engs[e])
                ins.then_inc(dq[e][slot], 16)
                o.sem, o.val = dq[e][slot], 16 * (n // K + 1)
                dlast[e][slot] = o.val
                dn[e] += 1
        for k in self.CE:
            if cnt[k] > 0:
                wait("sp", sem[k], cnt[k])
        for q in dq:
            for i in range(K):
                if dlast[q][i]:
                    wait("sp", dq[q][i], dlast[q][i])


class Prog:
    def __init__(self, nlayers=L, debug=(), stop=None, flags=(), stop_layer=0):
        self.stop_layer = stop_layer
        self.nlayers = nlayers
        self.debug = debug
        self.stop = stop
        self.flags = flags
        self.es = ExitStack()
        self.nc = bass.Bass("TRN2", target_bir_lowering=False)
        self.S = Sched(self.nc, self.es)
        self.dr = {}
        self.db = {}
        self.build()

    def dram(self, name, shape, dt, kind="Internal"):
        if kind == "Internal":
            t = self.nc.dram_tensor(name, list(shape), dt)
        else:
            t = self.nc.dram_tensor(name, list(shape), dt, kind=kind)
        self.dr[name] = t.ap()
        self.db[name] = Buf(name)
        return t.ap()

    def build(self):
        nc, S = self.nc, self.S
        dram = self.dram
        dram("xT0", [D, T], F32, "ExternalInput")
        dram("cT", [128, 8, 2], F32, "ExternalInput")
        dram("adabT", [L, 128, 48], F32, "ExternalInput")
        dram("n1gT", [L, 128, 8], F32, "ExternalInput")
        dram("n2gT", [L, 128, 8], F32, "ExternalInput")
        dram("fgT", [128, 8], F32, "ExternalInput")
        dram("qnT", [L, 128, 2], F32, "ExternalInput")
        dram("kvnT", [L, 128, 1], F32, "ExternalInput")
        dram("decB", [L, 128, 8], F32, "ExternalInput")
        dram("sinkB", [L, 1, 4 * 512], F32, "ExternalInput")
        dram("wuq", [L, 256, 768], F32, "ExternalInput")
        dram("wukv", [L, 128, 512], F32, "ExternalInput")
        dram("ident", [128, 128], F32, "ExternalInput")
        dram("c64", [64, T], F32, "ExternalInput")
        dram("s64", [64, T], F32, "ExternalInput")
        dram("c32", [32, T], F32, "ExternalInput")
        dram("s32", [32, T], F32, "ExternalInput")
        dram("rm", [3 * 8 * 128, 512], BF16, "ExternalInput")
        dram("ms", [3 * 6 * 128, 512], BF16, "ExternalInput")
        dram("rete", [128, 2192], F32, "ExternalInput")
        self.wsh = dict(ada=(L * D // NC, 6 * D), win=(L * D // NC, XW), wout=(L * D // NC, D),
                        w1=(L * D // NC, FFN), w3=(L * D // NC, FFN), w2=(L * FFN // NC, D),
                        toep=(L * 4 * 8 * 128 // NC, 512))
        for k, (r, c) in self.wsh.items():
            if "nogather" not in self.flags:
                dram(k + "_sh", [r, c], F32, "ExternalInput")
            dram(k + "_src", [r, c], F32)
            dram(k + "_g", [r * NC, c], F32)
        dram("outT", [D, TL], F32, "ExternalOutput")
        dram("XA", [D, T], F32)
        dram("XB", [D, T], F32)
        dram("XM", [D, T], F32)
        dram("h2T", [D, T], BF16)
        dram("uT", [FFN, T], BF16)
        dram("mxT", [D, T], BF16)
        dram("qmT", [4 * 96, T], BF16)
        dram("latc", [160, LC], BF16)
        dram("rqT", [256, T], BF16)
        dram("rkT", [256, T], BF16)
        dram("rvtm", [T, 256], BF16)
        dram("sgfT", [256, T], BF16)
        dram("sgbT", [256, T], BF16)
        dram("nqT", [256, T], BF16)
        dram("nkT", [256, T], BF16)
        dram("nvtm", [T, 256], BF16)
        dram("sqT", [256, T], BF16)
        dram("skT", [128, T], BF16)
        dram("svtm", [T, 128], BF16)
        dram("UD", [34 * 64, 512], F32)
        dram("EXP", [EXR, 256], BF16)
        dram("GAT", [4 * EXR, 256], BF16)
        dram("GAT8", [8 * 256, 4096], BF16)
        dram("LATB", [4 * 160, 4096], BF16)
        dram("RETG8", [8 * 64, 512], F32)
        dram("HALOP", [96, 4096], BF16)
        dram("HALON", [96, 4096], BF16)
        dram("RETX", [64, 512], F32)
        dram("RETG", [4 * 64, 512], F32)
        for name in self.debug:
            pass

        self.ps = []
        for i in range(7):
            t = self.es.enter_context(nc.psum_tensor(f"ps{i}", [128, 512], F32))
            self.ps.append(Tile(t, Buf(f"ps{i}")))
        t = self.es.enter_context(nc.psum_tensor("psb", [128, 1024], BF16))
        self.psb = Tile(t, Buf("psb"))

        t = self.es.enter_context(nc.sbuf_tensor("epsb", [128, 1], F32))
        self.epsb = Tile(t, Buf("epsb"))
        S.op("pool", lambda e: e.memset(self.epsb.t[:, :], EPS), W=[self.epsb.b])
        t = self.es.enter_context(nc.sbuf_tensor("zpad", [128, 128], BF16))
        zp = Tile(t, Buf("zpad"))
        S.op("pool", lambda e: e.memset(zp.t[:, :], 0.0), W=[zp.b])
        for i in range(4):
            S.dma("sp", self.dr["EXP"][EX["skh"] + 128 * i:EX["skh"] + 128 * (i + 1), 128:256], zp.t[:, :], R=[zp.b], W=[self.db["EXP"]])
        if "nogather" not in self.flags:
            self.emit_weight_gather()
            S.barrier()
        xcur = "xT0"
        for l in range(self.nlayers if self.stop != "W" else 0):
            xnext = "XA" if l % 2 == 0 else "XB"
            done = self.layer(l, xcur, xnext)
            if not done:
                break
            xcur = xnext
        else:
            self.emit_final(xcur)
        self.emit_debug()
        S.emit()

    def emit_weight_gather(self):
        S, dr, db = self.S, self.dr, self.db
        for k in self.wsh:
            S.dma("sp", dr[k + "_src"], dr[k + "_sh"], R=[db[k + "_sh"]], W=[db[k + "_src"]])
        for k in self.wsh:
            S.cc(lambda e, k=k: e.collective_compute("AllGather", ALU.bypass, replica_groups=[list(range(NC))],
                                                     ins=[dr[k + "_src"].opt()], outs=[dr[k + "_g"].opt()]),
                 R=[db[k + "_src"]], W=[db[k + "_g"]])

    def mm(self, out, lhsT, rhs, start, stop, R, W):
        return self.S.op("pe", lambda e: e.matmul(out, lhsT, rhs, start=start, stop=stop), R=R, W=W)

    def rstd_op(self, out_ap, in_ap, R, W):
        S = self.S
        S.op("act", lambda e: e.activation(out=out_ap, in_=in_ap, func=AF.Sqrt, bias=self.epsb.t[0:in_ap.shape[0], 0:1], scale=1.0), R=list(R) + [self.epsb.b], W=list(W))
        S.op("dve", lambda e: e.reciprocal(out=out_ap, in_=out_ap), R=list(W), W=list(W))

    def norm_group(self, xs, W, scale_ap, shift_ap, hout, ones, psi, sq, rstd, tmp, scale_sq=1.0 / 32.0):
        S = self.S
        ps = self.ps[psi]
        for k in range(8):
            S.op("act", lambda e, k=k: e.activation(out=sq[k % 2].t[:, :W], in_=xs.t[:, k, :W], func=AF.Square, scale=scale_sq),
                 R=[xs.b], W=[sq[k % 2].b])
            self.mm(ps.t[:, :W], ones.t[:, :], sq[k % 2].t[:, :W], k == 0, k == 7, [ones.b, sq[k % 2].b], [ps.b])
        self.rstd_op(rstd.t[:, :W], ps.t[:, :W], [ps.b], [rstd.b])
        for k in range(8):
            if shift_ap is None:
                S.op("dve", lambda e, k=k: e.scalar_tensor_tensor(out=hout.t[:, k, :W], in0=xs.t[:, k, :W],
                                                                  scalar=scale_ap(k), in1=rstd.t[:, :W],
                                                                  op0=ALU.mult, op1=ALU.mult),
                     R=[xs.b, rstd.b], W=[hout.b])
            else:
                tk = tmp[k % 2]
                S.op("dve", lambda e, k=k, tk=tk: e.scalar_tensor_tensor(out=tk.t[:, :W], in0=xs.t[:, k, :W],
                                                                         scalar=scale_ap(k), in1=rstd.t[:, :W],
                                                                         op0=ALU.mult, op1=ALU.mult),
                     R=[xs.b, rstd.b], W=[tk.b])
                S.op("act", lambda e, k=k, tk=tk: e.activation(out=hout.t[:, k, :W], in_=tk.t[:, :W],
                                                               func=AF.Identity, bias=shift_ap(k), scale=1.0),
                     R=[tk.b], W=[hout.b])

    def emit_mod(self, l, P):
        S, dr, db = self.S, self.dr, self.db
        mod, s1, s2 = P["mod"], P["s1"], P["s2"]
        with S.phase():
            cT = S.sb("cT", [128, 8, 2], F32)
            sc = S.sb("sc", [128, 8, 2], BF16)
            adab = S.sb("adab", [128, 48], F32)
            n1g = S.sb("n1g", [128, 8], F32)
            n2g = S.sb("n2g", [128, 8], F32)
            tmp = S.sb("mtmp", [128, 8, 2], F32)
            S.dma("sp", cT.t[:, :, :], dr["cT"], R=[db["cT"]], W=[cT.b])
            S.dma("sp", adab.t[:, :], dr["adabT"][l], R=[db["adabT"]], W=[adab.b])
            S.dma("sp", n1g.t[:, :], dr["n1gT"][l], R=[db["n1gT"]], W=[n1g.b])
            S.dma("sp", n2g.t[:, :], dr["n2gT"][l], R=[db["n2gT"]], W=[n2g.b])
            S.op("act", lambda e: e.activation(out=sc.t[:, :, :], in_=cT.t[:, :, :], func=AF.Silu), R=[cT.b], W=[sc.b])
            wts = [S.sb("adaw", [128, 8, 768], BF16) for _ in range(2)]
            for pc in range(8):
                wt = wts[pc % 2]
                src = dr["ada_g"][l * D:(l + 1) * D, pc * 768:(pc + 1) * 768].rearrange("(k p) c -> p k c", p=128)
                S.dma("pool", wt.t[:, :, :], src, R=[db["ada_g"]], W=[wt.b])
                ps = self.ps[pc % 2]
                for n in range(6):
                    for k in range(8):
                        self.mm(ps.t[:, 2 * n:2 * n + 2], wt.t[:, k, 128 * n:128 * n + 128], sc.t[:, k, :],
                                k == 0, k == 7, [wt.b, sc.b], [ps.b])
                for n in range(6):
                    ch = pc * 6 + n
                    S.op("dve", lambda e, n=n, ch=ch, ps=ps: e.tensor_scalar(
                        out=mod.t[:, ch, :], in0=ps.t[:, 2 * n:2 * n + 2], scalar1=adab.t[:, ch:ch + 1], scalar2=None,
                        op0=ALU.add), R=[ps.b, adab.b], W=[mod.b])
            for (dst, g, off) in ((s1, n1g, 8), (s2, n2g, 32)):
                S.op("dve", lambda e, off=off: e.tensor_scalar(out=tmp.t[:, :, :], in0=mod.t[:, off:off + 8, :],
                                                               scalar1=1.0, scalar2=None, op0=ALU.add),
                     R=[mod.b], W=[tmp.b])
                for j in range(2):
                    S.op("dve", lambda e, j=j, dst=dst, g=g: e.tensor_tensor(out=dst.t[:, :, j], in0=tmp.t[:, :, j],
                                                                             in1=g.t[:, :], op=ALU.mult),
                         R=[tmp.b, g.b], W=[dst.b])

    def layer(self, l, xcur, xnext):
        S = self.S
        S.barrier()
        S.newsems()
        with S.phase():
            P = dict(mod=S.sb("mod", [128, 48, 2], F32), s1=S.sb("s1", [128, 8, 2], F32),
                     s2=S.sb("s2", [128, 8, 2], F32), lg=S.sb("lg", [128, 8], F32))
            self.emit_mod(l, P)
            self.emit_lconst(l, P)
            if self.stop == "M" and l == self.stop_layer:
                return False
            self.phase_A(l, xcur, P)
            if self.stop == "A" and l == self.stop_layer:
                return False
            self.exchange(l)
            if self.stop == "X" and l == self.stop_layer:
                return False
            self.phase_B(l, P)
            if self.stop == "B" and l == self.stop_layer:
                return False
            self.phase_C(l, xcur, xnext, P)
            if self.stop == "C" and l == self.stop_layer:
                return False
        return True

    def exchange(self, l):
        S, dr, db = self.S, self.dr, self.db
        grp = [list(range(NC))]
        S.cc(lambda e: e.collective_compute("AllGather", ALU.bypass, replica_groups=grp,
                                            ins=[dr["EXP"].opt()], outs=[dr["GAT8"].opt()]),
             R=[db["EXP"]], W=[db["GAT8"]])
        S.cc(lambda e: e.collective_compute("AllGather", ALU.bypass, replica_groups=grp,
                                            ins=[dr["RETX"].opt()], outs=[dr["RETG8"].opt()]),
             R=[db["RETX"]], W=[db["RETG8"]])
        S.barrier()

        def cpb(e, which):
            if not hasattr(self, "_dynv"):
                pid = e.partition_id()
                rb = e.snap((pid // 4) * 4, min_val=0, max_val=4)
                rp = e.snap(rb + ((pid % 4) + 3) % 4, min_val=0, max_val=7)
                rn = e.snap(rb + ((pid % 4) + 1) % 4, min_val=0, max_val=7)
                self._dynv = (rb, rp, rn)
            rb, rp, rn = self._dynv
            if which < 4:
                return e.dma_start(out=dr["LATB"][which * 160:(which + 1) * 160, :],
                                   in_=dr["GAT8"][bass.ds((rb + which) * 256, 160), :])
            if which == 4:
                return e.dma_start(out=dr["RETG"], in_=dr["RETG8"][bass.ds(rb * 64, 256), :])
            rk = rp if which == 5 else rn
            return e.dma_start(out=dr["HALOP" if which == 5 else "HALON"], in_=dr["GAT8"][bass.ds(rk * 256 + 160, 96), :])

        for w in range(4):
            self.dyn_dma("pool", lambda e, w=w: cpb(e, w), [db["GAT8"]], [db["LATB"]])
        self.dyn_dma("pool", lambda e: cpb(e, 4), [db["RETG8"]], [db["RETG"]])
        self.dyn_dma("pool", lambda e: cpb(e, 5), [db["GAT8"]], [db["HALOP"]])
        self.dyn_dma("pool", lambda e: cpb(e, 6), [db["GAT8"]], [db["HALON"]])
        S.barrier()

    def emit_lconst(self, l, P):
        S, dr, db = self.S, self.dr, self.db
        lg = P["lg"]
        with S.phase():
            dec = S.sb("dec", [128, 8], F32)
            e1 = S.sb("e1", [128, 8], F32)
            S.dma("sp", dec.t[:, :], dr["decB"][l], R=[db["decB"]], W=[dec.b])
            S.op("act", lambda e: e.activation(out=e1.t[:, :], in_=dec.t[:, :], func=AF.Exp, scale=-1.0), R=[dec.b], W=[e1.b])
            S.op("dve", lambda e: e.tensor_scalar(out=e1.t[:, :], in0=e1.t[:, :], scalar1=1.0, scalar2=None, op0=ALU.add),
                 R=[e1.b], W=[e1.b])
            S.op("act", lambda e: e.activation(out=dec.t[:, :], in_=e1.t[:, :], func=AF.Ln), R=[e1.b], W=[dec.b])
            S.op("dve", lambda e: e.tensor_scalar(out=lg.t[:, :], in0=dec.t[:, :], scalar1=-1.0, scalar2=None, op0=ALU.mult),
                 R=[dec.b], W=[lg.b])

    def phase_A(self, l, xcur, P):
        S, dr, db = self.S, self.dr, self.db
        mod, s1, lg = P["mod"], P["s1"], P["lg"]
        with S.phase():
            win = S.sb("win", [128, 8, XW], BF16)
            for k in range(8):
                S.dma("pool", win.t[:, k, :], dr["win_g"][l * D + k * 128:l * D + (k + 1) * 128, :],
                      R=[db["win_g"]], W=[win.b])
            wuq = S.sb("wuq", [128, 2, 768], BF16)
            S.dma("pool", wuq.t[:, :, :], dr["wuq"][l].rearrange("(k p) c -> p k c", p=128), R=[db["wuq"]], W=[wuq.b])
            ident = S.sb("ident", [128, 128], BF16)
            S.dma("pool", ident.t[:, :], dr["ident"], R=[db["ident"]], W=[ident.b])
            ones = S.sb("ones", [128, 128], F32)
            S.op("pool", lambda e: e.memset(ones.t[:, :], 1.0), W=[ones.b])
            qn = S.sb("qn", [128, 2], F32)
            kvn = S.sb("kvn", [128, 1], F32)
            S.dma("sp", qn.t[:, :], dr["qnT"][l], R=[db["qnT"]], W=[qn.b])
            S.dma("sp", kvn.t[:, :], dr["kvnT"][l], R=[db["kvnT"]], W=[kvn.b])
            rete = S.sb("rete", [128, 2192], F32)
            S.dma("sp", rete.t[:, :], dr["rete"], R=[db["rete"]], W=[rete.b])
            c128 = S.sb("c128", [128, 64], F32)
            S.op("pool", lambda e: e.memset(c128.t[:, :], 128.0), W=[c128.b])
            KDt = S.sb("KDt", [128, 2, 256], F32)
            Gt = S.sb("Gt", [128, 2, 256], F32)
            for d in range(2):
                for h in range(4):
                    col = lg.t[:, d * 4 + h:d * 4 + h + 1]
                    S.op("act", lambda e, d=d, h=h, col=col: e.activation(
                        out=KDt.t[:, d, h * 64:(h + 1) * 64], in_=rete.t[:, 2048 + 64 * d:2112 + 64 * d], func=AF.Exp,
                        scale=col), R=[rete.b, lg.b], W=[KDt.b])
                    S.op("act", lambda e, d=d, h=h, col=col: e.activation(
                        out=Gt.t[:, d, h * 64:(h + 1) * 64], in_=c128.t[:, :], func=AF.Exp, scale=col),
                         R=[c128.b, lg.b], W=[Gt.b])
            Zf = S.sb("Zf", [64, 512], F32)
            Pc = S.sb("Pc", [64, 256], F32)
            ptmp = S.sb("ptmp", [64, 256], F32)
            S.op("pool", lambda e: e.memset(Zf.t[:, :], 0.0), W=[Zf.b])
            S.op("pool", lambda e: e.memset(Pc.t[:, :], 1.0), W=[Pc.b])
            xs2 = [S.sb("xs", [128, 8, 512], F32) for _ in range(2)]
            hT2 = [S.sb("hT", [128, 8, 512], BF16) for _ in range(2)]
            sq2 = [S.sb("sq", [128, 512], F32) for _ in range(2)]
            tmp2 = [S.sb("ntmp", [128, 512], F32) for _ in range(2)]
            rstd = S.sb("rstd", [128, 512], F32)
            rstd2 = S.sb("rstd2", [128, 512], F32)
            C64 = [S.sb("C64", [128, 512], F32) for _ in range(2)]
            S64 = [S.sb("S64", [128, 512], F32) for _ in range(2)]
            Cq = [S.sb("Cq", [96, 512], F32) for _ in range(2)]
            Sq = [S.sb("Sq", [96, 512], F32) for _ in range(2)]
            C32 = [S.sb("C32", [32, 512], F32) for _ in range(2)]
            S32 = [S.sb("S32", [32, 512], F32) for _ in range(2)]
            for i in range(2):
                S.op("pool", lambda e, i=i: e.memset(Cq[i].t[0:64, :], 1.0), W=[Cq[i].b])
                S.op("pool", lambda e, i=i: e.memset(Sq[i].t[0:64, :], 0.0), W=[Sq[i].b])
            t1s = [S.sb("t1", [128, 512], F32) for _ in range(3)]
            t2s = [S.sb("t2", [128, 512], F32) for _ in range(3)]
            obs = [S.sb("ob", [128, 512], BF16) for _ in range(6)]
            cqn = S.sb("cqn", [128, 2, 512], BF16)
            rkbf = S.sb("rkbf", [128, 2, 512], BF16)
            vts = [S.sb("vt", [128, 640], BF16) for _ in range(3)]
            kscs = [S.sb("ksc", [128, 2, 256], BF16) for _ in range(2)]
            Ucs = [S.sb("Uc", [64, 512], F32) for _ in range(2)]
            st = dict(ps=0, t=0, ob=0)

            def nps():
                st["ps"] = st["ps"] % 5 + 1
                return self.ps[st["ps"]]

            def nt():
                st["t"] = (st["t"] + 1) % 3
                return t1s[st["t"]], t2s[st["t"]]

            def nob():
                st["ob"] = (st["ob"] + 1) % 6
                return obs[st["ob"]]

            def proj(hT, W, col, M):
                ps = nps()
                for k in range(8):
                    self.mm(ps.t[0:M, :W], win.t[:, k, col:col + M], hT.t[:, k, :W], k == 0, k == 7, [win.b, hT.b], [ps.b])
                return ps

            def rope(psA, psB, Ct, St, M, W, ob, obap, scale=None):
                t1, t2 = nt()
                if scale is None:
                    S.op("dve", lambda e: e.tensor_tensor(out=t1.t[0:M, :W], in0=psA.t[0:M, :W], in1=Ct.t[0:M, :W], op=ALU.mult),
                         R=[psA.b, Ct.b], W=[t1.b])
                    S.op("dve", lambda e: e.tensor_tensor(out=t2.t[0:M, :W], in0=psB.t[0:M, :W], in1=St.t[0:M, :W], op=ALU.mult),
                         R=[psB.b, St.b], W=[t2.b])
                else:
                    S.op("dve", lambda e: e.scalar_tensor_tensor(out=t1.t[0:M, :W], in0=psA.t[0:M, :W], scalar=scale,
                                                                 in1=Ct.t[0:M, :W], op0=ALU.mult, op1=ALU.mult),
                         R=[psA.b, Ct.b], W=[t1.b])
                    S.op("dve", lambda e: e.scalar_tensor_tensor(out=t2.t[0:M, :W], in0=psB.t[0:M, :W], scalar=scale,
                                                                 in1=St.t[0:M, :W], op0=ALU.mult, op1=ALU.mult),
                         R=[psB.b, St.b], W=[t2.b])
                S.op("pool", lambda e: e.tensor_tensor(out=obap, in0=t1.t[0:M, :W], in1=t2.t[0:M, :W], op=ALU.add),
                     R=[t1.b, t2.b], W=[ob.b])

            expm = dr["EXP"][0:2560, :].rearrange("(f a) c -> f (a c)", a=16)

            for g, (t0, W) in enumerate(GROUPS):
                lat = g < 8
                j = 0 if lat else 1
                xs, hT = xs2[g % 2], hT2[g % 2]
                S.dma("sp", xs.t[:, :, :W], dr[xcur][:, t0:t0 + W].rearrange("(k p) w -> p k w", p=128),
                      R=[db[xcur]], W=[xs.b])
                c64, s64, cq, sq_, c32, s32 = C64[g % 2], S64[g % 2], Cq[g % 2], Sq[g % 2], C32[g % 2], S32[g % 2]
                for hh in range(2):
                    S.dma("sp", c64.t[64 * hh:64 * hh + 64, :W], dr["c64"][:, t0:t0 + W], R=[db["c64"]], W=[c64.b])
                    S.dma("sp", s64.t[64 * hh:64 * hh + 64, :W], dr["s64"][:, t0:t0 + W], R=[db["s64"]], W=[s64.b])
                S.dma("sp", cq.t[64:96, :W], dr["c32"][:, t0:t0 + W], R=[db["c32"]], W=[cq.b])
                S.dma("sp", sq_.t[64:96, :W], dr["s32"][:, t0:t0 + W], R=[db["s32"]], W=[sq_.b])
                S.dma("sp", c32.t[:, :W], dr["c32"][:, t0:t0 + W], R=[db["c32"]], W=[c32.b])
                S.dma("sp", s32.t[:, :W], dr["s32"][:, t0:t0 + W], R=[db["s32"]], W=[s32.b])
                self.norm_group(xs, W, lambda k: s1.t[:, k, j:j + 1], lambda k: mod.t[:, k, j:j + 1], hT, ones, 0,
                                sq2, rstd, tmp2, scale_sq=1.0 / 32.0)
                psa = proj(hT, W, XO["cq"], 128)
                psb_ = proj(hT, W, XO["cq"] + 128, 128)
                pst = self.ps[0]
                for k, pp in enumerate((psa, psb_)):
                    S.op("act", lambda e, k=k, pp=pp: e.activation(out=sq2[k].t[:, :W], in_=pp.t[:, :W], func=AF.Square,
                                                                   scale=1.0 / 16.0), R=[pp.b], W=[sq2[k].b])
                    self.mm(pst.t[:, :W], ones.t[:, :], sq2[k].t[:, :W], k == 0, k == 1, [ones.b, sq2[k].b], [pst.b])
                self.rstd_op(rstd2.t[:, :W], pst.t[:, :W], [pst.b], [rstd2.b])
                for k, pp in enumerate((psa, psb_)):
                    S.op("dve", lambda e, k=k, pp=pp: e.scalar_tensor_tensor(
                        out=cqn.t[:, k, :W], in0=pp.t[:, :W], scalar=qn.t[:, k:k + 1], in1=rstd2.t[:, :W],
                        op0=ALU.mult, op1=ALU.mult), R=[pp.b, qn.b, rstd2.b], W=[cqn.b])
                for h in range(4):
                    p1, p2 = nps(), nps()
                    for kc in range(2):
                        self.mm(p1.t[0:96, :W], wuq.t[:, kc, h * 192:h * 192 + 96], cqn.t[:, kc, :W], kc == 0, kc == 1,
                                [wuq.b, cqn.b], [p1.b])
                    for kc in range(2):
                        self.mm(p2.t[0:96, :W], wuq.t[:, kc, h * 192 + 96:h * 192 + 192], cqn.t[:, kc, :W], kc == 0, kc == 1,
                                [wuq.b, cqn.b], [p2.b])
                    ob = nob()
                    rope(p1, p2, cq, sq_, 96, W, ob, ob.t[0:96, :W])
                    S.dma("pool", dr["qmT"][h * 96:(h + 1) * 96, t0:t0 + W], ob.t[0:96, :W], R=[ob.b], W=[db["qmT"]])
                pp = proj(hT, W, XO["ckv"], 128)
                S.op("act", lambda e, pp=pp: e.activation(out=sq2[0].t[:, :W], in_=pp.t[:, :W], func=AF.Square,
                                                          scale=float(128.0 ** -0.5)), R=[pp.b], W=[sq2[0].b])
                self.mm(pst.t[:, :W], ones.t[:, :], sq2[0].t[:, :W], True, True, [ones.b, sq2[0].b], [pst.b])
                self.rstd_op(rstd2.t[:, :W], pst.t[:, :W], [pst.b], [rstd2.b])
                ob = nob()
                S.op("dve", lambda e, pp=pp, ob=ob: e.scalar_tensor_tensor(
                    out=ob.t[:, :W], in0=pp.t[:, :W], scalar=kvn.t[:, 0:1], in1=rstd2.t[:, :W], op0=ALU.mult, op1=ALU.mult),
                     R=[pp.b, kvn.b, rstd2.b], W=[ob.b])
                if lat:
                    S.dma("pool", expm[0:128, t0:t0 + W], ob.t[:, :W], R=[ob.b], W=[db["EXP"]])
                else:
                    S.dma("pool", dr["latc"][0:128, :], ob.t[:, :W], R=[ob.b], W=[db["latc"]])
                p1 = proj(hT, W, XO["kr"], 32)
                p2 = proj(hT, W, XO["krp"], 32)
                ob = nob()
                rope(p1, p2, c32, s32, 32, W, ob, ob.t[0:32, :W])
                if lat:
                    S.dma("pool", expm[128:160, t0:t0 + W], ob.t[0:32, :W], R=[ob.b], W=[db["EXP"]])
                else:
                    S.dma("pool", dr["latc"][128:160, :], ob.t[0:32, :W], R=[ob.b], W=[db["latc"]])
                for jj in range(2):
                    p1 = proj(hT, W, XO["rq"] + 128 * jj, 128)
                    p2 = proj(hT, W, XO["rqp"] + 128 * jj, 128)
                    ob = nob()
                    rope(p1, p2, c64, s64, 128, W, ob, ob.t[:, :W])
                    S.dma("pool", dr["rqT"][128 * jj:128 * jj + 128, t0:t0 + W], ob.t[:, :W], R=[ob.b], W=[db["rqT"]])
                for jj in range(2):
                    p1 = proj(hT, W, XO["rk"] + 128 * jj, 128)
                    p2 = proj(hT, W, XO["rkp"] + 128 * jj, 128)
                    rope(p1, p2, c64, s64, 128, W, rkbf, rkbf.t[:, jj, :W], scale=0.125)
                    S.dma("pool", dr["rkT"][128 * jj:128 * jj + 128, t0:t0 + W], rkbf.t[:, jj, :W], R=[rkbf.b], W=[db["rkT"]])
                for jj in range(2):
                    p1 = proj(hT, W, XO["sq"] + 128 * jj, 128)
                    p2 = proj(hT, W, XO["sqp"] + 128 * jj, 128)
                    ob = nob()
                    rope(p1, p2, c64, s64, 128, W, ob, ob.t[:, :W], scale=0.125)
                    S.dma("pool", dr["sqT"][128 * jj:128 * jj + 128, t0:t0 + W], ob.t[:, :W], R=[ob.b], W=[db["sqT"]])
                p1 = proj(hT, W, XO["sk"], 128)
                p2 = proj(hT, W, XO["skp"], 128)
                ob = nob()
                rope(p1, p2, c64, s64, 128, W, ob, ob.t[:, :W])
                S.dma("pool", dr["skT"][:, t0:t0 + W], ob.t[:, :W], R=[ob.b], W=[db["skT"]])
                if g == 0:
                    S.dma("pool", dr["EXP"][EX["skh"]:EX["skh"] + 128, 0:128], ob.t[:, 0:128], R=[ob.b], W=[db["EXP"]])
                if g == 7:
                    S.dma("pool", dr["EXP"][EX["skt"]:EX["skt"] + 128, 0:128], ob.t[:, 384:512], R=[ob.b], W=[db["EXP"]])
                for nm, dst in (("gf", "sgfT"), ("gb", "sgbT")):
                    for jj in range(2):
                        pp = proj(hT, W, XO[nm] + 128 * jj, 128)
                        ob = nob()
                        S.op("act", lambda e, pp=pp, ob=ob: e.activation(out=ob.t[:, :W], in_=pp.t[:, :W], func=AF.Silu),
                             R=[pp.b], W=[ob.b])
                        S.dma("pool", dr[dst][128 * jj:128 * jj + 128, t0:t0 + W], ob.t[:, :W], R=[ob.b], W=[db[dst]])
                for jj in range(2):
                    pp = proj(hT, W, XO["nq"] + 128 * jj, 128)
                    ob = nob()
                    S.op("act", lambda e, pp=pp, ob=ob: e.activation(out=ob.t[:, :W], in_=pp.t[:, :W], func=AF.Identity,
                                                                     scale=0.125), R=[pp.b], W=[ob.b])
                    S.dma("pool", dr["nqT"][128 * jj:128 * jj + 128, t0:t0 + W], ob.t[:, :W], R=[ob.b], W=[db["nqT"]])
                for jj in range(2):
                    pp = proj(hT, W, XO["nk"] + 128 * jj, 128)
                    ob = nob()
                    S.op("act", lambda e, pp=pp, ob=ob: e.activation(out=ob.t[:, :W], in_=pp.t[:, :W], func=AF.Identity),
                         R=[pp.b], W=[ob.b])
                    S.dma("pool", dr["nkT"][128 * jj:128 * jj + 128, t0:t0 + W], ob.t[:, :W], R=[ob.b], W=[db["nkT"]])
                    if g == 0:
                        S.dma("pool", dr["EXP"][EX["nkh"] + 128 * jj:EX["nkh"] + 128 * jj + 128, :], ob.t[:, 0:256],
                              R=[ob.b], W=[db["EXP"]])
                    if g == 7:
                        S.dma("pool", dr["EXP"][EX["nkt"] + 128 * jj:EX["nkt"] + 128 * jj + 128, :], ob.t[:, 256:512],
                              R=[ob.b], W=[db["EXP"]])
                for a in range(W // 128):
                    c = (t0 + 128 * a) // 128
                    vt = vts[c % 3]
                    pv = nps()
                    for k in range(8):
                        self.mm(pv.t[:, 0:512], hT.t[:, k, 128 * a:128 * a + 128], win.t[:, k, XO["rv"]:XO["rv"] + 512],
                                k == 0, k == 7, [hT.b, win.b], [pv.b])
                    pv2 = nps()
                    for k in range(8):
                        self.mm(pv2.t[:, 0:128], hT.t[:, k, 128 * a:128 * a + 128], win.t[:, k, XO["sv"]:XO["sv"] + 128],
                                k == 0, k == 7, [hT.b, win.b], [pv2.b])
                    S.op("act", lambda e, pv=pv, vt=vt: e.activation(out=vt.t[:, 0:512], in_=pv.t[:, 0:512], func=AF.Identity),
                         R=[pv.b], W=[vt.b])
                    S.op("act", lambda e, pv2=pv2, vt=vt: e.activation(out=vt.t[:, 512:640], in_=pv2.t[:, 0:128], func=AF.Identity),
                         R=[pv2.b], W=[vt.b])
                    r0 = t0 + 128 * a
                    S.dma("pool", dr["rvtm"][r0:r0 + 128, :], vt.t[:, 0:256], R=[vt.b], W=[db["rvtm"]])
                    S.dma("pool", dr["nvtm"][r0:r0 + 128, :], vt.t[:, 256:512], R=[vt.b], W=[db["nvtm"]])
                    S.dma("pool", dr["svtm"][r0:r0 + 128, :], vt.t[:, 512:640], R=[vt.b], W=[db["svtm"]])
                    if g == 0 and a < 2:
                        S.dma("pool", dr["EXP"][EX["nvh"] + 128 * a:EX["nvh"] + 128 * a + 128, :], vt.t[:, 256:512],
                              R=[vt.b], W=[db["EXP"]])
                    if g == 0 and a == 0:
                        S.dma("pool", dr["EXP"][EX["svh"]:EX["svh"] + 128, 0:128], vt.t[:, 512:640], R=[vt.b], W=[db["EXP"]])
                    if g == 7 and a >= 2:
                        S.dma("pool", dr["EXP"][EX["nvt"] + 128 * (a - 2):EX["nvt"] + 128 * (a - 2) + 128, :], vt.t[:, 256:512],
                              R=[vt.b], W=[db["EXP"]])
                    if g == 7 and a == 3:
                        S.dma("pool", dr["EXP"][EX["svt"]:EX["svt"] + 128, 0:128], vt.t[:, 512:640], R=[vt.b], W=[db["EXP"]])
                    pb = self.psb
                    for jj in range(2):
                        S.op("pe", lambda e, jj=jj, a=a: e.transpose(pb.t[:, 128 * jj:128 * jj + 128],
                                                                     rkbf.t[:, jj, 128 * a:128 * a + 128], ident.t[:, :]),
                             R=[rkbf.b, ident.b], W=[pb.b])
                    ksc = kscs[c % 2]
                    for d in range(2):
                        S.op("dve", lambda e, d=d, ksc=ksc: e.tensor_tensor(out=ksc.t[:, d, :], in0=pb.t[:, 0:256],
                                                                            in1=KDt.t[:, d, :], op=ALU.mult),
                             R=[pb.b, KDt.b], W=[ksc.b])
                    pu = self.ps[6]
                    for d in range(2):
                        for h in range(4):
                            self.mm(pu.t[0:64, (d * 4 + h) * 64:(d * 4 + h) * 64 + 64], ksc.t[:, d, h * 64:(h + 1) * 64],
                                    vt.t[:, h * 64:(h + 1) * 64], True, True, [ksc.b, vt.b], [pu.b])
                    Uc = Ucs[c % 2]
                    S.op("act", lambda e, Uc=Uc: e.activation(out=Uc.t[:, :], in_=pu.t[0:64, :], func=AF.Identity),
                         R=[pu.b], W=[Uc.b])
                    S.dma("pool", dr["UD"][c * 64:(c + 1) * 64, :], Uc.t[:, :], R=[Uc.b], W=[db["UD"]])
                    if lat:
                        S.op("pool", lambda e: e.tensor_tensor(out=Zf.t[:, 0:256], in0=Zf.t[:, 0:256], in1=Gt.t[0:64, 0, :], op=ALU.mult),
                             R=[Zf.b, Gt.b], W=[Zf.b])
                        S.op("pool", lambda e, Uc=Uc: e.tensor_tensor(out=Zf.t[:, 0:256], in0=Zf.t[:, 0:256], in1=Uc.t[:, 0:256], op=ALU.add),
                             R=[Zf.b, Uc.b], W=[Zf.b])
                        S.op("pool", lambda e, Uc=Uc: e.tensor_tensor(out=ptmp.t[:, :], in0=Pc.t[:, :], in1=Uc.t[:, 256:512], op=ALU.mult),
                             R=[Pc.b, Uc.b], W=[ptmp.b])
                        S.op("pool", lambda e: e.tensor_tensor(out=Zf.t[:, 256:512], in0=Zf.t[:, 256:512], in1=ptmp.t[:, :], op=ALU.add),
                             R=[Zf.b, ptmp.b], W=[Zf.b])
                        S.op("pool", lambda e: e.tensor_tensor(out=Pc.t[:, :], in0=Pc.t[:, :], in1=Gt.t[0:64, 1, :], op=ALU.mult),
                             R=[Pc.b, Gt.b], W=[Pc.b])
            S.dma("pool", dr["RETX"], Zf.t[:, :], R=[Zf.b], W=[db["RETX"]])

    def attend(self, A, QT, dq, W, blocks, scale, dst_ap, dst_buf):
        S = self.S
        if "skip_attend" in self.flags:
            return
        st = A["st"]
        st["po"] = 1 - st["po"]
        po = self.ps[4 + st["po"]]
        nb = len(blocks)
        for bi, blk in enumerate(blocks):
            if blk[0] == "k":
                _, kap, kb, vap, vb, bap, bb = blk
                st["ps"] = (st["ps"] + 1) % 4
                pS = self.ps[st["ps"]]
                self.mm(pS.t[:, :W], kap, QT.t[0:dq, :W], True, bap is None, list(kb) + [QT.b], [pS.b])
                if bap is not None:
                    self.mm(pS.t[:, :W], A["ident"].t[:, :], bap, False, True, [A["ident"].b] + list(bb), [pS.b])
                st["pt"] = (st["pt"] + 1) % len(A["Pt"])
                Pt = A["Pt"][st["pt"]]
                S.op("act", lambda e, pS=pS, Pt=Pt: e.activation(out=Pt.t[:, :W], in_=pS.t[:, :W], func=AF.Exp, scale=scale),
                     R=[pS.b], W=[Pt.b])
                self.mm(po.t[0:65, :W], vap, Pt.t[:, :W], bi == 0, bi == nb - 1, list(vb) + [Pt.b], [po.b])
            else:
                _, pap, pb, vap, vb = blk
                self.mm(po.t[0:65, :W], vap, pap, bi == 0, bi == nb - 1, list(vb) + list(pb), [po.b])
        rec, oc, sel = A["rec"], A["oc"], A["sel"]
        pB = self.ps[6]
        S.op("act", lambda e: e.activation(out=oc.t[0:65, :W], in_=po.t[0:65, :W], func=AF.Identity), R=[po.b], W=[oc.b])
        self.mm(pB.t[0:64, :W], sel.t[0:65, 0:64], oc.t[0:65, :W], True, True, [sel.b, oc.b], [pB.b])
        S.op("dve", lambda e: e.reciprocal(out=rec.t[0:64, :W], in_=pB.t[0:64, :W]), R=[pB.b], W=[rec.b])
        st["ob"] = (st["ob"] + 1) % len(A["ob"])
        ob = A["ob"][st["ob"]]
        S.op("dve", lambda e: e.tensor_tensor(out=ob.t[0:64, :W], in0=oc.t[0:64, :W], in1=rec.t[0:64, :W], op=ALU.mult),
             R=[oc.b, rec.b], W=[ob.b])
        S.dma("pool", dst_ap, ob.t[0:64, :W], R=[ob.b], W=[dst_buf])

    def attn_tiles(self, with_ident=True):
        S, dr, db = self.S, self.dr, self.db
        A = dict(Pt=[S.sb("Pt", [128, 512], BF16) for _ in range(4)], rec=S.sb("rec", [128, 512], F32),
                 oc=S.sb("oc", [65, 512], F32), ob=[S.sb("aob", [64, 512], BF16) for _ in range(2)],
                 sel=S.sb("sel", [65, 64], F32), st=dict(ps=0, pt=0, po=0, ob=0),
                 QT=[S.sb("QT", [96, 512], BF16) for _ in range(2)])
        S.op("pool", lambda e: e.memset(A["sel"].t[0:65, :], 0.0), W=[A["sel"].b])
        S.op("pool", lambda e: e.memset(A["sel"].t[64:65, :], 1.0), W=[A["sel"].b])
        if with_ident:
            A["ident"] = S.sb("identb", [128, 128], BF16)
            S.dma("pool", A["ident"].t[:, :], dr["ident"], R=[db["ident"]], W=[A["ident"].b])
        return A

    def dyn_dma(self, q, fn, R, W):
        o = Op(q, fn, "d")
        self.S._deps(o, R, W)
        self.S.ops.append(o)

    def phase_B(self, l, P):
        if "no_mla" not in self.flags:
            self.phase_B_mla(l)
        if "no_na" not in self.flags:
            self.phase_B_na(l)
        if "no_swa" not in self.flags:
            self.phase_B_swa(l)
        if "no_ret" not in self.flags:
            self.phase_B_ret(l, P)

    def phase_B_mla(self, l):
        S, dr, db = self.S, self.dr, self.db
        NK = 4 * TL + LC
        NB = NK // 128
        scale = float(96.0 ** -0.5)
        with S.phase():
            if "ms0a" in self.flags:
                return
            A = self.attn_tiles(False)
            if "ms0b" in self.flags:
                return
            latK = S.sb("latK", [128, NK], BF16)
            KT = S.sb("KT", [96, NK], BF16)
            V = S.sb("V", [128, NB, 65], BF16)
            wukv = S.sb("wukv", [128, 512], BF16)
            S.dma("pool", wukv.t[:, :], dr["wukv"][l], R=[db["wukv"]], W=[wukv.b])
            S.op("pool", lambda e: e.memset(V.t[:, :, 64:65], 1.0), W=[V.b])
            if "ms1" in self.flags:
                return
            for r in range(4):
                S.dma("sp", latK.t[:, r * TL:(r + 1) * TL], dr["LATB"][r * 160:r * 160 + 128, :], R=[db["LATB"]], W=[latK.b])
                S.dma("sp", KT.t[64:96, r * TL:(r + 1) * TL], dr["LATB"][r * 160 + 128:r * 160 + 160, :], R=[db["LATB"]], W=[KT.b])
            S.dma("sp", latK.t[:, 4 * TL:NK], dr["latc"][0:128, :], R=[db["latc"]], W=[latK.b])
            S.dma("sp", KT.t[64:96, 4 * TL:NK], dr["latc"][128:160, :], R=[db["latc"]], W=[KT.b])
            if "ms2" in self.flags:
                return
            for h in range(1 if "mla_h1" in self.flags else 4):
                nblk = (NK + 511) // 512
                for bi in range(nblk):
                    w = min(512, NK - bi * 512)
                    ps = self.ps[bi % 4]
                    self.mm(ps.t[0:64, :w], wukv.t[:, h * 128:h * 128 + 64], latK.t[:, bi * 512:bi * 512 + w], True, True,
                            [wukv.b, latK.b], [ps.b])
                    eng = "act" if bi % 2 == 0 else "dve"
                    if eng == "act":
                        S.op("act", lambda e, ps=ps, bi=bi, w=w: e.activation(out=KT.t[0:64, bi * 512:bi * 512 + w], in_=ps.t[0:64, :w],
                                                                             func=AF.Identity), R=[ps.b], W=[KT.b])
                    else:
                        S.op("dve", lambda e, ps=ps, bi=bi, w=w: e.tensor_copy(out=KT.t[0:64, bi * 512:bi * 512 + w], in_=ps.t[0:64, :w]),
                             R=[ps.b], W=[KT.b])
                if "ms3" in self.flags:
                    continue
                for b8 in range((NB + 7) // 8):
                    n = min(8, NB - b8 * 8)
                    ps = self.ps[b8 % 4]
                    for i in range(n):
                        kb = b8 * 8 + i
                        self.mm(ps.t[:, 64 * i:64 * i + 64], latK.t[:, kb * 128:kb * 128 + 128], wukv.t[:, h * 128 + 64:h * 128 + 128],
                                True, True, [latK.b, wukv.b], [ps.b])
                    eng = "act" if b8 % 2 == 0 else "dve"
                    src = ps.t[:, 0:64 * n].rearrange("p (a d) -> p a d", d=64)
                    if eng == "act":
                        S.op("act", lambda e, src=src, b8=b8, n=n: e.activation(out=V.t[:, b8 * 8:b8 * 8 + n, 0:64], in_=src, func=AF.Identity),
                             R=[ps.b], W=[V.b])
                    else:
                        S.op("dve", lambda e, src=src, b8=b8, n=n: e.tensor_copy(out=V.t[:, b8 * 8:b8 * 8 + n, 0:64], in_=src),
                             R=[ps.b], W=[V.b])
                for g, (t0, W) in enumerate(GROUPS):
                    QT = A["QT"][g % 2]
                    S.dma("sp", QT.t[0:96, :W], dr["qmT"][h * 96:(h + 1) * 96, t0:t0 + W], R=[db["qmT"]], W=[QT.b])
                    kbs = range(NB) if g < 8 else range(NB - 2, NB)
                    blocks = [("k", KT.t[0:96, kb * 128:kb * 128 + 128], [KT.b], V.t[:, kb, :], [V.b], None, None) for kb in kbs]
                    self.attend(A, QT, 96, W, blocks, scale, dr["mxT"][h * 64:(h + 1) * 64, t0:t0 + W], db["mxT"])

    def phase_B_na(self, l):
        S, dr, db = self.S, self.dr, self.db
        with S.phase():
            A = self.attn_tiles(True)
            rmT = S.sb("rmT", [128, 3, 8, 512], BF16)
            S.dma("sp", rmT.t[:, :, :, :], dr["rm"].rearrange("(v j p) c -> p v j c", v=3, j=8), R=[db["rm"]], W=[rmT.b])
            toeps = [S.sb("toep", [128, 8, 512], F32) for _ in range(2)]
            biass = [S.sb("bias", [128, 3, 8, 512], BF16) for _ in range(2)]
            klocs = [S.sb("kloc", [64, 4608], BF16) for _ in range(2)]
            vlocs = [S.sb("vloc", [128, 36, 65], BF16) for _ in range(2)]
            kctxs = [S.sb("kctx", [64, 256], BF16) for _ in range(2)]
            vctxs = [S.sb("vctx", [128, 2, 65], BF16) for _ in range(2)]
            for i in range(2):
                S.op("pool", lambda e, i=i: e.memset(vlocs[i].t[:, :, 64:65], 1.0), W=[vlocs[i].b])
                S.op("pool", lambda e, i=i: e.memset(vctxs[i].t[:, :, 64:65], 1.0), W=[vctxs[i].b])
            for h in range(4):
                toep, bias, kloc, vloc, kctx, vctx = (x[h % 2] for x in (toeps, biass, klocs, vlocs, kctxs, vctxs))
                r0 = ((l * 4 + h) * 8) * 128
                S.dma("sp", toep.t[:, :, :], dr["toep_g"][r0:r0 + 1024, :].rearrange("(j p) c -> p j c", p=128),
                      R=[db["toep_g"]], W=[toep.b])
                for v in range(3):
                    for j in range(8):
                        eng = "pool" if (v * 8 + j) % 2 == 0 else "dve"
                        S.op(eng, lambda e, v=v, j=j, bias=bias, toep=toep: e.tensor_tensor(
                            out=bias.t[:, v, j, :], in0=toep.t[:, j, :], in1=rmT.t[:, v, j, :], op=ALU.add),
                             R=[toep.b, rmT.b], W=[bias.b])
                hs = slice(h * 64, (h + 1) * 64)
                S.dma("sp", kloc.t[:, 256:256 + TL], dr["nkT"][hs, 0:TL], R=[db["nkT"]], W=[kloc.b])
                S.dma("sp", kctx.t[:, :], dr["nkT"][hs, TL:T], R=[db["nkT"]], W=[kctx.b])
                S.dma("sp", vloc.t[:, 2:34, 0:64], dr["nvtm"][0:TL, hs].rearrange("(a p) d -> p a d", p=128),
                      R=[db["nvtm"]], W=[vloc.b])
                S.dma("sp", vctx.t[:, :, 0:64], dr["nvtm"][TL:T, hs].rearrange("(a p) d -> p a d", p=128),
                      R=[db["nvtm"]], W=[vctx.b])

                HP, HN = (dr[k].rearrange("r (a c) -> (r a) c", c=256) for k in ("HALOP", "HALON"))
                o = lambda k: EX[k] - 2560
                S.dma("sp", kloc.t[:, 0:256], HP[o("nkt") + h * 64:o("nkt") + h * 64 + 64, :], R=[db["HALOP"]], W=[kloc.b])
                S.dma("sp", kloc.t[:, 256 + TL:512 + TL], HN[o("nkh") + h * 64:o("nkh") + h * 64 + 64, :], R=[db["HALON"]], W=[kloc.b])
                S.dma("sp", vloc.t[:, 0:2, 0:64], HP[o("nvt"):o("nvt") + 256, h * 64:(h + 1) * 64].rearrange("(a p) d -> p a d", p=128),
                      R=[db["HALOP"]], W=[vloc.b])
                S.dma("sp", vloc.t[:, 34:36, 0:64], HN[o("nvh"):o("nvh") + 256, h * 64:(h + 1) * 64].rearrange("(a p) d -> p a d", p=128),
                      R=[db["HALON"]], W=[vloc.b])
                for g, (t0, W) in enumerate(GROUPS):
                    QT = A["QT"][g % 2]
                    S.dma("sp", QT.t[0:64, :W], dr["nqT"][hs, t0:t0 + W], R=[db["nqT"]], W=[QT.b])
                    blocks = []
                    if g < 8:
                        v = 0 if g == 0 else (2 if g == 7 else 1)
                        for j in range(8):
                            kb = 4 * g + j
                            blocks.append(("k", kloc.t[:, kb * 128:kb * 128 + 128], [kloc.b], vloc.t[:, kb, :], [vloc.b],
                                           bias.t[:, v, j, :W], [bias.b]))
                    for cb in range(2):
                        blocks.append(("k", kctx.t[:, cb * 128:cb * 128 + 128], [kctx.b], vctx.t[:, cb, :], [vctx.b], None, None))
                    self.attend(A, QT, 64, W, blocks, 1.0, dr["mxT"][512 + h * 64:512 + (h + 1) * 64, t0:t0 + W], db["mxT"])

    def phase_B_swa(self, l):
        S, dr, db = self.S, self.dr, self.db
        with S.phase():
            A = self.attn_tiles(True)
            msT = S.sb("msT", [128, 3, 6, 512], BF16)
            S.dma("sp", msT.t[:, :, :, :], dr["ms"].rearrange("(v j p) c -> p v j c", v=3, j=6), R=[db["ms"]], W=[msT.b])
            sk32 = S.sb("sk32", [1, 2048], F32)
            psink = S.sb("psink", [1, 2048], BF16)
            vsink = S.sb("vsink", [1, 65], BF16)
            S.dma("sp", sk32.t[:, :], dr["sinkB"][l], R=[db["sinkB"]], W=[sk32.b])
            S.op("act", lambda e: e.activation(out=psink.t[:, :], in_=sk32.t[:, :], func=AF.Exp), R=[sk32.b], W=[psink.b])
            S.op("pool", lambda e: e.memset(vsink.t[:, 0:64], 0.0), W=[vsink.b])
            S.op("pool", lambda e: e.memset(vsink.t[:, 64:65], 1.0), W=[vsink.b])
            klocs = [S.sb("skloc", [64, 256 + TL], BF16) for _ in range(2)]
            vlocs = [S.sb("svloc", [128, 34, 65], BF16) for _ in range(2)]
            kctxs = [S.sb("skctx", [64, 256], BF16) for _ in range(2)]
            vctxs = [S.sb("svctx", [128, 2, 65], BF16) for _ in range(2)]
            for i in range(2):
                S.op("pool", lambda e, i=i: e.memset(vlocs[i].t[:, :, 64:65], 1.0), W=[vlocs[i].b])
                S.op("pool", lambda e, i=i: e.memset(vctxs[i].t[:, :, 64:65], 1.0), W=[vctxs[i].b])
            for g2 in range(2):
                kloc, vloc, kctx, vctx = klocs[g2], vlocs[g2], kctxs[g2], vctxs[g2]
                hs = slice(g2 * 64, (g2 + 1) * 64)
                S.dma("sp", kloc.t[:, 128:128 + TL], dr["skT"][hs, 0:TL], R=[db["skT"]], W=[kloc.b])
                S.dma("sp", kctx.t[:, :], dr["skT"][hs, TL:T], R=[db["skT"]], W=[kctx.b])
                S.dma("sp", vloc.t[:, 1:33, 0:64], dr["svtm"][0:TL, hs].rearrange("(a p) d -> p a d", p=128),
                      R=[db["svtm"]], W=[vloc.b])
                S.dma("sp", vctx.t[:, :, 0:64], dr["svtm"][TL:T, hs].rearrange("(a p) d -> p a d", p=128),
                      R=[db["svtm"]], W=[vctx.b])

                HP, HN = (dr[k].rearrange("r (a c) -> (r a) c", c=256) for k in ("HALOP", "HALON"))
                o = lambda k: EX[k] - 2560
                S.dma("sp", kloc.t[:, 0:128], HP[o("skt") + g2 * 64:o("skt") + g2 * 64 + 64, 0:128], R=[db["HALOP"]], W=[kloc.b])
                S.dma("sp", kloc.t[:, 128 + TL:256 + TL], HN[o("skh") + g2 * 64:o("skh") + g2 * 64 + 64, 0:128], R=[db["HALON"]], W=[kloc.b])
                S.dma("sp", vloc.t[:, 0, 0:64], HP[o("svt"):o("svt") + 128, g2 * 64:(g2 + 1) * 64], R=[db["HALOP"]], W=[vloc.b])
                S.dma("sp", vloc.t[:, 33, 0:64], HN[o("svh"):o("svh") + 128, g2 * 64:(g2 + 1) * 64], R=[db["HALON"]], W=[vloc.b])
                for h in (2 * g2, 2 * g2 + 1):
                    for g, (t0, W) in enumerate(GROUPS):
                        QT = A["QT"][g % 2]
                        S.dma("sp", QT.t[0:64, :W], dr["sqT"][h * 64:(h + 1) * 64, t0:t0 + W], R=[db["sqT"]], W=[QT.b])
                        blocks = []
                        if g < 8:
                            v = 0 if g == 0 else (2 if g == 7 else 1)
                            for j in range(6):
                                kb = 4 * g + j
                                blocks.append(("k", kloc.t[:, kb * 128:kb * 128 + 128], [kloc.b], vloc.t[:, kb, :], [vloc.b],
                                               msT.t[:, v, j, :W], [msT.b]))
                        for cb in range(2):
                            blocks.append(("k", kctx.t[:, cb * 128:cb * 128 + 128], [kctx.b], vctx.t[:, cb, :], [vctx.b], None, None))
                        blocks.append(("p", psink.t[0:1, h * 512:h * 512 + W], [psink.b], vsink.t[0:1, :], [vsink.b]))
                        self.attend(A, QT, 64, W, blocks, 1.0, dr["mxT"][768 + h * 64:768 + (h + 1) * 64, t0:t0 + W], db["mxT"])

    def phase_B_ret(self, l, P):
        S, dr, db = self.S, self.dr, self.db
        lg = P["lg"]
        RO = dict(ef=0, eb=512, qf=1024, qb=1536, kf=2048, kb=2112, rk=2176)
        with S.phase():
            rete = S.sb("rete", [128, 2192], F32)
            S.dma("sp", rete.t[:, :], dr["rete"], R=[db["rete"]], W=[rete.b])
            c128 = S.sb("c128", [128, 64], F32)
            S.op("pool", lambda e: e.memset(c128.t[:, :], 128.0), W=[c128.b])
            ones = S.sb("ones", [128, 64], F32)
            S.op("pool", lambda e: e.memset(ones.t[:, :], 1.0), W=[ones.b])
            Dm = S.sb("Dm", [128, 2, 4, 512], F32)
            QD = S.sb("QD", [128, 2, 4, 512], F32)
            Gt = S.sb("Gt", [128, 2, 256], F32)
            coef = S.sb("coef", [128, 2, 4, 5], F32)
            for d in range(2):
                for h in range(4):
                    col = lg.t[:, d * 4 + h:d * 4 + h + 1]
                    S.op("act", lambda e, d=d, h=h, col=col: e.activation(out=Dm.t[:, d, h, :], in_=rete.t[:, 512 * d:512 * d + 512],
                                                                          func=AF.Exp, scale=col), R=[rete.b, lg.b], W=[Dm.b])
                    S.op("act", lambda e, d=d, h=h, col=col: e.activation(out=QD.t[:, d, h, :], in_=rete.t[:, 1024 + 512 * d:1536 + 512 * d],
                                                                          func=AF.Exp, scale=col), R=[rete.b, lg.b], W=[QD.b])
                    S.op("act", lambda e, d=d, h=h, col=col: e.activation(out=Gt.t[:, d, h * 64:(h + 1) * 64], in_=c128.t[:, :],
                                                                          func=AF.Exp, scale=col), R=[c128.b, lg.b], W=[Gt.b])
                    S.op("act", lambda e, d=d, h=h, col=col: e.activation(out=coef.t[:, d, h, :], in_=rete.t[:, RO["rk"] + 5 * d:RO["rk"] + 5 * d + 5],
                                                                          func=AF.Exp, scale=col), R=[rete.b, lg.b], W=[coef.b])
            UDs = S.sb("UDs", [64, 34, 512], F32)
            S.dma("sp", UDs.t[:, :, :], dr["UD"].rearrange("(c p) f -> p c f", p=64), R=[db["UD"]], W=[UDs.b])
            AG = S.sb("AG", [64, 4, 512], F32)
            S.dma("sp", AG.t[:, :, :], dr["RETG"].rearrange("(i p) f -> p i f", p=64), R=[db["RETG"]], W=[AG.b])
            Sall = S.sb("Sall", [64, 2, 34, 256], BF16)
            Rs = S.sb("Rs", [64, 2, 256], F32)
            sctx = S.sb("sctx", [64, 2, 256], F32)
            S.op("pool", lambda e: e.memset(Sall.t[:, 0, 32, :], 0.0), W=[Sall.b])
            S.op("pool", lambda e: e.memset(Sall.t[:, 1, 33, :], 0.0), W=[Sall.b])
            S.op("pool", lambda e: e.tensor_copy(out=Sall.t[:, 0, 33, :], in_=UDs.t[:, 32, 0:256]), R=[UDs.b], W=[Sall.b])
            S.op("pool", lambda e: e.tensor_copy(out=Sall.t[:, 1, 32, :], in_=UDs.t[:, 33, 256:512]), R=[UDs.b], W=[Sall.b])
            S.op("pool", lambda e: e.tensor_tensor(out=sctx.t[:, 0, :], in0=UDs.t[:, 32, 0:256], in1=Gt.t[0:64, 0, :], op=ALU.mult),
                 R=[UDs.b, Gt.b], W=[sctx.b])
            S.op("pool", lambda e: e.tensor_tensor(out=sctx.t[:, 0, :], in0=sctx.t[:, 0, :], in1=UDs.t[:, 33, 0:256], op=ALU.add),
                 R=[UDs.b, sctx.b], W=[sctx.b])
            S.op("pool", lambda e: e.tensor_tensor(out=sctx.t[:, 1, :], in0=UDs.t[:, 33, 256:512], in1=Gt.t[0:64, 1, :], op=ALU.mult),
                 R=[UDs.b, Gt.b], W=[sctx.b])
            S.op("pool", lambda e: e.tensor_tensor(out=sctx.t[:, 1, :], in0=sctx.t[:, 1, :], in1=UDs.t[:, 32, 256:512], op=ALU.add),
                 R=[UDs.b, sctx.b], W=[sctx.b])
            for d in range(2):
                for h in range(4):
                    hs = slice(h * 64, (h + 1) * 64)
                    S.op("dve", lambda e, d=d, h=h, hs=hs: e.tensor_scalar(out=Rs.t[:, d, hs], in0=sctx.t[:, d, hs],
                                                                           scalar1=coef.t[0:64, d, h, 4:5], scalar2=None, op0=ALU.mult),
                         R=[sctx.b, coef.b], W=[Rs.b])
                    for i in range(4):
                        S.op("dve", lambda e, d=d, h=h, hs=hs, i=i: e.scalar_tensor_tensor(
                            out=Rs.t[:, d, hs], in0=AG.t[:, i, d * 256 + h * 64:d * 256 + (h + 1) * 64], scalar=coef.t[0:64, d, h, i:i + 1],
                            in1=Rs.t[:, d, hs], op0=ALU.mult, op1=ALU.add), R=[AG.b, coef.b, Rs.b], W=[Rs.b])
            for d in range(2):
                order = range(32) if d == 0 else range(31, -1, -1)
                for c in order:
                    S.op("pool", lambda e, d=d, c=c: e.tensor_copy(out=Sall.t[:, d, c, :], in_=Rs.t[:, d, :]), R=[Rs.b], W=[Sall.b])
                    S.op("dve", lambda e, d=d: e.tensor_tensor(out=Rs.t[:, d, :], in0=Rs.t[:, d, :], in1=Gt.t[0:64, d, :], op=ALU.mult),
                         R=[Rs.b, Gt.b], W=[Rs.b])
                    S.op("dve", lambda e, d=d, c=c: e.tensor_tensor(out=Rs.t[:, d, :], in0=Rs.t[:, d, :], in1=UDs.t[:, c, d * 256:(d + 1) * 256], op=ALU.add),
                         R=[Rs.b, UDs.b], W=[Rs.b])
            qTs = [S.sb("rq", [64, 512], BF16) for _ in range(2)]
            kTs = [S.sb("rk", [64, 512], BF16) for _ in range(2)]
            vhs = [S.sb("rv", [128, 4, 64], BF16) for _ in range(2)]
            gfs = [S.sb("rgf", [64, 512], BF16) for _ in range(2)]
            gbs = [S.sb("rgb", [64, 512], BF16) for _ in range(2)]
            qss = [S.sb("qs", [64, 2, 512], BF16) for _ in range(2)]
            atm = [S.sb("atm", [128, 2, 512], BF16) for _ in range(2)]
            sqd = [S.sb("rsq", [64, 512], F32) for _ in range(2)]
            rsd = [S.sb("rrs", [64, 512], F32) for _ in range(2)]
            od = [S.sb("rod", [64, 512], F32) for _ in range(2)]
            obs = [S.sb("rob", [64, 512], BF16) for _ in range(2)]
            it = 0
            for g, (t0, W) in enumerate(GROUPS):
                nch = W // 128
                for h in range(4):
                    i2 = it % 2
                    it += 1
                    hs = slice(h * 64, (h + 1) * 64)
                    qT, kT, vh, gf, gb, qs, at = qTs[i2], kTs[i2], vhs[i2], gfs[i2], gbs[i2], qss[i2], atm[i2]
                    S.dma("sp", qT.t[:, :W], dr["rqT"][hs, t0:t0 + W], R=[db["rqT"]], W=[qT.b])
                    S.dma("sp", kT.t[:, :W], dr["rkT"][hs, t0:t0 + W], R=[db["rkT"]], W=[kT.b])
                    S.dma("sp", vh.t[:, 0:nch, :], dr["rvtm"][t0:t0 + W, hs].rearrange("(a p) d -> p a d", p=128), R=[db["rvtm"]], W=[vh.b])
                    S.dma("sp", gf.t[:, :W], dr["sgfT"][hs, t0:t0 + W], R=[db["sgfT"]], W=[gf.b])
                    S.dma("sp", gb.t[:, :W], dr["sgbT"][hs, t0:t0 + W], R=[db["sgbT"]], W=[gb.b])
                    for d in range(2):
                        S.op("pool", lambda e, d=d, qs=qs, qT=qT: e.tensor_tensor(out=qs.t[:, d, :W], in0=qT.t[:, :W], in1=QD.t[0:64, d, h, :W], op=ALU.mult),
                             R=[qT.b, QD.b], W=[qs.b])
                    pA = self.ps[i2]
                    for a in range(nch):
                        self.mm(pA.t[:, 128 * a:128 * a + 128], kT.t[:, 128 * a:128 * a + 128], qT.t[:, 128 * a:128 * a + 128], True, True,
                                [kT.b, qT.b], [pA.b])
                    for d in range(2):
                        S.op("dve", lambda e, d=d, at=at, pA=pA: e.tensor_tensor(out=at.t[:, d, :W], in0=pA.t[:, :W], in1=Dm.t[:, d, h, :W], op=ALU.mult),
                             R=[pA.b, Dm.b], W=[at.b])
                    for d in range(2):
                        po = self.ps[2 + d]
                        for a in range(nch):
                            c = t0 // 128 + a
                            self.mm(po.t[0:64, 128 * a:128 * a + 128], vh.t[:, a, :], at.t[:, d, 128 * a:128 * a + 128], True, False,
                                    [vh.b, at.b], [po.b])
                            self.mm(po.t[0:64, 128 * a:128 * a + 128], Sall.t[:, d, c, hs], qs.t[:, d, 128 * a:128 * a + 128], False, True,
                                    [Sall.b, qs.b], [po.b])
                        pst = self.ps[4 + d]
                        S.op("act", lambda e, d=d, po=po: e.activation(out=sqd[d].t[:, :W], in_=po.t[0:64, :W], func=AF.Square, scale=0.125),
                             R=[po.b], W=[sqd[d].b])
                        self.mm(pst.t[0:64, :W], ones.t[0:64, 0:64], sqd[d].t[:, :W], True, True, [ones.b, sqd[d].b], [pst.b])
                        self.rstd_op(rsd[d].t[:, :W], pst.t[0:64, :W], [pst.b], [rsd[d].b])
                        S.op("dve", lambda e, d=d, po=po: e.tensor_tensor(out=od[d].t[:, :W], in0=po.t[0:64, :W], in1=rsd[d].t[:, :W], op=ALU.mult),
                             R=[po.b, rsd[d].b], W=[od[d].b])
                        gg = gf if d == 0 else gb
                        S.op("pool", lambda e, d=d, gg=gg: e.tensor_tensor(out=od[d].t[:, :W], in0=od[d].t[:, :W], in1=gg.t[:, :W], op=ALU.mult),
                             R=[od[d].b, gg.b], W=[od[d].b])
                    ob = obs[i2]
                    S.op("pool", lambda e, ob=ob: e.tensor_tensor(out=ob.t[:, :W], in0=od[0].t[:, :W], in1=od[1].t[:, :W], op=ALU.add),
                         R=[od[0].b, od[1].b], W=[ob.b])
                    S.dma("pool", dr["mxT"][256 + h * 64:256 + (h + 1) * 64, t0:t0 + W], ob.t[:, :W], R=[ob.b], W=[db["mxT"]])

    def phase_C(self, l, xcur, xnext, P):
        S, dr, db = self.S, self.dr, self.db
        mod, s2 = P["mod"], P["s2"]
        with S.phase():
            wout = S.sb("wout", [128, 8, D], BF16)
            for k in range(8):
                S.dma("pool", wout.t[:, k, :], dr["wout_g"][l * D + k * 128:l * D + (k + 1) * 128, :], R=[db["wout_g"]], W=[wout.b])
            ones = S.sb("ones", [128, 128], F32)
            S.op("pool", lambda e: e.memset(ones.t[:, :], 1.0), W=[ones.b])
            xs2 = [S.sb("xs", [128, 8, 512], F32) for _ in range(2)]
            mx2 = [S.sb("mx", [128, 8, 512], BF16) for _ in range(2)]
            h22 = [S.sb("h2", [128, 8, 512], BF16) for _ in range(2)]
            sq2 = [S.sb("sq", [128, 512], F32) for _ in range(2)]
            tmp2 = [S.sb("ntmp", [128, 512], F32) for _ in range(2)]
            rstd = S.sb("rstd", [128, 512], F32)
            for g, (t0, W) in enumerate(GROUPS):
                j = 0 if g < 8 else 1
                xs, mx, h2 = xs2[g % 2], mx2[g % 2], h22[g % 2]
                S.dma("sp", xs.t[:, :, :W], dr[xcur][:, t0:t0 + W].rearrange("(k p) w -> p k w", p=128), R=[db[xcur]], W=[xs.b])
                S.dma("sp", mx.t[:, :, :W], dr["mxT"][:, t0:t0 + W].rearrange("(k p) w -> p k w", p=128), R=[db["mxT"]], W=[mx.b])
                for n in range(8):
                    ps = self.ps[1 + n % 4]
                    for k in range(8):
                        self.mm(ps.t[:, :W], wout.t[:, k, 128 * n:128 * n + 128], mx.t[:, k, :W], k == 0, k == 7, [wout.b, mx.b], [ps.b])
                    S.op("dve", lambda e, n=n, ps=ps, xs=xs, j=j: e.scalar_tensor_tensor(
                        out=xs.t[:, n, :W], in0=ps.t[:, :W], scalar=mod.t[:, 16 + n, j:j + 1], in1=xs.t[:, n, :W], op0=ALU.mult, op1=ALU.add),
                         R=[ps.b, mod.b, xs.b], W=[xs.b])
                S.dma("pool", dr["XM"][:, t0:t0 + W].rearrange("(k p) w -> p k w", p=128), xs.t[:, :, :W], R=[xs.b], W=[db["XM"]])
                self.norm_group(xs, W, lambda k, j=j: s2.t[:, k, j:j + 1], lambda k, j=j: mod.t[:, 24 + k, j:j + 1], h2, ones, 0, sq2, rstd, tmp2)
                S.dma("pool", dr["h2T"][:, t0:t0 + W].rearrange("(k p) w -> p k w", p=128), h2.t[:, :, :W], R=[h2.b], W=[db["h2T"]])
        with S.phase():
            w1 = S.sb("w1", [128, 8, FFN], BF16)
            w3 = S.sb("w3", [128, 8, FFN], BF16)
            for k in range(8):
                S.dma("pool", w1.t[:, k, :], dr["w1_g"][l * D + k * 128:l * D + (k + 1) * 128, :], R=[db["w1_g"]], W=[w1.b])
                S.dma("pool", w3.t[:, k, :], dr["w3_g"][l * D + k * 128:l * D + (k + 1) * 128, :], R=[db["w3_g"]], W=[w3.b])
            h22 = [S.sb("h2", [128, 8, 512], BF16) for _ in range(2)]
            us = [S.sb("u", [128, 22, 512], BF16) for _ in range(2)]
            sl2 = [S.sb("sl", [128, 512], F32) for _ in range(2)]
            for g, (t0, W) in enumerate(GROUPS):
                h2, u = h22[g % 2], us[g % 2]
                S.dma("sp", h2.t[:, :, :W], dr["h2T"][:, t0:t0 + W].rearrange("(k p) w -> p k w", p=128), R=[db["h2T"]], W=[h2.b])
                for m in range(22):
                    p1 = self.ps[(2 * m) % 6]
                    p3 = self.ps[(2 * m + 1) % 6]
                    for k in range(8):
                        self.mm(p1.t[:, :W], w1.t[:, k, 128 * m:128 * m + 128], h2.t[:, k, :W], k == 0, k == 7, [w1.b, h2.b], [p1.b])
                    for k in range(8):
                        self.mm(p3.t[:, :W], w3.t[:, k, 128 * m:128 * m + 128], h2.t[:, k, :W], k == 0, k == 7, [w3.b, h2.b], [p3.b])
                    sl = sl2[m % 2]
                    S.op("act", lambda e, p1=p1, sl=sl: e.activation(out=sl.t[:, :W], in_=p1.t[:, :W], func=AF.Silu), R=[p1.b], W=[sl.b])
                    S.op("dve", lambda e, p3=p3, sl=sl, u=u, m=m: e.tensor_tensor(out=u.t[:, m, :W], in0=sl.t[:, :W], in1=p3.t[:, :W], op=ALU.mult),
                         R=[p3.b, sl.b], W=[u.b])
                S.dma("pool", dr["uT"][:, t0:t0 + W].rearrange("(m p) w -> p m w", p=128), u.t[:, :, :W], R=[u.b], W=[db["uT"]])
        with S.phase():
            w2 = S.sb("w2", [128, 22, D], BF16)
            for m in range(22):
                S.dma("pool", w2.t[:, m, :], dr["w2_g"][l * FFN + m * 128:l * FFN + (m + 1) * 128, :], R=[db["w2_g"]], W=[w2.b])
            us = [S.sb("u", [128, 22, 512], BF16) for _ in range(2)]
            xs2 = [S.sb("xs", [128, 8, 512], F32) for _ in range(2)]
            for g, (t0, W) in enumerate(GROUPS):
                j = 0 if g < 8 else 1
                u, xs = us[g % 2], xs2[g % 2]
                S.dma("sp", u.t[:, :, :W], dr["uT"][:, t0:t0 + W].rearrange("(m p) w -> p m w", p=128), R=[db["uT"]], W=[u.b])
                S.dma("sp", xs.t[:, :, :W], dr["XM"][:, t0:t0 + W].rearrange("(k p) w -> p k w", p=128), R=[db["XM"]], W=[xs.b])
                for n in range(8):
                    ps = self.ps[n % 6]
                    for m in range(22):
                        self.mm(ps.t[:, :W], w2.t[:, m, 128 * n:128 * n + 128], u.t[:, m, :W], m == 0, m == 21, [w2.b, u.b], [ps.b])
                    S.op("dve", lambda e, n=n, ps=ps, xs=xs, j=j: e.scalar_tensor_tensor(
                        out=xs.t[:, n, :W], in0=ps.t[:, :W], scalar=mod.t[:, 40 + n, j:j + 1], in1=xs.t[:, n, :W], op0=ALU.mult, op1=ALU.add),
                         R=[ps.b, mod.b, xs.b], W=[xs.b])
                S.dma("pool", dr[xnext][:, t0:t0 + W].rearrange("(k p) w -> p k w", p=128), xs.t[:, :, :W], R=[xs.b], W=[db[xnext]])

    def emit_final(self, xcur):
        S, dr, db = self.S, self.dr, self.db
        with S.phase():
            fg = S.sb("fg", [128, 8], F32)
            S.dma("sp", fg.t[:, :], dr["fgT"], R=[db["fgT"]], W=[fg.b])
            ones = S.sb("ones", [128, 128], F32)
            S.op("pool", lambda e: e.memset(ones.t[:, :], 1.0), W=[ones.b])
            xs2 = [S.sb("xs", [128, 8, 512], F32) for _ in range(2)]
            oo2 = [S.sb("oo", [128, 8, 512], F32) for _ in range(2)]
            sq2 = [S.sb("sq", [128, 512], F32) for _ in range(2)]
            rstd = S.sb("rstd", [128, 512], F32)
            for g, (t0, W) in enumerate(GROUPS[:8]):
                xs, oo = xs2[g % 2], oo2[g % 2]
                S.dma("sp", xs.t[:, :, :W], dr[xcur][:, t0:t0 + W].rearrange("(k p) w -> p k w", p=128), R=[db[xcur]], W=[xs.b])
                self.norm_group(xs, W, lambda k: fg.t[:, k:k + 1], None, oo, ones, 0, sq2, rstd, None)
                S.dma("pool", dr["outT"][:, t0:t0 + W].rearrange("(k p) w -> p k w", p=128), oo.t[:, :, :W], R=[oo.b], W=[db["outT"]])

    def emit_debug(self):
        S, dr, db = self.S, self.dr, self.db
        for name in self.debug:
            src = dr[name]
            dst = self.nc.dram_tensor("dbg_" + name, list(src.tensor.shape), src.tensor.dtype, kind="ExternalOutput").ap()
            S.dma("sp", dst, src, R=[db[name]], W=[Buf("dbg")])


def _perm_idx(dh):
    q = dh // 4
    return np.concatenate([np.arange(q, 2 * q), np.arange(0, q), np.arange(3 * q, 4 * q), np.arange(2 * q, 3 * q)])


def _rope_tables(dh, rows, cols, nlat):
    h = dh // 2
    inv = (np.float32(THETA) ** (-(np.arange(0, h, 2, dtype=np.float32)) / np.float32(h))).astype(np.float32)
    angr = (rows.astype(np.float32)[None, :] * inv[:, None]).astype(np.float32)
    angc = (cols.astype(np.float32)[None, :] * inv[:, None]).astype(np.float32)
    C = np.concatenate([np.cos(angr), np.cos(angr), np.cos(angc), np.cos(angc)], 0).astype(np.float32)
    Sn = np.concatenate([-np.sin(angr), np.sin(angr), -np.sin(angc), np.sin(angc)], 0).astype(np.float32)
    Cf = np.ones((dh, T), np.float32)
    Sf = np.zeros((dh, T), np.float32)
    Cf[:, :nlat] = C
    Sf[:, :nlat] = Sn
    return Cf, Sf


def _win_cols():
    o = dict(cq=0, ckv=256, kr=384, rq=416, rk=672, rv=928, gf=1184, gb=1440, nq=1696, nk=1952, nv=2208, sq=2464,
             sk=2720, sv=2848)
    p64 = _perm_idx(64)
    p32 = _perm_idx(32)

    def heads(off, nh):
        return np.concatenate([off + h * 64 + p64 for h in range(nh)])

    cols = [np.arange(o["cq"], o["cq"] + 256), np.arange(o["ckv"], o["ckv"] + 128), np.arange(o["kr"], o["kr"] + 32),
            o["kr"] + p32, np.arange(o["rq"], o["rq"] + 256), heads(o["rq"], 4), np.arange(o["rk"], o["rk"] + 256),
            heads(o["rk"], 4), np.arange(o["gf"], o["gf"] + 256), np.arange(o["gb"], o["gb"] + 256),
            np.arange(o["nq"], o["nq"] + 256), np.arange(o["nk"], o["nk"] + 256), np.arange(o["sq"], o["sq"] + 256),
            heads(o["sq"], 4), np.arange(o["sk"], o["sk"] + 128), heads(o["sk"], 2), np.arange(o["rv"], o["rv"] + 256),
            np.arange(o["nv"], o["nv"] + 256), np.arange(o["sv"], o["sv"] + 128)]
    c = np.concatenate(cols)
    assert c.shape[0] == XW
    return c


def _wuq_cols():
    p32 = _perm_idx(32)
    cols = []
    for h in range(4):
        cols.append(np.arange(h * 96, h * 96 + 96))
        cols.append(np.concatenate([np.arange(h * 96, h * 96 + 64), h * 96 + 64 + p32]))
    return np.concatenate(cols)


def _shard_rows(w2d, core):
    r = w2d.shape[0] // NC
    return np.ascontiguousarray(w2d[core * r:(core + 1) * r])


def prep_inputs(inp):
    f32 = np.float32
    x, c, ctx, c_ctx = (np.asarray(inp[k], f32) for k in ("x", "c", "ctx", "c_ctx"))
    wc = _win_cols()
    win_ext = np.ascontiguousarray(np.asarray(inp["w_in"], f32)[:, :, wc]).reshape(L * D, XW)
    wuq_ext = np.ascontiguousarray(np.asarray(inp["mla_w_uq"], f32)[:, :, _wuq_cols()])
    ada = np.asarray(inp["ada_w"], f32).reshape(L * D, 6 * D)
    wout = np.asarray(inp["w_out"], f32).reshape(L * D, D)
    w1 = np.asarray(inp["ffn_w1"], f32).reshape(L * D, FFN)
    w3 = np.asarray(inp["ffn_w3"], f32).reshape(L * D, FFN)
    w2 = np.asarray(inp["ffn_w2"], f32).reshape(L * FFN, D)
    rpb = np.asarray(inp["na_rpb"], f32)
    jb = np.arange(8)[:, None, None, None, None]
    ko = np.arange(2)[None, :, None, None, None]
    ck = np.arange(64)[None, None, :, None, None]
    qi = np.arange(8)[None, None, None, :, None]
    cq = np.arange(64)[None, None, None, None, :]
    drr = np.clip(2 * jb - 4 + ko - qi + 7, 0, 14) + 0 * ck + 0 * cq
    dcc = np.clip(ck - cq, -15, 15) + 15 + 0 * jb + 0 * ko + 0 * qi
    toep = rpb[:, :, drr, dcc].reshape(L * 4 * 8 * 128, 512)
    shared = dict(
        adabT=np.ascontiguousarray(np.asarray(inp["ada_b"], f32).reshape(L, 48, 128).transpose(0, 2, 1)),
        n1gT=np.ascontiguousarray(np.asarray(inp["norm1_g"], f32).reshape(L, 8, 128).transpose(0, 2, 1)),
        n2gT=np.ascontiguousarray(np.asarray(inp["norm2_g"], f32).reshape(L, 8, 128).transpose(0, 2, 1)),
        fgT=np.ascontiguousarray(np.asarray(inp["final_norm_g"], f32).reshape(8, 128).T),
        qnT=np.ascontiguousarray(np.asarray(inp["mla_q_norm"], f32).reshape(L, 2, 128).transpose(0, 2, 1)),
        kvnT=np.ascontiguousarray(np.asarray(inp["mla_kv_norm"], f32).reshape(L, 1, 128).transpose(0, 2, 1)),
        decB=np.ascontiguousarray(np.broadcast_to(np.asarray(inp["ret_decay"], f32).reshape(L, 1, 8), (L, 128, 8))),
        sinkB=np.ascontiguousarray(np.repeat(np.asarray(inp["swa_sink"], f32), 512, axis=1).reshape(L, 1, 2048)),
        wuq=wuq_ext, wukv=np.ascontiguousarray(np.asarray(inp["mla_w_ukv"], f32)),
        ident=np.eye(128, dtype=f32),
    )
    in_maps = []
    ii = np.arange(128)
    for core in range(NC):
        b, r = core // 4, core % 4
        t0 = r * TL
        tt = np.arange(t0, t0 + TL)
        rows, cols = tt // 64, tt % 64
        m = dict(shared)
        m["xT0"] = np.ascontiguousarray(np.concatenate([x[b, t0:t0 + TL].T, ctx[b].T], axis=1))
        m["cT"] = np.ascontiguousarray(np.stack([c[b].reshape(8, 128).T, c_ctx.reshape(8, 128).T], axis=-1))
        m["c64"], m["s64"] = _rope_tables(64, rows, cols, TL)
        m["c32"], m["s32"] = _rope_tables(32, rows, cols, TL)
        rm = np.full((3, 8, 2, 64, 8, 64), NEG, f32)
        ckk = np.arange(64)[:, None]
        cqq = np.arange(64)[None, :]
        c0 = np.clip(cqq - 8, 0, 48)
        colok = (ckk >= c0) & (ckk < c0 + 16)
        for v, gi in enumerate((0, 3, 7)):
            for jblk in range(8):
                for koff in range(2):
                    kr = 64 * r + 2 * (4 * gi - 2 + jblk) + koff
                    for q_i in range(8):
                        qr = 64 * r + 8 * gi + q_i
                        r0 = min(max(qr - 4, 0), 248)
                        if 0 <= kr < 256 and r0 <= kr < r0 + 8:
                            rm[v, jblk, koff, :, q_i, :] = np.where(colok, 0.0, NEG)
        m["rm"] = rm.reshape(3 * 8 * 128, 512).astype(ml_dtypes.bfloat16)
        ms = np.full((3, 6, 128, 512), NEG, f32)
        for v, gi in enumerate((0, 3, 7)):
            tq = 4096 * r + 512 * gi + np.arange(512)[None, :]
            for jblk in range(6):
                tk = 4096 * r + 512 * gi - 128 + 128 * jblk + np.arange(128)[:, None]
                ok = (tk >= 0) & (tk < SEQ) & (np.abs(tk - tq) <= 128)
                ms[v, jblk] = np.where(ok, 0.0, NEG)
        m["ms"] = ms.reshape(3 * 6 * 128, 512).astype(ml_dtypes.bfloat16)
        rete = np.zeros((128, 2192), f32)
        jj = ii[:, None]
        iq = ii[None, :]
        rete[:, 0:512] = np.tile(np.where(iq >= jj, iq - jj, BIGE), (1, 4))
        rete[:, 512:1024] = np.tile(np.where(jj >= iq, jj - iq, BIGE), (1, 4))
        rete[:, 1024:1536] = np.tile(iq + 1 + 0 * jj, (1, 4))
        rete[:, 1536:2048] = np.tile(128 - iq + 0 * jj, (1, 4))
        rete[:, 2048:2112] = 127 - jj
        rete[:, 2112:2176] = jj
        for i in range(4):
            rete[:, 2176 + i] = TL * (r - 1 - i) if i < r else BIGE
            rete[:, 2181 + i] = TL * (i - r - 1) if i > r else BIGE
        rete[:, 2180] = TL * r
        rete[:, 2185] = TL * (3 - r)
        m["rete"] = rete
        for k, w in (("ada", ada), ("win", win_ext), ("wout", wout), ("w1", w1), ("w3", w3), ("w2", w2), ("toep", toep)):
            m[k + "_sh"] = _shard_rows(w, core)
        in_maps.append(m)
    return in_maps


_PROG = None


def kernel(**inputs):
    global _PROG
    if _PROG is None:
        _PROG = Prog()
    in_maps = prep_inputs(inputs)
    res = run_bass_kernel_spmd(_PROG.nc, in_maps, core_ids=list(range(NC)))
    out = np.empty((B, SEQ, D), np.float32)
    for core in range(NC):
        b, r = core // 4, core % 4
        out[b, r * TL:(r + 1) * TL, :] = res.results[core]["outT"].T
    return out
```

```python
import numpy as np
from contextlib import ExitStack
import ml_dtypes
import concourse.bass as bass
import concourse.mybir as mybir
from concourse.bass_utils import run_bass_kernel_spmd

F32 = mybir.dt.float32
BF16 = mybir.dt.bfloat16
AF = mybir.ActivationFunctionType
ALU = mybir.AluOpType

NC = 8
D = 1024
B = 2
SEQ = 16384
L = 4
TL = 4096
LC = 256
T = TL + LC
GROUPS = [(g * 512, 512) for g in range(8)] + [(TL, LC)]
FFN = 2816
EPS = 1e-6
THETA = 10000.0
NEG = -1e30
BIGE = 1e9

XO = dict(cq=0, ckv=256, kr=384, krp=416, rq=448, rqp=704, rk=960, rkp=1216, gf=1472, gb=1728,
          nq=1984, nk=2240, sq=2496, sqp=2752, sk=3008, skp=3136, rv=3264, nv=3520, sv=3776)
XW = 3904
EX = dict(mla=0, nkh=2560, nkt=2816, nvh=3072, nvt=3328, skh=3584, skt=3712, svh=3840, svt=3968)
EXR = 4096


class Buf:
    __slots__ = ("name", "w", "r")

    def __init__(self, name):
        self.name = name
        self.w = None
        self.r = []


class Tile:
    def __init__(self, t, b):
        self.t = t
        self.b = b


class Op:
    __slots__ = ("eng", "fn", "waits", "signal", "kind", "count", "sem", "val")

    def __init__(self, eng, fn, kind):
        self.eng = eng
        self.fn = fn
        self.kind = kind
        self.waits = []
        self.signal = False
        self.count = None
        self.sem = None
        self.val = None


class Sched:
    CE = ("pe", "act", "dve", "pool")

    def __init__(self, nc, es):
        self.nc = nc
        self.es = es
        self.ops = []
        self.pstack = []
        self.nbuf = 0
        self.strict_same = True

    def buf(self, name):
        return Buf(name)

    def sb(self, name, shape, dt):
        self.nbuf += 1
        t = self.pstack[-1].enter_context(self.nc.sbuf_tensor(f"{name}_{self.nbuf}", list(shape), dt))
        return Tile(t, Buf(name))

    def phase(self):
        sch = self

        class _P:
            def __enter__(s):
                sch.pstack.append(ExitStack())
                return s

            def __exit__(s, *a):
                sch.barrier()
                sch.pstack.pop().close()
                return False

        return _P()

    def _deps(self, op, R, W):
        for b in R:
            if b.w is not None:
                op.waits.append(b.w)
        for b in W:
            if b.w is not None:
                op.waits.append(b.w)
            op.waits.extend(b.r)
        for b in R:
            if op.kind == "c":
                b.r = [x for x in b.r if not (x.kind == "c" and x.eng == op.eng)]
            b.r.append(op)
        for b in W:
            b.w = op
            b.r = []

    @staticmethod
    def _eager(fn):
        calls = []

        class _Rec:
            def __getattr__(self, name):
                def f(*a, **k):
                    calls.append((name, a, k))
                    return self
                return f

        fn(_Rec())
        assert len(calls) == 1, calls
        name, a, k = calls[0]
        return lambda e: getattr(e, name)(*a, **k)

    def op(self, eng, fn, R=(), W=()):
        o = Op(eng, self._eager(fn), "c")
        self._deps(o, R, W)
        self.ops.append(o)
        return o

    def dma(self, q, out, in_, R=(), W=(), **kw):
        o = Op(q, (lambda e, out=out, in_=in_, kw=kw: e.dma_start(out=out, in_=in_, **kw)), "d")
        self._deps(o, R, W)
        self.ops.append(o)
        return o

    def cc(self, fn, R=(), W=()):
        o = Op("pool", self._eager(fn), "cc")
        self._deps(o, R, W)
        self.ops.append(o)
        return o

    def barrier(self):
        o = Op("sp", None, "bar")
        self.ops.append(o)

    def newsems(self):
        self.ops.append(Op("sp", None, "ns"))

    def emit(self):
        nc = self.nc
        es = self.es
        engs = {"pe": nc.tensor, "act": nc.scalar, "dve": nc.vector, "pool": nc.gpsimd, "sp": nc.sync}
        sem = {k: es.enter_context(nc.semaphore("s_" + k)) for k in self.CE}
        ccsem = es.enter_context(nc.semaphore("s_cc"))
        barsem = es.enter_context(nc.semaphore("s_bar"))
        K = 8
        dq = {q: [es.enter_context(nc.semaphore(f"d_{q}{i}")) for i in range(K)] for q in ("sp", "pool", "act")}
        dn = {q: 0 for q in dq}
        dlast = {q: [0] * K for q in dq}
        cnt = {k: 0 for k in self.CE}
        cccnt = 0
        barn = 0
        waited = {}
        for o in self.ops:
            for p in o.waits:
                p.signal = True

        def wait(e, s, v):
            key = (e, id(s))
            if waited.get(key, 0) >= v:
                return
            engs[e].wait_ge(s, v)
            waited[key] = v

        for o in self.ops:
            e = o.eng
            if o.kind == "ns":
                nsn = getattr(self, "_nsn", 0) + 1
                self._nsn = nsn
                sem = {k: es.enter_context(nc.semaphore(f"s{nsn}_" + k)) for k in self.CE}
                cnt = {k: 0 for k in self.CE}
                continue
            if o.kind == "bar":
                for k in self.CE:
                    if cnt[k] > 0:
                        wait("sp", sem[k], cnt[k])
                if cccnt:
                    wait("sp", ccsem, cccnt)
                for q in dq:
                    for i in range(K):
                        if dlast[q][i]:
                            wait("sp", dq[q][i], dlast[q][i])
                barn += 1
                nc.sync.sem_inc(barsem, 1)
                for k in ("pe", "act", "dve", "pool"):
                    wait(k, barsem, barn)
                continue
            for p in o.waits:
                if p.kind == "c" and p.eng == e and (e == "pe" or not self.strict_same):
                    continue
                if p.kind == "d" and p.eng == e and False:
                    continue
                wait(e, p.sem, p.val)
            if o.kind == "c":
                ins = o.fn(engs[e])
                if o.signal:
                    ins.then_inc(sem[e], 1)
                    cnt[e] += 1
                    o.sem, o.val = sem[e], cnt[e]
            elif o.kind == "cc":
                ins = o.fn(engs[e])
                ins.then_inc(ccsem)
                cccnt += 1
                o.sem, o.val = ccsem, cccnt
            else:
                n = dn[e]
                slot = n % K
                if n >= K:
                    wait(e, dq[e][slot], 16 * (n // K))
                ins = o.fn(engs[e])
                ins.then_inc(dq[e][slot], 16)
                o.sem, o.val = dq[e][slot], 16 * (n // K + 1)
                dlast[e][slot] = o.val
                dn[e] += 1
        for k in self.CE:
            if cnt[k] > 0:
                wait("sp", sem[k], cnt[k])
        for q in dq:
            for i in range(K):
                if dlast[q][i]:
                    wait("sp", dq[q][i], dlast[q][i])


class Prog:
    def __init__(self, nlayers=L, debug=(), stop=None, flags=(), stop_layer=0):
        self.stop_layer = stop_layer
        self.nlayers = nlayers
        self.debug = debug
        self.stop = stop
        self.flags = flags
        self.es = ExitStack()
        self.nc = bass.Bass("TRN2", target_bir_lowering=False)
        self.S = Sched(self.nc, self.es)
        self.dr = {}
        self.db = {}
        self.build()

    def dram(self, name, shape, dt, kind="Internal"):
        if kind == "Internal":
            t = self.nc.dram_tensor(name, list(shape), dt)
        else:
            t = self.nc.dram_tensor(name, list(shape), dt, kind=kind)
        self.dr[name] = t.ap()
        self.db[name] = Buf(name)
        return t.ap()

    def build(self):
        nc, S = self.nc, self.S
        dram = self.dram
        dram("xT0", [D, T], F32, "ExternalInput")
        dram("cT", [128, 8, 2], F32, "ExternalInput")
        dram("adabT", [L, 128, 48], F32, "ExternalInput")
        dram("n1gT", [L, 128, 8], F32, "ExternalInput")
        dram("n2gT", [L, 128, 8], F32, "ExternalInput")
        dram("fgT", [128, 8], F32, "ExternalInput")
        dram("qnT", [L, 128, 2], F32, "ExternalInput")
        dram("kvnT", [L, 128, 1], F32, "ExternalInput")
        dram("decB", [L, 128, 8], F32, "ExternalInput")
        dram("sinkB", [L, 1, 4 * 512], F32, "ExternalInput")
        dram("wuq", [L, 256, 768], F32, "ExternalInput")
        dram("wukv", [L, 128, 512], F32, "ExternalInput")
        dram("ident", [128, 128], F32, "ExternalInput")
        dram("c64", [64, T], F32, "ExternalInput")
        dram("s64", [64, T], F32, "ExternalInput")
        dram("c32", [32, T], F32, "ExternalInput")
        dram("s32", [32, T], F32, "ExternalInput")
        dram("rm", [3 * 8 * 128, 512], BF16, "ExternalInput")
        dram("ms", [3 * 6 * 128, 512], BF16, "ExternalInput")
        dram("rete", [128, 2192], F32, "ExternalInput")
        self.wsh = dict(ada=(L * D // NC, 6 * D), win=(L * D // NC, XW), wout=(L * D // NC, D),
                        w1=(L * D // NC, FFN), w3=(L * D // NC, FFN), w2=(L * FFN // NC, D),
                        toep=(L * 4 * 8 * 128 // NC, 512))
        for k, (r, c) in self.wsh.items():
            if "nogather" not in self.flags:
                dram(k + "_sh", [r, c], F32, "ExternalInput")
            dram(k + "_src", [r, c], F32)
            dram(k + "_g", [r * NC, c], F32)
        dram("outT", [D, TL], F32, "ExternalOutput")
        dram("XA", [D, T], F32)
        dram("XB", [D, T], F32)
        dram("XM", [D, T], F32)
        dram("h2T", [D, T], BF16)
        dram("uT", [FFN, T], BF16)
        dram("mxT", [D, T], BF16)
        dram("qmT", [4 * 96, T], BF16)
        dram("latc", [160, LC], BF16)
        dram("rqT", [256, T], BF16)
        dram("rkT", [256, T], BF16)
        dram("rvtm", [T, 256], BF16)
        dram("sgfT", [256, T], BF16)
        dram("sgbT", [256, T], BF16)
        dram("nqT", [256, T], BF16)
        dram("nkT", [256, T], BF16)
        dram("nvtm", [T, 256], BF16)
        dram("sqT", [256, T], BF16)
        dram("skT", [128, T], BF16)
        dram("svtm", [T, 128], BF16)
        dram("UD", [34 * 64, 512], F32)
        dram("EXP", [EXR, 256], BF16)
        dram("GAT", [4 * EXR, 256], BF16)
        dram("GAT8", [8 * 256, 4096], BF16)
        dram("LATB", [4 * 160, 4096], BF16)
        dram("RETG8", [8 * 64, 512], F32)
        dram("HALOP", [96, 4096], BF16)
        dram("HALON", [96, 4096], BF16)
        dram("RETX", [64, 512], F32)
        dram("RETG", [4 * 64, 512], F32)
        for name in self.debug:
            pass

        self.ps = []
        for i in range(7):
            t = self.es.enter_context(nc.psum_tensor(f"ps{i}", [128, 512], F32))
            self.ps.append(Tile(t, Buf(f"ps{i}")))
        t = self.es.enter_context(nc.psum_tensor("psb", [128, 1024], BF16))
        self.psb = Tile(t, Buf("psb"))

        t = self.es.enter_context(nc.sbuf_tensor("epsb", [128, 1], F32))
        self.epsb = Tile(t, Buf("epsb"))
        S.op("pool", lambda e: e.memset(self.epsb.t[:, :], EPS), W=[self.epsb.b])
        if "nogather" not in self.flags:
            self.emit_weight_gather()
            S.barrier()
        t = self.es.enter_context(nc.sbuf_tensor("zpad", [128, 128], BF16))
        zp = Tile(t, Buf("zpad"))
        S.op("pool", lambda e: e.memset(zp.t[:, :], 0.0), W=[zp.b])
        for i in range(4):
            S.dma("sp", self.dr["EXP"][EX["skh"] + 128 * i:EX["skh"] + 128 * (i + 1), 128:256], zp.t[:, :], R=[zp.b], W=[self.db["EXP"]])
        xcur = "xT0"
        for l in range(self.nlayers if self.stop != "W" else 0):
            xnext = "XA" if l % 2 == 0 else "XB"
            done = self.layer(l, xcur, xnext)
            if not done:
                break
            xcur = xnext
        else:
            self.emit_final(xcur)
        self.emit_debug()
        S.emit()

    def emit_weight_gather(self):
        S, dr, db = self.S, self.dr, self.db
        for k in self.wsh:
            S.dma("sp", dr[k + "_src"], dr[k + "_sh"], R=[db[k + "_sh"]], W=[db[k + "_src"]])
        for k in self.wsh:
            S.cc(lambda e, k=k: e.collective_compute("AllGather", ALU.bypass, replica_groups=[list(range(NC))],
                                                     ins=[dr[k + "_src"].opt()], outs=[dr[k + "_g"].opt()]),
                 R=[db[k + "_src"]], W=[db[k + "_g"]])

    def mm(self, out, lhsT, rhs, start, stop, R, W):
        return self.S.op("pe", lambda e: e.matmul(out, lhsT, rhs, start=start, stop=stop), R=R, W=W)

    def rstd_op(self, out_ap, in_ap, R, W):
        S = self.S
        S.op("act", lambda e: e.activation(out=out_ap, in_=in_ap, func=AF.Sqrt, bias=self.epsb.t[0:in_ap.shape[0], 0:1], scale=1.0), R=list(R) + [self.epsb.b], W=list(W))
        S.op("dve", lambda e: e.reciprocal(out=out_ap, in_=out_ap), R=list(W), W=list(W))

    def norm_group(self, xs, W, scale_ap, shift_ap, hout, ones, psi, sq, rstd, tmp, scale_sq=1.0 / 32.0):
        S = self.S
        ps = self.ps[psi]
        for k in range(8):
            S.op("act", lambda e, k=k: e.activation(out=sq[k % 2].t[:, :W], in_=xs.t[:, k, :W], func=AF.Square, scale=scale_sq),
                 R=[xs.b], W=[sq[k % 2].b])
            self.mm(ps.t[:, :W], ones.t[:, :], sq[k % 2].t[:, :W], k == 0, k == 7, [ones.b, sq[k % 2].b], [ps.b])
        self.rstd_op(rstd.t[:, :W], ps.t[:, :W], [ps.b], [rstd.b])
        for k in range(8):
            if shift_ap is None:
                S.op("dve", lambda e, k=k: e.scalar_tensor_tensor(out=hout.t[:, k, :W], in0=xs.t[:, k, :W],
                                                                  scalar=scale_ap(k), in1=rstd.t[:, :W],
                                                                  op0=ALU.mult, op1=ALU.mult),
                     R=[xs.b, rstd.b], W=[hout.b])
            else:
                tk = tmp[k % 2]
                S.op("dve", lambda e, k=k, tk=tk: e.scalar_tensor_tensor(out=tk.t[:, :W], in0=xs.t[:, k, :W],
                                                                         scalar=scale_ap(k), in1=rstd.t[:, :W],
                                                                         op0=ALU.mult, op1=ALU.mult),
                     R=[xs.b, rstd.b], W=[tk.b])
                S.op("act", lambda e, k=k, tk=tk: e.activation(out=hout.t[:, k, :W], in_=tk.t[:, :W],
                                                               func=AF.Identity, bias=shift_ap(k), scale=1.0),
                     R=[tk.b], W=[hout.b])

    def emit_mod(self, l, P):
        S, dr, db = self.S, self.dr, self.db
        mod, s1, s2 = P["mod"], P["s1"], P["s2"]
        with S.phase():
            cT = S.sb("cT", [128, 8, 2], F32)
            sc = S.sb("sc", [128, 8, 2], BF16)
            adab = S.sb("adab", [128, 48], F32)
            n1g = S.sb("n1g", [128, 8], F32)
            n2g = S.sb("n2g", [128, 8], F32)
            tmp = S.sb("mtmp", [128, 8, 2], F32)
            S.dma("sp", cT.t[:, :, :], dr["cT"], R=[db["cT"]], W=[cT.b])
            S.dma("sp", adab.t[:, :], dr["adabT"][l], R=[db["adabT"]], W=[adab.b])
            S.dma("sp", n1g.t[:, :], dr["n1gT"][l], R=[db["n1gT"]], W=[n1g.b])
            S.dma("sp", n2g.t[:, :], dr["n2gT"][l], R=[db["n2gT"]], W=[n2g.b])
            S.op("act", lambda e: e.activation(out=sc.t[:, :, :], in_=cT.t[:, :, :], func=AF.Silu), R=[cT.b], W=[sc.b])
            wts = [S.sb("adaw", [128, 8, 768], BF16) for _ in range(2)]
            for pc in range(8):
                wt = wts[pc % 2]
                src = dr["ada_g"][l * D:(l + 1) * D, pc * 768:(pc + 1) * 768].rearrange("(k p) c -> p k c", p=128)
                S.dma("pool", wt.t[:, :, :], src, R=[db["ada_g"]], W=[wt.b])
                ps = self.ps[pc % 2]
                for n in range(6):
                    for k in range(8):
                        self.mm(ps.t[:, 2 * n:2 * n + 2], wt.t[:, k, 128 * n:128 * n + 128], sc.t[:, k, :],
                                k == 0, k == 7, [wt.b, sc.b], [ps.b])
                for n in range(6):
                    ch = pc * 6 + n
                    S.op("dve", lambda e, n=n, ch=ch, ps=ps: e.tensor_scalar(
                        out=mod.t[:, ch, :], in0=ps.t[:, 2 * n:2 * n + 2], scalar1=adab.t[:, ch:ch + 1], scalar2=None,
                        op0=ALU.add), R=[ps.b, adab.b], W=[mod.b])
            for (dst, g, off) in ((s1, n1g, 8), (s2, n2g, 32)):
                S.op("dve", lambda e, off=off: e.tensor_scalar(out=tmp.t[:, :, :], in0=mod.t[:, off:off + 8, :],
                                                               scalar1=1.0, scalar2=None, op0=ALU.add),
                     R=[mod.b], W=[tmp.b])
                for j in range(2):
                    S.op("dve", lambda e, j=j, dst=dst, g=g: e.tensor_tensor(out=dst.t[:, :, j], in0=tmp.t[:, :, j],
                                                                             in1=g.t[:, :], op=ALU.mult),
                         R=[tmp.b, g.b], W=[dst.b])

    def layer(self, l, xcur, xnext):
        S = self.S
        S.barrier()
        S.newsems()
        with S.phase():
            P = dict(mod=S.sb("mod", [128, 48, 2], F32), s1=S.sb("s1", [128, 8, 2], F32),
                     s2=S.sb("s2", [128, 8, 2], F32), lg=S.sb("lg", [128, 8], F32))
            self.emit_mod(l, P)
            self.emit_lconst(l, P)
            if self.stop == "M" and l == self.stop_layer:
                return False
            self.phase_A(l, xcur, P)
            if self.stop == "A" and l == self.stop_layer:
                return False
            self.exchange(l)
            if self.stop == "X" and l == self.stop_layer:
                return False
            self.phase_B(l, P)
            if self.stop == "B" and l == self.stop_layer:
                return False
            self.phase_C(l, xcur, xnext, P)
            if self.stop == "C" and l == self.stop_layer:
                return False
        return True

    def exchange(self, l):
        S, dr, db = self.S, self.dr, self.db
        grp = [list(range(NC))]
        S.cc(lambda e: e.collective_compute("AllGather", ALU.bypass, replica_groups=grp,
                                            ins=[dr["EXP"].opt()], outs=[dr["GAT8"].opt()]),
             R=[db["EXP"]], W=[db["GAT8"]])
        S.cc(lambda e: e.collective_compute("AllGather", ALU.bypass, replica_groups=grp,
                                            ins=[dr["RETX"].opt()], outs=[dr["RETG8"].opt()]),
             R=[db["RETX"]], W=[db["RETG8"]])
        S.barrier()

        def cpb(e, which):
            if not hasattr(self, "_dynv"):
                pid = e.partition_id()
                rb = e.snap((pid // 4) * 4, min_val=0, max_val=4)
                rp = e.snap(rb + ((pid % 4) + 3) % 4, min_val=0, max_val=7)
                rn = e.snap(rb + ((pid % 4) + 1) % 4, min_val=0, max_val=7)
                self._dynv = (rb, rp, rn)
            rb, rp, rn = self._dynv
            if which < 4:
                return e.dma_start(out=dr["LATB"][which * 160:(which + 1) * 160, :],
                                   in_=dr["GAT8"][bass.ds((rb + which) * 256, 160), :])
            if which == 4:
                return e.dma_start(out=dr["RETG"], in_=dr["RETG8"][bass.ds(rb * 64, 256), :])
            rk = rp if which == 5 else rn
            return e.dma_start(out=dr["HALOP" if which == 5 else "HALON"], in_=dr["GAT8"][bass.ds(rk * 256 + 160, 96), :])

        for w in range(4):
            self.dyn_dma("pool", lambda e, w=w: cpb(e, w), [db["GAT8"]], [db["LATB"]])
        self.dyn_dma("pool", lambda e: cpb(e, 4), [db["RETG8"]], [db["RETG"]])
        self.dyn_dma("pool", lambda e: cpb(e, 5), [db["GAT8"]], [db["HALOP"]])
        self.dyn_dma("pool", lambda e: cpb(e, 6), [db["GAT8"]], [db["HALON"]])
        S.barrier()

    def emit_lconst(self, l, P):
        S, dr, db = self.S, self.dr, self.db
        lg = P["lg"]
        with S.phase():
            dec = S.sb("dec", [128, 8], F32)
            e1 = S.sb("e1", [128, 8], F32)
            S.dma("sp", dec.t[:, :], dr["decB"][l], R=[db["decB"]], W=[dec.b])
            S.op("act", lambda e: e.activation(out=e1.t[:, :], in_=dec.t[:, :], func=AF.Exp, scale=-1.0), R=[dec.b], W=[e1.b])
            S.op("dve", lambda e: e.tensor_scalar(out=e1.t[:, :], in0=e1.t[:, :], scalar1=1.0, scalar2=None, op0=ALU.add),
                 R=[e1.b], W=[e1.b])
            S.op("act", lambda e: e.activation(out=dec.t[:, :], in_=e1.t[:, :], func=AF.Ln), R=[e1.b], W=[dec.b])
            S.op("dve", lambda e: e.tensor_scalar(out=lg.t[:, :], in0=dec.t[:, :], scalar1=-1.0, scalar2=None, op0=ALU.mult),
                 R=[dec.b], W=[lg.b])

    def phase_A(self, l, xcur, P):
        S, dr, db = self.S, self.dr, self.db
        mod, s1, lg = P["mod"], P["s1"], P["lg"]
        with S.phase():
            win = S.sb("win", [128, 8, XW], BF16)
            for k in range(8):
                S.dma("pool", win.t[:, k, :], dr["win_g"][l * D + k * 128:l * D + (k + 1) * 128, :],
                      R=[db["win_g"]], W=[win.b])
            wuq = S.sb("wuq", [128, 2, 768], BF16)
            S.dma("pool", wuq.t[:, :, :], dr["wuq"][l].rearrange("(k p) c -> p k c", p=128), R=[db["wuq"]], W=[wuq.b])
            ident = S.sb("ident", [128, 128], BF16)
            S.dma("pool", ident.t[:, :], dr["ident"], R=[db["ident"]], W=[ident.b])
            ones = S.sb("ones", [128, 128], F32)
            S.op("pool", lambda e: e.memset(ones.t[:, :], 1.0), W=[ones.b])
            qn = S.sb("qn", [128, 2], F32)
            kvn = S.sb("kvn", [128, 1], F32)
            S.dma("sp", qn.t[:, :], dr["qnT"][l], R=[db["qnT"]], W=[qn.b])
            S.dma("sp", kvn.t[:, :], dr["kvnT"][l], R=[db["kvnT"]], W=[kvn.b])
            rete = S.sb("rete", [128, 2192], F32)
            S.dma("sp", rete.t[:, :], dr["rete"], R=[db["rete"]], W=[rete.b])
            c128 = S.sb("c128", [128, 64], F32)
            S.op("pool", lambda e: e.memset(c128.t[:, :], 128.0), W=[c128.b])
            KDt = S.sb("KDt", [128, 2, 256], F32)
            Gt = S.sb("Gt", [128, 2, 256], F32)
            for d in range(2):
                for h in range(4):
                    col = lg.t[:, d * 4 + h:d * 4 + h + 1]
                    S.op("act", lambda e, d=d, h=h, col=col: e.activation(
                        out=KDt.t[:, d, h * 64:(h + 1) * 64], in_=rete.t[:, 2048 + 64 * d:2112 + 64 * d], func=AF.Exp,
                        scale=col), R=[rete.b, lg.b], W=[KDt.b])
                    S.op("act", lambda e, d=d, h=h, col=col: e.activation(
                        out=Gt.t[:, d, h * 64:(h + 1) * 64], in_=c128.t[:, :], func=AF.Exp, scale=col),
                         R=[c128.b, lg.b], W=[Gt.b])
            Zf = S.sb("Zf", [64, 512], F32)
            Pc = S.sb("Pc", [64, 256], F32)
            ptmp = S.sb("ptmp", [64, 256], F32)
            S.op("pool", lambda e: e.memset(Zf.t[:, :], 0.0), W=[Zf.b])
            S.op("pool", lambda e: e.memset(Pc.t[:, :], 1.0), W=[Pc.b])
            xs2 = [S.sb("xs", [128, 8, 512], F32) for _ in range(2)]
            hT2 = [S.sb("hT", [128, 8, 512], BF16) for _ in range(2)]
            sq2 = [S.sb("sq", [128, 512], F32) for _ in range(2)]
            tmp2 = [S.sb("ntmp", [128, 512], F32) for _ in range(2)]
            rstd = S.sb("rstd", [128, 512], F32)
            rstd2 = S.sb("rstd2", [128, 512], F32)
            C64 = [S.sb("C64", [128, 512], F32) for _ in range(2)]
            S64 = [S.sb("S64", [128, 512], F32) for _ in range(2)]
            Cq = [S.sb("Cq", [96, 512], F32) for _ in range(2)]
            Sq = [S.sb("Sq", [96, 512], F32) for _ in range(2)]
            C32 = [S.sb("C32", [32, 512], F32) for _ in range(2)]
            S32 = [S.sb("S32", [32, 512], F32) for _ in range(2)]
            for i in range(2):
                S.op("pool", lambda e, i=i: e.memset(Cq[i].t[0:64, :], 1.0), W=[Cq[i].b])
                S.op("pool", lambda e, i=i: e.memset(Sq[i].t[0:64, :], 0.0), W=[Sq[i].b])
            t1s = [S.sb("t1", [128, 512], F32) for _ in range(3)]
            t2s = [S.sb("t2", [128, 512], F32) for _ in range(3)]
            obs = [S.sb("ob", [128, 512], BF16) for _ in range(6)]
            cqn = S.sb("cqn", [128, 2, 512], BF16)
            rkbf = S.sb("rkbf", [128, 2, 512], BF16)
            vts = [S.sb("vt", [128, 640], BF16) for _ in range(3)]
            kscs = [S.sb("ksc", [128, 2, 256], BF16) for _ in range(2)]
            Ucs = [S.sb("Uc", [64, 512], F32) for _ in range(2)]
            st = dict(ps=0, t=0, ob=0)

            def nps():
                st["ps"] = st["ps"] % 5 + 1
                return self.ps[st["ps"]]

            def nt():
                st["t"] = (st["t"] + 1) % 3
                return t1s[st["t"]], t2s[st["t"]]

            def nob():
                st["ob"] = (st["ob"] + 1) % 6
                return obs[st["ob"]]

            def proj(hT, W, col, M):
                ps = nps()
                for k in range(8):
                    self.mm(ps.t[0:M, :W], win.t[:, k, col:col + M], hT.t[:, k, :W], k == 0, k == 7, [win.b, hT.b], [ps.b])
                return ps

            def rope(psA, psB, Ct, St, M, W, ob, obap, scale=None):
                t1, t2 = nt()
                if scale is None:
                    S.op("dve", lambda e: e.tensor_tensor(out=t1.t[0:M, :W], in0=psA.t[0:M, :W], in1=Ct.t[0:M, :W], op=ALU.mult),
                         R=[psA.b, Ct.b], W=[t1.b])
                    S.op("dve", lambda e: e.tensor_tensor(out=t2.t[0:M, :W], in0=psB.t[0:M, :W], in1=St.t[0:M, :W], op=ALU.mult),
                         R=[psB.b, St.b], W=[t2.b])
                else:
                    S.op("dve", lambda e: e.scalar_tensor_tensor(out=t1.t[0:M, :W], in0=psA.t[0:M, :W], scalar=scale,
                                                                 in1=Ct.t[0:M, :W], op0=ALU.mult, op1=ALU.mult),
                         R=[psA.b, Ct.b], W=[t1.b])
                    S.op("dve", lambda e: e.scalar_tensor_tensor(out=t2.t[0:M, :W], in0=psB.t[0:M, :W], scalar=scale,
                                                                 in1=St.t[0:M, :W], op0=ALU.mult, op1=ALU.mult),
                         R=[psB.b, St.b], W=[t2.b])
                S.op("pool", lambda e: e.tensor_tensor(out=obap, in0=t1.t[0:M, :W], in1=t2.t[0:M, :W], op=ALU.add),
                     R=[t1.b, t2.b], W=[ob.b])

            expm = dr["EXP"][0:2560, :].rearrange("(f a) c -> f (a c)", a=16)

            for g, (t0, W) in enumerate(GROUPS):
                lat = g < 8
                j = 0 if lat else 1
                xs, hT = xs2[g % 2], hT2[g % 2]
                S.dma("sp", xs.t[:, :, :W], dr[xcur][:, t0:t0 + W].rearrange("(k p) w -> p k w", p=128),
                      R=[db[xcur]], W=[xs.b])
                c64, s64, cq, sq_, c32, s32 = C64[g % 2], S64[g % 2], Cq[g % 2], Sq[g % 2], C32[g % 2], S32[g % 2]
                for hh in range(2):
                    S.dma("sp", c64.t[64 * hh:64 * hh + 64, :W], dr["c64"][:, t0:t0 + W], R=[db["c64"]], W=[c64.b])
                    S.dma("sp", s64.t[64 * hh:64 * hh + 64, :W], dr["s64"][:, t0:t0 + W], R=[db["s64"]], W=[s64.b])
                S.dma("sp", cq.t[64:96, :W], dr["c32"][:, t0:t0 + W], R=[db["c32"]], W=[cq.b])
                S.dma("sp", sq_.t[64:96, :W], dr["s32"][:, t0:t0 + W], R=[db["s32"]], W=[sq_.b])
                S.dma("sp", c32.t[:, :W], dr["c32"][:, t0:t0 + W], R=[db["c32"]], W=[c32.b])
                S.dma("sp", s32.t[:, :W], dr["s32"][:, t0:t0 + W], R=[db["s32"]], W=[s32.b])
                self.norm_group(xs, W, lambda k: s1.t[:, k, j:j + 1], lambda k: mod.t[:, k, j:j + 1], hT, ones, 0,
                                sq2, rstd, tmp2, scale_sq=1.0 / 32.0)
                psa = proj(hT, W, XO["cq"], 128)
                psb_ = proj(hT, W, XO["cq"] + 128, 128)
                pst = self.ps[0]
                for k, pp in enumerate((psa, psb_)):
                    S.op("act", lambda e, k=k, pp=pp: e.activation(out=sq2[k].t[:, :W], in_=pp.t[:, :W], func=AF.Square,
                                                                   scale=1.0 / 16.0), R=[pp.b], W=[sq2[k].b])
                    self.mm(pst.t[:, :W], ones.t[:, :], sq2[k].t[:, :W], k == 0, k == 1, [ones.b, sq2[k].b], [pst.b])
                self.rstd_op(rstd2.t[:, :W], pst.t[:, :W], [pst.b], [rstd2.b])
                for k, pp in enumerate((psa, psb_)):
                    S.op("dve", lambda e, k=k, pp=pp: e.scalar_tensor_tensor(
                        out=cqn.t[:, k, :W], in0=pp.t[:, :W], scalar=qn.t[:, k:k + 1], in1=rstd2.t[:, :W],
                        op0=ALU.mult, op1=ALU.mult), R=[pp.b, qn.b, rstd2.b], W=[cqn.b])
                for h in range(4):
                    p1, p2 = nps(), nps()
                    for kc in range(2):
                        self.mm(p1.t[0:96, :W], wuq.t[:, kc, h * 192:h * 192 + 96], cqn.t[:, kc, :W], kc == 0, kc == 1,
                                [wuq.b, cqn.b], [p1.b])
                    for kc in range(2):
                        self.mm(p2.t[0:96, :W], wuq.t[:, kc, h * 192 + 96:h * 192 + 192], cqn.t[:, kc, :W], kc == 0, kc == 1,
                                [wuq.b, cqn.b], [p2.b])
                    ob = nob()
                    rope(p1, p2, cq, sq_, 96, W, ob, ob.t[0:96, :W])
                    S.dma("pool", dr["qmT"][h * 96:(h + 1) * 96, t0:t0 + W], ob.t[0:96, :W], R=[ob.b], W=[db["qmT"]])
                pp = proj(hT, W, XO["ckv"], 128)
                S.op("act", lambda e, pp=pp: e.activation(out=sq2[0].t[:, :W], in_=pp.t[:, :W], func=AF.Square,
                                                          scale=float(128.0 ** -0.5)), R=[pp.b], W=[sq2[0].b])
                self.mm(pst.t[:, :W], ones.t[:, :], sq2[0].t[:, :W], True, True, [ones.b, sq2[0].b], [pst.b])
                self.rstd_op(rstd2.t[:, :W], pst.t[:, :W], [pst.b], [rstd2.b])
                ob = nob()
                S.op("dve", lambda e, pp=pp, ob=ob: e.scalar_tensor_tensor(
                    out=ob.t[:, :W], in0=pp.t[:, :W], scalar=kvn.t[:, 0:1], in1=rstd2.t[:, :W], op0=ALU.mult, op1=ALU.mult),
                     R=[pp.b, kvn.b, rstd2.b], W=[ob.b])
                if lat:
                    S.dma("pool", expm[0:128, t0:t0 + W], ob.t[:, :W], R=[ob.b], W=[db["EXP"]])
                else:
                    S.dma("pool", dr["latc"][0:128, :], ob.t[:, :W], R=[ob.b], W=[db["latc"]])
                p1 = proj(hT, W, XO["kr"], 32)
                p2 = proj(hT, W, XO["krp"], 32)
                ob = nob()
                rope(p1, p2, c32, s32, 32, W, ob, ob.t[0:32, :W])
                if lat:
                    S.dma("pool", expm[128:160, t0:t0 + W], ob.t[0:32, :W], R=[ob.b], W=[db["EXP"]])
                else:
                    S.dma("pool", dr["latc"][128:160, :], ob.t[0:32, :W], R=[ob.b], W=[db["latc"]])
                for jj in range(2):
                    p1 = proj(hT, W, XO["rq"] + 128 * jj, 128)
                    p2 = proj(hT, W, XO["rqp"] + 128 * jj, 128)
                    ob = nob()
                    rope(p1, p2, c64, s64, 128, W, ob, ob.t[:, :W])
                    S.dma("pool", dr["rqT"][128 * jj:128 * jj + 128, t0:t0 + W], ob.t[:, :W], R=[ob.b], W=[db["rqT"]])
                for jj in range(2):
                    p1 = proj(hT, W, XO["rk"] + 128 * jj, 128)
                    p2 = proj(hT, W, XO["rkp"] + 128 * jj, 128)
                    rope(p1, p2, c64, s64, 128, W, rkbf, rkbf.t[:, jj, :W], scale=0.125)
                    S.dma("pool", dr["rkT"][128 * jj:128 * jj + 128, t0:t0 + W], rkbf.t[:, jj, :W], R=[rkbf.b], W=[db["rkT"]])
                for jj in range(2):
                    p1 = proj(hT, W, XO["sq"] + 128 * jj, 128)
                    p2 = proj(hT, W, XO["sqp"] + 128 * jj, 128)
                    ob = nob()
                    rope(p1, p2, c64, s64, 128, W, ob, ob.t[:, :W], scale=0.125)
                    S.dma("pool", dr["sqT"][128 * jj:128 * jj + 128, t0:t0 + W], ob.t[:, :W], R=[ob.b], W=[db["sqT"]])
                p1 = proj(hT, W, XO["sk"], 128)
                p2 = proj(hT, W, XO["skp"], 128)
                ob = nob()
                rope(p1, p2, c64, s64, 128, W, ob, ob.t[:, :W])
                S.dma("pool", dr["skT"][:, t0:t0 + W], ob.t[:, :W], R=[ob.b], W=[db["skT"]])
                if g == 0:
                    S.dma("pool", dr["EXP"][EX["skh"]:EX["skh"] + 128, 0:128], ob.t[:, 0:128], R=[ob.b], W=[db["EXP"]])
                if g == 7:
                    S.dma("pool", dr["EXP"][EX["skt"]:EX["skt"] + 128, 0:128], ob.t[:, 384:512], R=[ob.b], W=[db["EXP"]])
                for nm, dst in (("gf", "sgfT"), ("gb", "sgbT")):
                    for jj in range(2):
                        pp = proj(hT, W, XO[nm] + 128 * jj, 128)
                        ob = nob()
                        S.op("act", lambda e, pp=pp, ob=ob: e.activation(out=ob.t[:, :W], in_=pp.t[:, :W], func=AF.Silu),
                             R=[pp.b], W=[ob.b])
                        S.dma("pool", dr[dst][128 * jj:128 * jj + 128, t0:t0 + W], ob.t[:, :W], R=[ob.b], W=[db[dst]])
                for jj in range(2):
                    pp = proj(hT, W, XO["nq"] + 128 * jj, 128)
                    ob = nob()
                    S.op("act", lambda e, pp=pp, ob=ob: e.activation(out=ob.t[:, :W], in_=pp.t[:, :W], func=AF.Identity,
                                                                     scale=0.125), R=[pp.b], W=[ob.b])
                    S.dma("pool", dr["nqT"][128 * jj:128 * jj + 128, t0:t0 + W], ob.t[:, :W], R=[ob.b], W=[db["nqT"]])
                for jj in range(2):
                    pp = proj(hT, W, XO["nk"] + 128 * jj, 128)
                    ob = nob()
                    S.op("act", lambda e, pp=pp, ob=ob: e.activation(out=ob.t[:, :W], in_=pp.t[:, :W], func=AF.Identity),
                         R=[pp.b], W=[ob.b])
                    S.dma("pool", dr["nkT"][128 * jj:128 * jj + 128, t0:t0 + W], ob.t[:, :W], R=[ob.b], W=[db["nkT"]])
                    if g == 0:
                        S.dma("pool", dr["EXP"][EX["nkh"] + 128 * jj:EX["nkh"] + 128 * jj + 128, :], ob.t[:, 0:256],
                              R=[ob.b], W=[db["EXP"]])
                    if g == 7:
                        S.dma("pool", dr["EXP"][EX["nkt"] + 128 * jj:EX["nkt"] + 128 * jj + 128, :], ob.t[:, 256:512],
                              R=[ob.b], W=[db["EXP"]])
                for a in range(W // 128):
                    c = (t0 + 128 * a) // 128
                    vt = vts[c % 3]
                    pv = nps()
                    for k in range(8):
                        self.mm(pv.t[:, 0:512], hT.t[:, k, 128 * a:128 * a + 128], win.t[:, k, XO["rv"]:XO["rv"] + 512],
                                k == 0, k == 7, [hT.b, win.b], [pv.b])
                    pv2 = nps()
                    for k in range(8):
                        self.mm(pv2.t[:, 0:128], hT.t[:, k, 128 * a:128 * a + 128], win.t[:, k, XO["sv"]:XO["sv"] + 128],
                                k == 0, k == 7, [hT.b, win.b], [pv2.b])
                    S.op("act", lambda e, pv=pv, vt=vt: e.activation(out=vt.t[:, 0:512], in_=pv.t[:, 0:512], func=AF.Identity),
                         R=[pv.b], W=[vt.b])
                    S.op("act", lambda e, pv2=pv2, vt=vt: e.activation(out=vt.t[:, 512:640], in_=pv2.t[:, 0:128], func=AF.Identity),
                         R=[pv2.b], W=[vt.b])
                    r0 = t0 + 128 * a
                    S.dma("pool", dr["rvtm"][r0:r0 + 128, :], vt.t[:, 0:256], R=[vt.b], W=[db["rvtm"]])
                    S.dma("pool", dr["nvtm"][r0:r0 + 128, :], vt.t[:, 256:512], R=[vt.b], W=[db["nvtm"]])
                    S.dma("pool", dr["svtm"][r0:r0 + 128, :], vt.t[:, 512:640], R=[vt.b], W=[db["svtm"]])
                    if g == 0 and a < 2:
                        S.dma("pool", dr["EXP"][EX["nvh"] + 128 * a:EX["nvh"] + 128 * a + 128, :], vt.t[:, 256:512],
                              R=[vt.b], W=[db["EXP"]])
                    if g == 0 and a == 0:
                        S.dma("pool", dr["EXP"][EX["svh"]:EX["svh"] + 128, 0:128], vt.t[:, 512:640], R=[vt.b], W=[db["EXP"]])
                    if g == 7 and a >= 2:
                        S.dma("pool", dr["EXP"][EX["nvt"] + 128 * (a - 2):EX["nvt"] + 128 * (a - 2) + 128, :], vt.t[:, 256:512],
                              R=[vt.b], W=[db["EXP"]])
                    if g == 7 and a == 3:
                        S.dma("pool", dr["EXP"][EX["svt"]:EX["svt"] + 128, 0:128], vt.t[:, 512:640], R=[vt.b], W=[db["EXP"]])
                    pb = self.psb
                    for jj in range(2):
                        S.op("pe", lambda e, jj=jj, a=a: e.transpose(pb.t[:, 128 * jj:128 * jj + 128],
                                                                     rkbf.t[:, jj, 128 * a:128 * a + 128], ident.t[:, :]),
                             R=[rkbf.b, ident.b], W=[pb.b])
                    ksc = kscs[c % 2]
                    for d in range(2):
                        S.op("dve", lambda e, d=d, ksc=ksc: e.tensor_tensor(out=ksc.t[:, d, :], in0=pb.t[:, 0:256],
                                                                            in1=KDt.t[:, d, :], op=ALU.mult),
                             R=[pb.b, KDt.b], W=[ksc.b])
                    pu = self.ps[6]
                    for d in range(2):
                        for h in range(4):
                            self.mm(pu.t[0:64, (d * 4 + h) * 64:(d * 4 + h) * 64 + 64], ksc.t[:, d, h * 64:(h + 1) * 64],
                                    vt.t[:, h * 64:(h + 1) * 64], True, True, [ksc.b, vt.b], [pu.b])
                    Uc = Ucs[c % 2]
                    S.op("act", lambda e, Uc=Uc: e.activation(out=Uc.t[:, :], in_=pu.t[0:64, :], func=AF.Identity),
                         R=[pu.b], W=[Uc.b])
                    S.dma("pool", dr["UD"][c * 64:(c + 1) * 64, :], Uc.t[:, :], R=[Uc.b], W=[db["UD"]])
                    if lat:
                        S.op("pool", lambda e: e.tensor_tensor(out=Zf.t[:, 0:256], in0=Zf.t[:, 0:256], in1=Gt.t[0:64, 0, :], op=ALU.mult),
                             R=[Zf.b, Gt.b], W=[Zf.b])
                        S.op("pool", lambda e, Uc=Uc: e.tensor_tensor(out=Zf.t[:, 0:256], in0=Zf.t[:, 0:256], in1=Uc.t[:, 0:256], op=ALU.add),
                             R=[Zf.b, Uc.b], W=[Zf.b])
                        S.op("pool", lambda e, Uc=Uc: e.tensor_tensor(out=ptmp.t[:, :], in0=Pc.t[:, :], in1=Uc.t[:, 256:512], op=ALU.mult),
                             R=[Pc.b, Uc.b], W=[ptmp.b])
                        S.op("pool", lambda e: e.tensor_tensor(out=Zf.t[:, 256:512], in0=Zf.t[:, 256:512], in1=ptmp.t[:, :], op=ALU.add),
                             R=[Zf.b, ptmp.b], W=[Zf.b])
                        S.op("pool", lambda e: e.tensor_tensor(out=Pc.t[:, :], in0=Pc.t[:, :], in1=Gt.t[0:64, 1, :], op=ALU.mult),
                             R=[Pc.b, Gt.b], W=[Pc.b])
            S.dma("pool", dr["RETX"], Zf.t[:, :], R=[Zf.b], W=[db["RETX"]])

    def attend(self, A, QT, dq, W, blocks, scale, dst_ap, dst_buf):
        S = self.S
        if "skip_attend" in self.flags:
            return
        st = A["st"]
        st["po"] = 1 - st["po"]
        po = self.ps[4 + st["po"]]
        nb = len(blocks)
        pend = []

        def flush(keep):
            while len(pend) > keep:
                bi2, vap2, rhs2, rb2 = pend.pop(0)
                self.mm(po.t[0:65, :W], vap2, rhs2, bi2 == 0, bi2 == nb - 1, rb2, [po.b])

        for bi, blk in enumerate(blocks):
            if blk[0] == "k":
                _, kap, kb, vap, vb, bap, bb = blk
                st["ps"] = (st["ps"] + 1) % 4
                pS = self.ps[st["ps"]]
                self.mm(pS.t[:, :W], kap, QT.t[0:dq, :W], True, bap is None, list(kb) + [QT.b], [pS.b])
                if bap is not None:
                    self.mm(pS.t[:, :W], A["ident"].t[:, :], bap, False, True, [A["ident"].b] + list(bb), [pS.b])
                st["pt"] = (st["pt"] + 1) % len(A["Pt"])
                Pt = A["Pt"][st["pt"]]
                S.op("act", lambda e, pS=pS, Pt=Pt: e.activation(out=Pt.t[:, :W], in_=pS.t[:, :W], func=AF.Exp, scale=scale),
                     R=[pS.b], W=[Pt.b])
                pend.append((bi, vap, Pt.t[:, :W], list(vb) + [Pt.b]))
                flush(2)
            else:
                _, pap, pb, vap, vb = blk
                pend.append((bi, vap, pap, list(vb) + list(pb)))
        flush(0)
        rec, oc, sel = A["rec"], A["oc"], A["sel"]
        pB = self.ps[6]
        S.op("act", lambda e: e.activation(out=oc.t[0:65, :W], in_=po.t[0:65, :W], func=AF.Identity), R=[po.b], W=[oc.b])
        self.mm(pB.t[0:64, :W], sel.t[0:65, 0:64], oc.t[0:65, :W], True, True, [sel.b, oc.b], [pB.b])
        S.op("dve", lambda e: e.reciprocal(out=rec.t[0:64, :W], in_=pB.t[0:64, :W]), R=[pB.b], W=[rec.b])
        st["ob"] = (st["ob"] + 1) % len(A["ob"])
        ob = A["ob"][st["ob"]]
        S.op("dve", lambda e: e.tensor_tensor(out=ob.t[0:64, :W], in0=oc.t[0:64, :W], in1=rec.t[0:64, :W], op=ALU.mult),
             R=[oc.b, rec.b], W=[ob.b])
        S.dma("pool", dst_ap, ob.t[0:64, :W], R=[ob.b], W=[dst_buf])

    def attn_tiles(self, with_ident=True):
        S, dr, db = self.S, self.dr, self.db
        A = dict(Pt=[S.sb("Pt", [128, 512], BF16) for _ in range(4)], rec=S.sb("rec", [128, 512], F32),
                 oc=S.sb("oc", [65, 512], F32), ob=[S.sb("aob", [64, 512], BF16) for _ in range(2)],
                 sel=S.sb("sel", [65, 64], F32), st=dict(ps=0, pt=0, po=0, ob=0),
                 QT=[S.sb("QT", [96, 512], BF16) for _ in range(2)])
        S.op("pool", lambda e: e.memset(A["sel"].t[0:65, :], 0.0), W=[A["sel"].b])
        S.op("pool", lambda e: e.memset(A["sel"].t[64:65, :], 1.0), W=[A["sel"].b])
        if with_ident:
            A["ident"] = S.sb("identb", [128, 128], BF16)
            S.dma("pool", A["ident"].t[:, :], dr["ident"], R=[db["ident"]], W=[A["ident"].b])
        return A

    def dyn_dma(self, q, fn, R, W):
        o = Op(q, fn, "d")
        self.S._deps(o, R, W)
        self.S.ops.append(o)

    def phase_B(self, l, P):
        if "no_mla" not in self.flags:
            self.phase_B_mla(l)
        if "no_na" not in self.flags:
            self.phase_B_na(l)
        if "no_swa" not in self.flags:
            self.phase_B_swa(l)
        if "no_ret" not in self.flags:
            self.phase_B_ret(l, P)

    def phase_B_mla(self, l):
        S, dr, db = self.S, self.dr, self.db
        NK = 4 * TL + LC
        NB = NK // 128
        scale = float(96.0 ** -0.5)
        with S.phase():
            if "ms0a" in self.flags:
                return
            A = self.attn_tiles(False)
            if "ms0b" in self.flags:
                return
            latK = S.sb("latK", [128, NK], BF16)
            KT = S.sb("KT", [96, NK], BF16)
            V = S.sb("V", [128, NB, 65], BF16)
            wukv = S.sb("wukv", [128, 512], BF16)
            S.dma("pool", wukv.t[:, :], dr["wukv"][l], R=[db["wukv"]], W=[wukv.b])
            S.op("pool", lambda e: e.memset(V.t[:, :, 64:65], 1.0), W=[V.b])
            if "ms1" in self.flags:
                return
            for r in range(4):
                S.dma("sp", latK.t[:, r * TL:(r + 1) * TL], dr["LATB"][r * 160:r * 160 + 128, :], R=[db["LATB"]], W=[latK.b])
                S.dma("sp", KT.t[64:96, r * TL:(r + 1) * TL], dr["LATB"][r * 160 + 128:r * 160 + 160, :], R=[db["LATB"]], W=[KT.b])
            S.dma("sp", latK.t[:, 4 * TL:NK], dr["latc"][0:128, :], R=[db["latc"]], W=[latK.b])
            S.dma("sp", KT.t[64:96, 4 * TL:NK], dr["latc"][128:160, :], R=[db["latc"]], W=[KT.b])
            if "ms2" in self.flags:
                return
            for h in range(1 if "mla_h1" in self.flags else 4):
                nblk = (NK + 511) // 512
                for bi in range(nblk):
                    w = min(512, NK - bi * 512)
                    ps = self.ps[bi % 4]
                    self.mm(ps.t[0:64, :w], wukv.t[:, h * 128:h * 128 + 64], latK.t[:, bi * 512:bi * 512 + w], True, True,
                            [wukv.b, latK.b], [ps.b])
                    eng = "act" if bi % 2 == 0 else "dve"
                    if eng == "act":
                        S.op("act", lambda e, ps=ps, bi=bi, w=w: e.activation(out=KT.t[0:64, bi * 512:bi * 512 + w], in_=ps.t[0:64, :w],
                                                                             func=AF.Identity), R=[ps.b], W=[KT.b])
                    else:
                        S.op("dve", lambda e, ps=ps, bi=bi, w=w: e.tensor_copy(out=KT.t[0:64, bi * 512:bi * 512 + w], in_=ps.t[0:64, :w]),
                             R=[ps.b], W=[KT.b])
                if "ms3" in self.flags:
                    continue
                for b8 in range((NB + 7) // 8):
                    n = min(8, NB - b8 * 8)
                    ps = self.ps[b8 % 4]
                    for i in range(n):
                        kb = b8 * 8 + i
                        self.mm(ps.t[:, 64 * i:64 * i + 64], latK.t[:, kb * 128:kb * 128 + 128], wukv.t[:, h * 128 + 64:h * 128 + 128],
                                True, True, [latK.b, wukv.b], [ps.b])
                    eng = "act" if b8 % 2 == 0 else "dve"
                    src = ps.t[:, 0:64 * n].rearrange("p (a d) -> p a d", d=64)
                    if eng == "act":
                        S.op("act", lambda e, src=src, b8=b8, n=n: e.activation(out=V.t[:, b8 * 8:b8 * 8 + n, 0:64], in_=src, func=AF.Identity),
                             R=[ps.b], W=[V.b])
                    else:
                        S.op("dve", lambda e, src=src, b8=b8, n=n: e.tensor_copy(out=V.t[:, b8 * 8:b8 * 8 + n, 0:64], in_=src),
                             R=[ps.b], W=[V.b])
                for g, (t0, W) in enumerate(GROUPS):
                    QT = A["QT"][g % 2]
                    S.dma("sp", QT.t[0:96, :W], dr["qmT"][h * 96:(h + 1) * 96, t0:t0 + W], R=[db["qmT"]], W=[QT.b])
                    kbs = range(NB) if g < 8 else range(NB - 2, NB)
                    blocks = [("k", KT.t[0:96, kb * 128:kb * 128 + 128], [KT.b], V.t[:, kb, :], [V.b], None, None) for kb in kbs]
                    self.attend(A, QT, 96, W, blocks, scale, dr["mxT"][h * 64:(h + 1) * 64, t0:t0 + W], db["mxT"])

    def phase_B_na(self, l):
        S, dr, db = self.S, self.dr, self.db
        with S.phase():
            A = self.attn_tiles(True)
            rmT = S.sb("rmT", [128, 3, 8, 512], BF16)
            S.dma("sp", rmT.t[:, :, :, :], dr["rm"].rearrange("(v j p) c -> p v j c", v=3, j=8), R=[db["rm"]], W=[rmT.b])
            toeps = [S.sb("toep", [128, 8, 512], F32) for _ in range(2)]
            biass = [S.sb("bias", [128, 3, 8, 512], BF16) for _ in range(2)]
            klocs = [S.sb("kloc", [64, 4608], BF16) for _ in range(2)]
            vlocs = [S.sb("vloc", [128, 36, 65], BF16) for _ in range(2)]
            kctxs = [S.sb("kctx", [64, 256], BF16) for _ in range(2)]
            vctxs = [S.sb("vctx", [128, 2, 65], BF16) for _ in range(2)]
            for i in range(2):
                S.op("pool", lambda e, i=i: e.memset(vlocs[i].t[:, :, 64:65], 1.0), W=[vlocs[i].b])
                S.op("pool", lambda e, i=i: e.memset(vctxs[i].t[:, :, 64:65], 1.0), W=[vctxs[i].b])
            for h in range(4):
                toep, bias, kloc, vloc, kctx, vctx = (x[h % 2] for x in (toeps, biass, klocs, vlocs, kctxs, vctxs))
                r0 = ((l * 4 + h) * 8) * 128
                S.dma("sp", toep.t[:, :, :], dr["toep_g"][r0:r0 + 1024, :].rearrange("(j p) c -> p j c", p=128),
                      R=[db["toep_g"]], W=[toep.b])
                for v in range(3):
                    for j in range(8):
                        eng = "pool" if (v * 8 + j) % 2 == 0 else "dve"
                        S.op(eng, lambda e, v=v, j=j, bias=bias, toep=toep: e.tensor_tensor(
                            out=bias.t[:, v, j, :], in0=toep.t[:, j, :], in1=rmT.t[:, v, j, :], op=ALU.add),
                             R=[toep.b, rmT.b], W=[bias.b])
                hs = slice(h * 64, (h + 1) * 64)
                S.dma("sp", kloc.t[:, 256:256 + TL], dr["nkT"][hs, 0:TL], R=[db["nkT"]], W=[kloc.b])
                S.dma("sp", kctx.t[:, :], dr["nkT"][hs, TL:T], R=[db["nkT"]], W=[kctx.b])
                S.dma("sp", vloc.t[:, 2:34, 0:64], dr["nvtm"][0:TL, hs].rearrange("(a p) d -> p a d", p=128),
                      R=[db["nvtm"]], W=[vloc.b])
                S.dma("sp", vctx.t[:, :, 0:64], dr["nvtm"][TL:T, hs].rearrange("(a p) d -> p a d", p=128),
                      R=[db["nvtm"]], W=[vctx.b])

                HP, HN = (dr[k].rearrange("r (a c) -> (r a) c", c=256) for k in ("HALOP", "HALON"))
                o = lambda k: EX[k] - 2560
                S.dma("sp", kloc.t[:, 0:256], HP[o("nkt") + h * 64:o("nkt") + h * 64 + 64, :], R=[db["HALOP"]], W=[kloc.b])
                S.dma("sp", kloc.t[:, 256 + TL:512 + TL], HN[o("nkh") + h * 64:o("nkh") + h * 64 + 64, :], R=[db["HALON"]], W=[kloc.b])
                S.dma("sp", vloc.t[:, 0:2, 0:64], HP[o("nvt"):o("nvt") + 256, h * 64:(h + 1) * 64].rearrange("(a p) d -> p a d", p=128),
                      R=[db["HALOP"]], W=[vloc.b])
                S.dma("sp", vloc.t[:, 34:36, 0:64], HN[o("nvh"):o("nvh") + 256, h * 64:(h + 1) * 64].rearrange("(a p) d -> p a d", p=128),
                      R=[db["HALON"]], W=[vloc.b])
                for g, (t0, W) in enumerate(GROUPS):
                    QT = A["QT"][g % 2]
                    S.dma("sp", QT.t[0:64, :W], dr["nqT"][hs, t0:t0 + W], R=[db["nqT"]], W=[QT.b])
                    blocks = []
                    if g < 8:
                        v = 0 if g == 0 else (2 if g == 7 else 1)
                        for j in range(8):
                            kb = 4 * g + j
                            blocks.append(("k", kloc.t[:, kb * 128:kb * 128 + 128], [kloc.b], vloc.t[:, kb, :], [vloc.b],
                                           bias.t[:, v, j, :W], [bias.b]))
                    for cb in range(2):
                        blocks.append(("k", kctx.t[:, cb * 128:cb * 128 + 128], [kctx.b], vctx.t[:, cb, :], [vctx.b], None, None))
                    self.attend(A, QT, 64, W, blocks, 1.0, dr["mxT"][512 + h * 64:512 + (h + 1) * 64, t0:t0 + W], db["mxT"])

    def phase_B_swa(self, l):
        S, dr, db = self.S, self.dr, self.db
        with S.phase():
            A = self.attn_tiles(True)
            msT = S.sb("msT", [128, 3, 6, 512], BF16)
            S.dma("sp", msT.t[:, :, :, :], dr["ms"].rearrange("(v j p) c -> p v j c", v=3, j=6), R=[db["ms"]], W=[msT.b])
            sk32 = S.sb("sk32", [1, 2048], F32)
            psink = S.sb("psink", [1, 2048], BF16)
            vsink = S.sb("vsink", [1, 65], BF16)
            S.dma("sp", sk32.t[:, :], dr["sinkB"][l], R=[db["sinkB"]], W=[sk32.b])
            S.op("act", lambda e: e.activation(out=psink.t[:, :], in_=sk32.t[:, :], func=AF.Exp), R=[sk32.b], W=[psink.b])
            S.op("pool", lambda e: e.memset(vsink.t[:, 0:64], 0.0), W=[vsink.b])
            S.op("pool", lambda e: e.memset(vsink.t[:, 64:65], 1.0), W=[vsink.b])
            klocs = [S.sb("skloc", [64, 256 + TL], BF16) for _ in range(2)]
            vlocs = [S.sb("svloc", [128, 34, 65], BF16) for _ in range(2)]
            kctxs = [S.sb("skctx", [64, 256], BF16) for _ in range(2)]
            vctxs = [S.sb("svctx", [128, 2, 65], BF16) for _ in range(2)]
            for i in range(2):
                S.op("pool", lambda e, i=i: e.memset(vlocs[i].t[:, :, 64:65], 1.0), W=[vlocs[i].b])
                S.op("pool", lambda e, i=i: e.memset(vctxs[i].t[:, :, 64:65], 1.0), W=[vctxs[i].b])
            for g2 in range(2):
                kloc, vloc, kctx, vctx = klocs[g2], vlocs[g2], kctxs[g2], vctxs[g2]
                hs = slice(g2 * 64, (g2 + 1) * 64)
                S.dma("sp", kloc.t[:, 128:128 + TL], dr["skT"][hs, 0:TL], R=[db["skT"]], W=[kloc.b])
                S.dma("sp", kctx.t[:, :], dr["skT"][hs, TL:T], R=[db["skT"]], W=[kctx.b])
                S.dma("sp", vloc.t[:, 1:33, 0:64], dr["svtm"][0:TL, hs].rearrange("(a p) d -> p a d", p=128),
                      R=[db["svtm"]], W=[vloc.b])
                S.dma("sp", vctx.t[:, :, 0:64], dr["svtm"][TL:T, hs].rearrange("(a p) d -> p a d", p=128),
                      R=[db["svtm"]], W=[vctx.b])

                HP, HN = (dr[k].rearrange("r (a c) -> (r a) c", c=256) for k in ("HALOP", "HALON"))
                o = lambda k: EX[k] - 2560
                S.dma("sp", kloc.t[:, 0:128], HP[o("skt") + g2 * 64:o("skt") + g2 * 64 + 64, 0:128], R=[db["HALOP"]], W=[kloc.b])
                S.dma("sp", kloc.t[:, 128 + TL:256 + TL], HN[o("skh") + g2 * 64:o("skh") + g2 * 64 + 64, 0:128], R=[db["HALON"]], W=[kloc.b])
                S.dma("sp", vloc.t[:, 0, 0:64], HP[o("svt"):o("svt") + 128, g2 * 64:(g2 + 1) * 64], R=[db["HALOP"]], W=[vloc.b])
                S.dma("sp", vloc.t[:, 33, 0:64], HN[o("svh"):o("svh") + 128, g2 * 64:(g2 + 1) * 64], R=[db["HALON"]], W=[vloc.b])
                for h in (2 * g2, 2 * g2 + 1):
                    for g, (t0, W) in enumerate(GROUPS):
                        QT = A["QT"][g % 2]
                        S.dma("sp", QT.t[0:64, :W], dr["sqT"][h * 64:(h + 1) * 64, t0:t0 + W], R=[db["sqT"]], W=[QT.b])
                        blocks = []
                        if g < 8:
                            v = 0 if g == 0 else (2 if g == 7 else 1)
                            for j in range(6):
                                kb = 4 * g + j
                                blocks.append(("k", kloc.t[:, kb * 128:kb * 128 + 128], [kloc.b], vloc.t[:, kb, :], [vloc.b],
                                               msT.t[:, v, j, :W], [msT.b]))
                        for cb in range(2):
                            blocks.append(("k", kctx.t[:, cb * 128:cb * 128 + 128], [kctx.b], vctx.t[:, cb, :], [vctx.b], None, None))
                        blocks.append(("p", psink.t[0:1, h * 512:h * 512 + W], [psink.b], vsink.t[0:1, :], [vsink.b]))
                        self.attend(A, QT, 64, W, blocks, 1.0, dr["mxT"][768 + h * 64:768 + (h + 1) * 64, t0:t0 + W], db["mxT"])

    def phase_B_ret(self, l, P):
        S, dr, db = self.S, self.dr, self.db
        lg = P["lg"]
        RO = dict(ef=0, eb=512, qf=1024, qb=1536, kf=2048, kb=2112, rk=2176)
        with S.phase():
            rete = S.sb("rete", [128, 2192], F32)
            S.dma("sp", rete.t[:, :], dr["rete"], R=[db["rete"]], W=[rete.b])
            c128 = S.sb("c128", [128, 64], F32)
            S.op("pool", lambda e: e.memset(c128.t[:, :], 128.0), W=[c128.b])
            ones = S.sb("ones", [128, 64], F32)
            S.op("pool", lambda e: e.memset(ones.t[:, :], 1.0), W=[ones.b])
            Dm = S.sb("Dm", [128, 2, 4, 512], F32)
            QD = S.sb("QD", [128, 2, 4, 512], F32)
            Gt = S.sb("Gt", [128, 2, 256], F32)
            coef = S.sb("coef", [128, 2, 4, 5], F32)
            for d in range(2):
                for h in range(4):
                    col = lg.t[:, d * 4 + h:d * 4 + h + 1]
                    S.op("act", lambda e, d=d, h=h, col=col: e.activation(out=Dm.t[:, d, h, :], in_=rete.t[:, 512 * d:512 * d + 512],
                                                                          func=AF.Exp, scale=col), R=[rete.b, lg.b], W=[Dm.b])
                    S.op("act", lambda e, d=d, h=h, col=col: e.activation(out=QD.t[:, d, h, :], in_=rete.t[:, 1024 + 512 * d:1536 + 512 * d],
                                                                          func=AF.Exp, scale=col), R=[rete.b, lg.b], W=[QD.b])
                    S.op("act", lambda e, d=d, h=h, col=col: e.activation(out=Gt.t[:, d, h * 64:(h + 1) * 64], in_=c128.t[:, :],
                                                                          func=AF.Exp, scale=col), R=[c128.b, lg.b], W=[Gt.b])
                    S.op("act", lambda e, d=d, h=h, col=col: e.activation(out=coef.t[:, d, h, :], in_=rete.t[:, RO["rk"] + 5 * d:RO["rk"] + 5 * d + 5],
                                                                          func=AF.Exp, scale=col), R=[rete.b, lg.b], W=[coef.b])
            UDs = S.sb("UDs", [64, 34, 512], F32)
            S.dma("sp", UDs.t[:, :, :], dr["UD"].rearrange("(c p) f -> p c f", p=64), R=[db["UD"]], W=[UDs.b])
            AG = S.sb("AG", [64, 4, 512], F32)
            S.dma("sp", AG.t[:, :, :], dr["RETG"].rearrange("(i p) f -> p i f", p=64), R=[db["RETG"]], W=[AG.b])
            Sall = S.sb("Sall", [64, 2, 34, 256], BF16)
            Rs = S.sb("Rs", [64, 2, 256], F32)
            sctx = S.sb("sctx", [64, 2, 256], F32)
            S.op("pool", lambda e: e.memset(Sall.t[:, 0, 32, :], 0.0), W=[Sall.b])
            S.op("pool", lambda e: e.memset(Sall.t[:, 1, 33, :], 0.0), W=[Sall.b])
            S.op("pool", lambda e: e.tensor_copy(out=Sall.t[:, 0, 33, :], in_=UDs.t[:, 32, 0:256]), R=[UDs.b], W=[Sall.b])
            S.op("pool", lambda e: e.tensor_copy(out=Sall.t[:, 1, 32, :], in_=UDs.t[:, 33, 256:512]), R=[UDs.b], W=[Sall.b])
            S.op("pool", lambda e: e.tensor_tensor(out=sctx.t[:, 0, :], in0=UDs.t[:, 32, 0:256], in1=Gt.t[0:64, 0, :], op=ALU.mult),
                 R=[UDs.b, Gt.b], W=[sctx.b])
            S.op("pool", lambda e: e.tensor_tensor(out=sctx.t[:, 0, :], in0=sctx.t[:, 0, :], in1=UDs.t[:, 33, 0:256], op=ALU.add),
                 R=[UDs.b, sctx.b], W=[sctx.b])
            S.op("pool", lambda e: e.tensor_tensor(out=sctx.t[:, 1, :], in0=UDs.t[:, 33, 256:512], in1=Gt.t[0:64, 1, :], op=ALU.mult),
                 R=[UDs.b, Gt.b], W=[sctx.b])
            S.op("pool", lambda e: e.tensor_tensor(out=sctx.t[:, 1, :], in0=sctx.t[:, 1, :], in1=UDs.t[:, 32, 256:512], op=ALU.add),
                 R=[UDs.b, sctx.b], W=[sctx.b])
            for d in range(2):
                for h in range(4):
                    hs = slice(h * 64, (h + 1) * 64)
                    S.op("dve", lambda e, d=d, h=h, hs=hs: e.tensor_scalar(out=Rs.t[:, d, hs], in0=sctx.t[:, d, hs],
                                                                           scalar1=coef.t[0:64, d, h, 4:5], scalar2=None, op0=ALU.mult),
                         R=[sctx.b, coef.b], W=[Rs.b])
                    for i in range(4):
                        S.op("dve", lambda e, d=d, h=h, hs=hs, i=i: e.scalar_tensor_tensor(
                            out=Rs.t[:, d, hs], in0=AG.t[:, i, d * 256 + h * 64:d * 256 + (h + 1) * 64], scalar=coef.t[0:64, d, h, i:i + 1],
                            in1=Rs.t[:, d, hs], op0=ALU.mult, op1=ALU.add), R=[AG.b, coef.b, Rs.b], W=[Rs.b])
            for d in range(2):
                order = range(32) if d == 0 else range(31, -1, -1)
                for c in order:
                    S.op("pool", lambda e, d=d, c=c: e.tensor_copy(out=Sall.t[:, d, c, :], in_=Rs.t[:, d, :]), R=[Rs.b], W=[Sall.b])
                    S.op("dve", lambda e, d=d: e.tensor_tensor(out=Rs.t[:, d, :], in0=Rs.t[:, d, :], in1=Gt.t[0:64, d, :], op=ALU.mult),
                         R=[Rs.b, Gt.b], W=[Rs.b])
                    S.op("dve", lambda e, d=d, c=c: e.tensor_tensor(out=Rs.t[:, d, :], in0=Rs.t[:, d, :], in1=UDs.t[:, c, d * 256:(d + 1) * 256], op=ALU.add),
                         R=[Rs.b, UDs.b], W=[Rs.b])
            qTs = [S.sb("rq", [64, 512], BF16) for _ in range(2)]
            kTs = [S.sb("rk", [64, 512], BF16) for _ in range(2)]
            vhs = [S.sb("rv", [128, 4, 64], BF16) for _ in range(2)]
            gfs = [S.sb("rgf", [64, 512], BF16) for _ in range(2)]
            gbs = [S.sb("rgb", [64, 512], BF16) for _ in range(2)]
            qss = [S.sb("qs", [64, 2, 512], BF16) for _ in range(2)]
            atm = [S.sb("atm", [128, 2, 512], BF16) for _ in range(2)]
            sqd = [S.sb("rsq", [64, 512], F32) for _ in range(2)]
            rsd = [S.sb("rrs", [64, 512], F32) for _ in range(2)]
            od = [S.sb("rod", [64, 512], F32) for _ in range(2)]
            obs = [S.sb("rob", [64, 512], BF16) for _ in range(2)]
            it = 0
            for g, (t0, W) in enumerate(GROUPS):
                nch = W // 128
                for h in range(4):
                    i2 = it % 2
                    it += 1
                    hs = slice(h * 64, (h + 1) * 64)
                    qT, kT, vh, gf, gb, qs, at = qTs[i2], kTs[i2], vhs[i2], gfs[i2], gbs[i2], qss[i2], atm[i2]
                    S.dma("sp", qT.t[:, :W], dr["rqT"][hs, t0:t0 + W], R=[db["rqT"]], W=[qT.b])
                    S.dma("sp", kT.t[:, :W], dr["rkT"][hs, t0:t0 + W], R=[db["rkT"]], W=[kT.b])
                    S.dma("sp", vh.t[:, 0:nch, :], dr["rvtm"][t0:t0 + W, hs].rearrange("(a p) d -> p a d", p=128), R=[db["rvtm"]], W=[vh.b])
                    S.dma("sp", gf.t[:, :W], dr["sgfT"][hs, t0:t0 + W], R=[db["sgfT"]], W=[gf.b])
                    S.dma("sp", gb.t[:, :W], dr["sgbT"][hs, t0:t0 + W], R=[db["sgbT"]], W=[gb.b])
                    for d in range(2):
                        S.op("pool", lambda e, d=d, qs=qs, qT=qT: e.tensor_tensor(out=qs.t[:, d, :W], in0=qT.t[:, :W], in1=QD.t[0:64, d, h, :W], op=ALU.mult),
                             R=[qT.b, QD.b], W=[qs.b])
                    pA = self.ps[i2]
                    for a in range(nch):
                        self.mm(pA.t[:, 128 * a:128 * a + 128], kT.t[:, 128 * a:128 * a + 128], qT.t[:, 128 * a:128 * a + 128], True, True,
                                [kT.b, qT.b], [pA.b])
                    for d in range(2):
                        S.op("dve", lambda e, d=d, at=at, pA=pA: e.tensor_tensor(out=at.t[:, d, :W], in0=pA.t[:, :W], in1=Dm.t[:, d, h, :W], op=ALU.mult),
                             R=[pA.b, Dm.b], W=[at.b])
                    for d in range(2):
                        po = self.ps[2 + d]
                        for a in range(nch):
                            c = t0 // 128 + a
                            self.mm(po.t[0:64, 128 * a:128 * a + 128], vh.t[:, a, :], at.t[:, d, 128 * a:128 * a + 128], True, False,
                                    [vh.b, at.b], [po.b])
                            self.mm(po.t[0:64, 128 * a:128 * a + 128], Sall.t[:, d, c, hs], qs.t[:, d, 128 * a:128 * a + 128], False, True,
                                    [Sall.b, qs.b], [po.b])
                        pst = self.ps[4 + d]
                        S.op("act", lambda e, d=d, po=po: e.activation(out=sqd[d].t[:, :W], in_=po.t[0:64, :W], func=AF.Square, scale=0.125),
                             R=[po.b], W=[sqd[d].b])
                        self.mm(pst.t[0:64, :W], ones.t[0:64, 0:64], sqd[d].t[:, :W], True, True, [ones.b, sqd[d].b], [pst.b])
                        self.rstd_op(rsd[d].t[:, :W], pst.t[0:64, :W], [pst.b], [rsd[d].b])
                        S.op("dve", lambda e, d=d, po=po: e.tensor_tensor(out=od[d].t[:, :W], in0=po.t[0:64, :W], in1=rsd[d].t[:, :W], op=ALU.mult),
                             R=[po.b, rsd[d].b], W=[od[d].b])
                        gg = gf if d == 0 else gb
                        S.op("pool", lambda e, d=d, gg=gg: e.tensor_tensor(out=od[d].t[:, :W], in0=od[d].t[:, :W], in1=gg.t[:, :W], op=ALU.mult),
                             R=[od[d].b, gg.b], W=[od[d].b])
                    ob = obs[i2]
                    S.op("pool", lambda e, ob=ob: e.tensor_tensor(out=ob.t[:, :W], in0=od[0].t[:, :W], in1=od[1].t[:, :W], op=ALU.add),
                         R=[od[0].b, od[1].b], W=[ob.b])
                    S.dma("pool", dr["mxT"][256 + h * 64:256 + (h + 1) * 64, t0:t0 + W], ob.t[:, :W], R=[ob.b], W=[db["mxT"]])

    def phase_C(self, l, xcur, xnext, P):
        S, dr, db = self.S, self.dr, self.db
        mod, s2 = P["mod"], P["s2"]
        with S.phase():
            wout = S.sb("wout", [128, 8, D], BF16)
            for k in range(8):
                S.dma("pool", wout.t[:, k, :], dr["wout_g"][l * D + k * 128:l * D + (k + 1) * 128, :], R=[db["wout_g"]], W=[wout.b])
            ones = S.sb("ones", [128, 128], F32)
            S.op("pool", lambda e: e.memset(ones.t[:, :], 1.0), W=[ones.b])
            xs2 = [S.sb("xs", [128, 8, 512], F32) for _ in range(2)]
            mx2 = [S.sb("mx", [128, 8, 512], BF16) for _ in range(2)]
            h22 = [S.sb("h2", [128, 8, 512], BF16) for _ in range(2)]
            sq2 = [S.sb("sq", [128, 512], F32) for _ in range(2)]
            tmp2 = [S.sb("ntmp", [128, 512], F32) for _ in range(2)]
            rstd = S.sb("rstd", [128, 512], F32)
            for g, (t0, W) in enumerate(GROUPS):
                j = 0 if g < 8 else 1
                xs, mx, h2 = xs2[g % 2], mx2[g % 2], h22[g % 2]
                S.dma("sp", xs.t[:, :, :W], dr[xcur][:, t0:t0 + W].rearrange("(k p) w -> p k w", p=128), R=[db[xcur]], W=[xs.b])
                S.dma("sp", mx.t[:, :, :W], dr["mxT"][:, t0:t0 + W].rearrange("(k p) w -> p k w", p=128), R=[db["mxT"]], W=[mx.b])
                for n in range(8):
                    ps = self.ps[1 + n % 4]
                    for k in range(8):
                        self.mm(ps.t[:, :W], wout.t[:, k, 128 * n:128 * n + 128], mx.t[:, k, :W], k == 0, k == 7, [wout.b, mx.b], [ps.b])
                    S.op("dve", lambda e, n=n, ps=ps, xs=xs, j=j: e.scalar_tensor_tensor(
                        out=xs.t[:, n, :W], in0=ps.t[:, :W], scalar=mod.t[:, 16 + n, j:j + 1], in1=xs.t[:, n, :W], op0=ALU.mult, op1=ALU.add),
                         R=[ps.b, mod.b, xs.b], W=[xs.b])
                S.dma("pool", dr["XM"][:, t0:t0 + W].rearrange("(k p) w -> p k w", p=128), xs.t[:, :, :W], R=[xs.b], W=[db["XM"]])
                self.norm_group(xs, W, lambda k, j=j: s2.t[:, k, j:j + 1], lambda k, j=j: mod.t[:, 24 + k, j:j + 1], h2, ones, 0, sq2, rstd, tmp2)
                S.dma("pool", dr["h2T"][:, t0:t0 + W].rearrange("(k p) w -> p k w", p=128), h2.t[:, :, :W], R=[h2.b], W=[db["h2T"]])
        with S.phase():
            w1 = S.sb("w1", [128, 8, FFN], BF16)
            w3 = S.sb("w3", [128, 8, FFN], BF16)
            for k in range(8):
                S.dma("pool", w1.t[:, k, :], dr["w1_g"][l * D + k * 128:l * D + (k + 1) * 128, :], R=[db["w1_g"]], W=[w1.b])
                S.dma("pool", w3.t[:, k, :], dr["w3_g"][l * D + k * 128:l * D + (k + 1) * 128, :], R=[db["w3_g"]], W=[w3.b])
            h22 = [S.sb("h2", [128, 8, 512], BF16) for _ in range(2)]
            us = [S.sb("u", [128, 22, 512], BF16) for _ in range(2)]
            sl2 = [S.sb("sl", [128, 512], F32) for _ in range(2)]
            for g, (t0, W) in enumerate(GROUPS):
                h2, u = h22[g % 2], us[g % 2]
                S.dma("sp", h2.t[:, :, :W], dr["h2T"][:, t0:t0 + W].rearrange("(k p) w -> p k w", p=128), R=[db["h2T"]], W=[h2.b])
                for m in range(22):
                    p1 = self.ps[(2 * m) % 6]
                    p3 = self.ps[(2 * m + 1) % 6]
                    for k in range(8):
                        self.mm(p1.t[:, :W], w1.t[:, k, 128 * m:128 * m + 128], h2.t[:, k, :W], k == 0, k == 7, [w1.b, h2.b], [p1.b])
                    for k in range(8):
                        self.mm(p3.t[:, :W], w3.t[:, k, 128 * m:128 * m + 128], h2.t[:, k, :W], k == 0, k == 7, [w3.b, h2.b], [p3.b])
                    sl = sl2[m % 2]
                    S.op("act", lambda e, p1=p1, sl=sl: e.activation(out=sl.t[:, :W], in_=p1.t[:, :W], func=AF.Silu), R=[p1.b], W=[sl.b])
                    S.op("dve", lambda e, p3=p3, sl=sl, u=u, m=m: e.tensor_tensor(out=u.t[:, m, :W], in0=sl.t[:, :W], in1=p3.t[:, :W], op=ALU.mult),
                         R=[p3.b, sl.b], W=[u.b])
                S.dma("pool", dr["uT"][:, t0:t0 + W].rearrange("(m p) w -> p m w", p=128), u.t[:, :, :W], R=[u.b], W=[db["uT"]])
        with S.phase():
            w2 = S.sb("w2", [128, 22, D], BF16)
            for m in range(22):
                S.dma("pool", w2.t[:, m, :], dr["w2_g"][l * FFN + m * 128:l * FFN + (m + 1) * 128, :], R=[db["w2_g"]], W=[w2.b])
            us = [S.sb("u", [128, 22, 512], BF16) for _ in range(2)]
            xs2 = [S.sb("xs", [128, 8, 512], F32) for _ in range(2)]
            for g, (t0, W) in enumerate(GROUPS):
                j = 0 if g < 8 else 1
                u, xs = us[g % 2], xs2[g % 2]
                S.dma("sp", u.t[:, :, :W], dr["uT"][:, t0:t0 + W].rearrange("(m p) w -> p m w", p=128), R=[db["uT"]], W=[u.b])
                S.dma("sp", xs.t[:, :, :W], dr["XM"][:, t0:t0 + W].rearrange("(k p) w -> p k w", p=128), R=[db["XM"]], W=[xs.b])
                for n in range(8):
                    ps = self.ps[n % 6]
                    for m in range(22):
                        self.mm(ps.t[:, :W], w2.t[:, m, 128 * n:128 * n + 128], u.t[:, m, :W], m == 0, m == 21, [w2.b, u.b], [ps.b])
                    S.op("dve", lambda e, n=n, ps=ps, xs=xs, j=j: e.scalar_tensor_tensor(
                        out=xs.t[:, n, :W], in0=ps.t[:, :W], scalar=mod.t[:, 40 + n, j:j + 1], in1=xs.t[:, n, :W], op0=ALU.mult, op1=ALU.add),
                         R=[ps.b, mod.b, xs.b], W=[xs.b])
                S.dma("pool", dr[xnext][:, t0:t0 + W].rearrange("(k p) w -> p k w", p=128), xs.t[:, :, :W], R=[xs.b], W=[db[xnext]])

    def emit_final(self, xcur):
        S, dr, db = self.S, self.dr, self.db
        with S.phase():
            fg = S.sb("fg", [128, 8], F32)
            S.dma("sp", fg.t[:, :], dr["fgT"], R=[db["fgT"]], W=[fg.b])
            ones = S.sb("ones", [128, 128], F32)
            S.op("pool", lambda e: e.memset(ones.t[:, :], 1.0), W=[ones.b])
            xs2 = [S.sb("xs", [128, 8, 512], F32) for _ in range(2)]
            oo2 = [S.sb("oo", [128, 8, 512], F32) for _ in range(2)]
            sq2 = [S.sb("sq", [128, 512], F32) for _ in range(2)]
            rstd = S.sb("rstd", [128, 512], F32)
            for g, (t0, W) in enumerate(GROUPS[:8]):
                xs, oo = xs2[g % 2], oo2[g % 2]
                S.dma("sp", xs.t[:, :, :W], dr[xcur][:, t0:t0 + W].rearrange("(k p) w -> p k w", p=128), R=[db[xcur]], W=[xs.b])
                self.norm_group(xs, W, lambda k: fg.t[:, k:k + 1], None, oo, ones, 0, sq2, rstd, None)
                S.dma("pool", dr["outT"][:, t0:t0 + W].rearrange("(k p) w -> p k w", p=128), oo.t[:, :, :W], R=[oo.b], W=[db["outT"]])

    def emit_debug(self):
        S, dr, db = self.S, self.dr, self.db
        for name in self.debug:
            src = dr[name]
            dst = self.nc.dram_tensor("dbg_" + name, list(src.tensor.shape), src.tensor.dtype, kind="ExternalOutput").ap()
            S.dma("sp", dst, src, R=[db[name]], W=[Buf("dbg")])


def _perm_idx(dh):
    q = dh // 4
    return np.concatenate([np.arange(q, 2 * q), np.arange(0, q), np.arange(3 * q, 4 * q), np.arange(2 * q, 3 * q)])


def _rope_tables(dh, rows, cols, nlat):
    h = dh // 2
    inv = (np.float32(THETA) ** (-(np.arange(0, h, 2, dtype=np.float32)) / np.float32(h))).astype(np.float32)
    angr = (rows.astype(np.float32)[None, :] * inv[:, None]).astype(np.float32)
    angc = (cols.astype(np.float32)[None, :] * inv[:, None]).astype(np.float32)
    C = np.concatenate([np.cos(angr), np.cos(angr), np.cos(angc), np.cos(angc)], 0).astype(np.float32)
    Sn = np.concatenate([-np.sin(angr), np.sin(angr), -np.sin(angc), np.sin(angc)], 0).astype(np.float32)
    Cf = np.ones((dh, T), np.float32)
    Sf = np.zeros((dh, T), np.float32)
    Cf[:, :nlat] = C
    Sf[:, :nlat] = Sn
    return Cf, Sf


def _win_cols():
    o = dict(cq=0, ckv=256, kr=384, rq=416, rk=672, rv=928, gf=1184, gb=1440, nq=1696, nk=1952, nv=2208, sq=2464,
             sk=2720, sv=2848)
    p64 = _perm_idx(64)
    p32 = _perm_idx(32)

    def heads(off, nh):
        return np.concatenate([off + h * 64 + p64 for h in range(nh)])

    cols = [np.arange(o["cq"], o["cq"] + 256), np.arange(o["ckv"], o["ckv"] + 128), np.arange(o["kr"], o["kr"] + 32),
            o["kr"] + p32, np.arange(o["rq"], o["rq"] + 256), heads(o["rq"], 4), np.arange(o["rk"], o["rk"] + 256),
            heads(o["rk"], 4), np.arange(o["gf"], o["gf"] + 256), np.arange(o["gb"], o["gb"] + 256),
            np.arange(o["nq"], o["nq"] + 256), np.arange(o["nk"], o["nk"] + 256), np.arange(o["sq"], o["sq"] + 256),
            heads(o["sq"], 4), np.arange(o["sk"], o["sk"] + 128), heads(o["sk"], 2), np.arange(o["rv"], o["rv"] + 256),
            np.arange(o["nv"], o["nv"] + 256), np.arange(o["sv"], o["sv"] + 128)]
    c = np.concatenate(cols)
    assert c.shape[0] == XW
    return c


def _wuq_cols():
    p32 = _perm_idx(32)
    cols = []
    for h in range(4):
        cols.append(np.arange(h * 96, h * 96 + 96))
        cols.append(np.concatenate([np.arange(h * 96, h * 96 + 64), h * 96 + 64 + p32]))
    return np.concatenate(cols)


def _shard_rows(w2d, core):
    r = w2d.shape[0] // NC
    return np.ascontiguousarray(w2d[core * r:(core + 1) * r])


def prep_inputs(inp):
    f32 = np.float32
    x, c, ctx, c_ctx = (np.asarray(inp[k], f32) for k in ("x", "c", "ctx", "c_ctx"))
    wc = _win_cols()
    win_ext = np.ascontiguousarray(np.asarray(inp["w_in"], f32)[:, :, wc]).reshape(L * D, XW)
    wuq_ext = np.ascontiguousarray(np.asarray(inp["mla_w_uq"], f32)[:, :, _wuq_cols()])
    ada = np.asarray(inp["ada_w"], f32).reshape(L * D, 6 * D)
    wout = np.asarray(inp["w_out"], f32).reshape(L * D, D)
    w1 = np.asarray(inp["ffn_w1"], f32).reshape(L * D, FFN)
    w3 = np.asarray(inp["ffn_w3"], f32).reshape(L * D, FFN)
    w2 = np.asarray(inp["ffn_w2"], f32).reshape(L * FFN, D)
    rpb = np.asarray(inp["na_rpb"], f32)
    jb = np.arange(8)[:, None, None, None, None]
    ko = np.arange(2)[None, :, None, None, None]
    ck = np.arange(64)[None, None, :, None, None]
    qi = np.arange(8)[None, None, None, :, None]
    cq = np.arange(64)[None, None, None, None, :]
    drr = np.clip(2 * jb - 4 + ko - qi + 7, 0, 14) + 0 * ck + 0 * cq
    dcc = np.clip(ck - cq, -15, 15) + 15 + 0 * jb + 0 * ko + 0 * qi
    toep = rpb[:, :, drr, dcc].reshape(L * 4 * 8 * 128, 512)
    shared = dict(
        adabT=np.ascontiguousarray(np.asarray(inp["ada_b"], f32).reshape(L, 48, 128).transpose(0, 2, 1)),
        n1gT=np.ascontiguousarray(np.asarray(inp["norm1_g"], f32).reshape(L, 8, 128).transpose(0, 2, 1)),
        n2gT=np.ascontiguousarray(np.asarray(inp["norm2_g"], f32).reshape(L, 8, 128).transpose(0, 2, 1)),
        fgT=np.ascontiguousarray(np.asarray(inp["final_norm_g"], f32).reshape(8, 128).T),
        qnT=np.ascontiguousarray(np.asarray(inp["mla_q_norm"], f32).reshape(L, 2, 128).transpose(0, 2, 1)),
        kvnT=np.ascontiguousarray(np.asarray(inp["mla_kv_norm"], f32).reshape(L, 1, 128).transpose(0, 2, 1)),
        decB=np.ascontiguousarray(np.broadcast_to(np.asarray(inp["ret_decay"], f32).reshape(L, 1, 8), (L, 128, 8))),
        sinkB=np.ascontiguousarray(np.repeat(np.asarray(inp["swa_sink"], f32), 512, axis=1).reshape(L, 1, 2048)),
        wuq=wuq_ext, wukv=np.ascontiguousarray(np.asarray(inp["mla_w_ukv"], f32)),
        ident=np.eye(128, dtype=f32),
    )
    in_maps = []
    ii = np.arange(128)
    for core in range(NC):
        b, r = core // 4, core % 4
        t0 = r * TL
        tt = np.arange(t0, t0 + TL)
        rows, cols = tt // 64, tt % 64
        m = dict(shared)
        m["xT0"] = np.ascontiguousarray(np.concatenate([x[b, t0:t0 + TL].T, ctx[b].T], axis=1))
        m["cT"] = np.ascontiguousarray(np.stack([c[b].reshape(8, 128).T, c_ctx.reshape(8, 128).T], axis=-1))
        m["c64"], m["s64"] = _rope_tables(64, rows, cols, TL)
        m["c32"], m["s32"] = _rope_tables(32, rows, cols, TL)
        rm = np.full((3, 8, 2, 64, 8, 64), NEG, f32)
        ckk = np.arange(64)[:, None]
        cqq = np.arange(64)[None, :]
        c0 = np.clip(cqq - 8, 0, 48)
        colok = (ckk >= c0) & (ckk < c0 + 16)
        for v, gi in enumerate((0, 3, 7)):
            for jblk in range(8):
                for koff in range(2):
                    kr = 64 * r + 2 * (4 * gi - 2 + jblk) + koff
                    for q_i in range(8):
                        qr = 64 * r + 8 * gi + q_i
                        r0 = min(max(qr - 4, 0), 248)
                        if 0 <= kr < 256 and r0 <= kr < r0 + 8:
                            rm[v, jblk, koff, :, q_i, :] = np.where(colok, 0.0, NEG)
        m["rm"] = rm.reshape(3 * 8 * 128, 512).astype(ml_dtypes.bfloat16)
        ms = np.full((3, 6, 128, 512), NEG, f32)
        for v, gi in enumerate((0, 3, 7)):
            tq = 4096 * r + 512 * gi + np.arange(512)[None, :]
            for jblk in range(6):
                tk = 4096 * r + 512 * gi - 128 + 128 * jblk + np.arange(128)[:, None]
                ok = (tk >= 0) & (tk < SEQ) & (np.abs(tk - tq) <= 128)
                ms[v, jblk] = np.where(ok, 0.0, NEG)
        m["ms"] = ms.reshape(3 * 6 * 128, 512).astype(ml_dtypes.bfloat16)
        rete = np.zeros((128, 2192), f32)
        jj = ii[:, None]
        iq = ii[None, :]
        rete[:, 0:512] = np.tile(np.where(iq >= jj, iq - jj, BIGE), (1, 4))
        rete[:, 512:1024] = np.tile(np.where(jj >= iq, jj - iq, BIGE), (1, 4))
        rete[:, 1024:1536] = np.tile(iq + 1 + 0 * jj, (1, 4))
        rete[:, 1536:2048] = np.tile(128 - iq + 0 * jj, (1, 4))
        rete[:, 2048:2112] = 127 - jj
        rete[:, 2112:2176] = jj
        for i in range(4):
            rete[:, 2176 + i] = TL * (r - 1 - i) if i < r else BIGE
            rete[:, 2181 + i] = TL * (i - r - 1) if i > r else BIGE
        rete[:, 2180] = TL * r
        rete[:, 2185] = TL * (3 - r)
        m["rete"] = rete
        for k, w in (("ada", ada), ("win", win_ext), ("wout", wout), ("w1", w1), ("w3", w3), ("w2", w2), ("toep", toep)):
            m[k + "_sh"] = _shard_rows(w, core)
        in_maps.append(m)
    return in_maps


_PROG = None


def kernel(**inputs):
    global _PROG
    if _PROG is None:
        _PROG = Prog()
    in_maps = prep_inputs(inputs)
    res = run_bass_kernel_spmd(_PROG.nc, in_maps, core_ids=list(range(NC)))
    out = np.empty((B, SEQ, D), np.float32)
    for core in range(NC):
        b, r = core // 4, core % 4
        out[b, r * TL:(r + 1) * TL, :] = res.results[core]["outT"].T
    return out
```
